# Optimizing a Trainium2 kernel written in Bass

```python
import jax, jax.numpy as jnp
from jax import lax
import numpy as np

D_MODEL = 1024
BATCH = 4
SEQ = 4096
DEPTH = 2

HEAD_DIM = 64
BLOCK = 128
A_Q_HEADS = 8
A_KV_HEADS = 2
A_GROUP = A_Q_HEADS // A_KV_HEADS
A_WINDOW = 128
B_HEADS = 8
C_HEADS = 16
C_PATTERNS = ((128, 1), (512, 4), (2048, 16))
MEM_LEN = 256
X_HEADS = 4
X_HEAD_DIM = D_MODEL // X_HEADS
D_FF = 2816
RMS_EPS = 1e-6

A_Q_W = A_Q_HEADS * HEAD_DIM
A_KV_W = A_KV_HEADS * HEAD_DIM
B_W = B_HEADS * HEAD_DIM
EVEN_IN = A_Q_W + 2 * A_KV_W + 3 * B_W
EVEN_MIX = A_Q_W + B_W
ODD_IN = 3 * C_HEADS * HEAD_DIM
ODD_MIX = C_HEADS * HEAD_DIM

kernel_name = 'hybrid_swa_stickbreak_dilated_block'


def rms_norm(x, g):
    xf = x.astype(jnp.float32)
    y = xf * lax.rsqrt(jnp.mean(xf * xf, axis=-1, keepdims=True) + RMS_EPS)
    return (y * g.astype(jnp.float32)).astype(x.dtype)


def alibi_slopes(n_heads):
    return jnp.asarray(2.0 ** (-8.0 * np.arange(1, n_heads + 1) / n_heads), dtype=jnp.float32)


def swiglu_ffn(x, w_gu, w_down):
    gate, up = jnp.split(x @ w_gu, 2, axis=-1)
    return (jax.nn.silu(gate) * up) @ w_down


def banded_attention(q, k, v, slopes, max_dist, step, sinks=None):
    b, l, hkv, g, dh = q.shape
    nb = -(-l // BLOCK)
    lp = nb * BLOCK
    pad = lp - l
    qb = jnp.pad(q, ((0, 0), (0, pad), (0, 0), (0, 0), (0, 0))).reshape(b, nb, BLOCK, hkv, g, dh)
    kv_pad = ((0, 0), (BLOCK, pad), (0, 0), (0, 0))
    k = jnp.pad(k, kv_pad).reshape(b, nb + 1, BLOCK, hkv, dh)
    v = jnp.pad(v, kv_pad).reshape(b, nb + 1, BLOCK, hkv, dh)
    kb = jnp.concatenate([k[:, :-1], k[:, 1:]], axis=2)
    vb = jnp.concatenate([v[:, :-1], v[:, 1:]], axis=2)
    s = jnp.einsum('bnqhgd,bnkhd->bnhgqk', qb, kb).astype(jnp.float32) * (dh ** -0.5)
    dist = jnp.arange(BLOCK)[:, None] + BLOCK - jnp.arange(2 * BLOCK)[None, :]
    kpos = jnp.arange(nb)[:, None] * BLOCK - BLOCK + jnp.arange(2 * BLOCK)[None, :]
    valid = (dist >= 0) & (dist <= max_dist) & (kpos[:, None, :] >= 0)
    bias = -(slopes.astype(jnp.float32) * step)[:, :, None, None] * dist.astype(jnp.float32)
    s = jnp.where(valid[None, :, None, None], s + bias[None, None], -jnp.inf)
    m = jnp.max(s, axis=-1)
    if sinks is not None:
        sk = sinks.astype(jnp.float32)[..., None]
        m = jnp.maximum(m, sk)
    p = jnp.exp(s - m[..., None])
    denom = jnp.sum(p, axis=-1)
    if sinks is not None:
        denom = denom + jnp.exp(sk - m)
    o = jnp.einsum('bnhgqk,bnkhd->bnqhgd', (p / denom[..., None]).astype(v.dtype), vb)
    lse = m + jnp.log(denom)
    o = o.reshape(b, lp, hkv, g, dh)[:, :l]
    lse = jnp.moveaxis(lse, -1, 2).reshape(b, lp, hkv, g)[:, :l]
    return o, lse


def stick_breaking_attention(q, k, v):
    b, s, h, dh = q.shape
    nb = s // BLOCK
    qb = q.reshape(b, nb, BLOCK, h, dh).transpose(1, 0, 2, 3, 4)
    spos = jnp.arange(s)
    scale = dh ** -0.5

    def one_block(args):
        i, qi = args
        z = jnp.einsum('bqhd,bkhd->bhqk', qi, k).astype(jnp.float32) * scale
        tpos = i * BLOCK + jnp.arange(BLOCK)
        strict = spos[None, :] < tpos[:, None]
        log_keep = jnp.where(strict, jax.nn.log_sigmoid(-z), 0.0)
        log_after = lax.cumsum(log_keep, axis=3, reverse=True) - log_keep
        a = jnp.where(strict, jnp.exp(jax.nn.log_sigmoid(z) + log_after), 0.0)
        return jnp.einsum('bhqk,bkhd->bqhd', a.astype(v.dtype), v)

    o = lax.map(one_block, (jnp.arange(nb), qb))
    return o.transpose(1, 0, 2, 3, 4).reshape(b, s, h, dh)


def dilated_attention(q, k, v, slopes):
    b, s, h, dh = q.shape
    outs, lses = [], []
    for window, dil in C_PATTERNS:
        sp = -(-s // dil) * dil
        ls = sp // dil

        def strided(t):
            t = jnp.pad(t, ((0, 0), (0, sp - s), (0, 0), (0, 0)))
            return t.reshape(b, ls, dil, h, dh).transpose(0, 2, 1, 3, 4).reshape(b * dil, ls, h, dh)

        o, lse = banded_attention(strided(q)[:, :, :, None], strided(k), strided(v),
                                  slopes[:, None], window // dil, dil)
        o = o[:, :, :, 0].reshape(b, dil, ls, h, dh).transpose(0, 2, 1, 3, 4).reshape(b, sp, h, dh)[:, :s]
        lse = lse[..., 0].reshape(b, dil, ls, h).transpose(0, 2, 1, 3).reshape(b, sp, h)[:, :s]
        outs.append(o)
        lses.append(lse)
    w = jax.nn.softmax(jnp.stack(lses), axis=0)
    o = jnp.sum(w[..., None] * jnp.stack(outs).astype(jnp.float32), axis=0)
    return o.astype(q.dtype)


def even_mixer(h, w_in, q_gain, k_gain, sinks, w_out):
    b, s, _ = h.shape
    cuts = np.cumsum([A_Q_W, A_KV_W, A_KV_W, B_W, B_W]).tolist()
    qa, ka, va, qb, kb, vb = jnp.split(h @ w_in, cuts, axis=-1)
    qa = rms_norm(qa.reshape(b, s, A_KV_HEADS, A_GROUP, HEAD_DIM), q_gain)
    ka = rms_norm(ka.reshape(b, s, A_KV_HEADS, HEAD_DIM), k_gain)
    va = va.reshape(b, s, A_KV_HEADS, HEAD_DIM)
    slopes_a = alibi_slopes(A_Q_HEADS).reshape(A_KV_HEADS, A_GROUP)
    o_a, _ = banded_attention(qa, ka, va, slopes_a, A_WINDOW - 1, 1,
                              sinks.reshape(A_KV_HEADS, A_GROUP))
    o_b = stick_breaking_attention(qb.reshape(b, s, B_HEADS, HEAD_DIM),
                                   kb.reshape(b, s, B_HEADS, HEAD_DIM),
                                   vb.reshape(b, s, B_HEADS, HEAD_DIM))
    o = jnp.concatenate([o_a.reshape(b, s, A_Q_W), o_b.reshape(b, s, B_W)], axis=-1)
    return o @ w_out


def odd_mixer(h, w_in, q_gain, k_gain, w_out):
    b, s, _ = h.shape
    qkv = (h @ w_in).reshape(b, s, 3, C_HEADS, HEAD_DIM)
    q = rms_norm(qkv[:, :, 0], q_gain)
    k = rms_norm(qkv[:, :, 1], k_gain)
    o = dilated_attention(q, k, qkv[:, :, 2], alibi_slopes(C_HEADS))
    return o.reshape(b, s, ODD_MIX) @ w_out


def memory_cross_attention(h, m, w_q, w_kv, q_gain, k_gain, w_o):
    b, s, _ = h.shape
    q = rms_norm((h @ w_q).reshape(b, s, X_HEADS, X_HEAD_DIM), q_gain)
    kv = (m @ w_kv).reshape(b, m.shape[1], 2, X_HEADS, X_HEAD_DIM)
    k = rms_norm(kv[:, :, 0], k_gain)
    v = kv[:, :, 1]
    sc = jnp.einsum('bqhd,bkhd->bhqk', q, k).astype(jnp.float32) * (X_HEAD_DIM ** -0.5)
    p = jax.nn.softmax(sc, axis=-1).astype(v.dtype)
    o = jnp.einsum('bhqk,bkhd->bqhd', p, v).reshape(b, s, X_HEADS * X_HEAD_DIM)
    return o @ w_o


def setup_inputs(seed: int = 0) -> dict:
    key = jax.random.key(seed)
    k = jax.random.split(key, 25)
    n_even = (DEPTH + 1) // 2
    n_odd = DEPTH // 2
    f32 = jnp.float32

    def dense(kk, shape, fan_in):
        return jax.random.normal(kk, shape, f32) * (fan_in ** -0.5)

    def gain(kk, shape):
        return 1.0 + 0.02 * jax.random.normal(kk, shape, f32)

    return {
        'x': jax.random.normal(k[0], (BATCH, SEQ, D_MODEL), f32),
        'mem': jax.random.normal(k[1], (BATCH, MEM_LEN, D_MODEL), f32),
        'ffn1_norm': gain(k[2], (DEPTH, D_MODEL)),
        'ffn1_w_gu': dense(k[3], (DEPTH, D_MODEL, 2 * D_FF), D_MODEL),
        'ffn1_w_down': dense(k[4], (DEPTH, D_FF, D_MODEL), D_FF),
        'mix_norm': gain(k[5], (DEPTH, D_MODEL)),
        'ev_w_in': dense(k[6], (n_even, D_MODEL, EVEN_IN), D_MODEL),
        'ev_q_gain': gain(k[7], (n_even, HEAD_DIM)),
        'ev_k_gain': gain(k[8], (n_even, HEAD_DIM)),
        'ev_sinks': 0.5 * jax.random.normal(k[9], (n_even, A_Q_HEADS), f32),
        'ev_w_out': dense(k[10], (n_even, EVEN_MIX, D_MODEL), EVEN_MIX),
        'od_w_in': dense(k[11], (n_odd, D_MODEL, ODD_IN), D_MODEL),
        'od_q_gain': gain(k[12], (n_odd, HEAD_DIM)),
        'od_k_gain': gain(k[13], (n_odd, HEAD_DIM)),
        'od_w_out': dense(k[14], (n_odd, ODD_MIX, D_MODEL), ODD_MIX),
        'xa_norm': gain(k[15], (DEPTH, D_MODEL)),
        'xa_mem_norm': gain(k[16], (DEPTH, D_MODEL)),
        'xa_w_q': dense(k[17], (DEPTH, D_MODEL, X_HEADS * X_HEAD_DIM), D_MODEL),
        'xa_w_kv': dense(k[18], (DEPTH, D_MODEL, 2 * X_HEADS * X_HEAD_DIM), D_MODEL),
        'xa_q_gain': gain(k[19], (DEPTH, X_HEAD_DIM)),
        'xa_k_gain': gain(k[20], (DEPTH, X_HEAD_DIM)),
        'xa_w_o': dense(k[21], (DEPTH, X_HEADS * X_HEAD_DIM, D_MODEL), X_HEADS * X_HEAD_DIM),
        'ffn2_norm': gain(k[22], (DEPTH, D_MODEL)),
        'ffn2_w_gu': dense(k[23], (DEPTH, D_MODEL, 2 * D_FF), D_MODEL),
        'ffn2_w_down': dense(k[24], (DEPTH, D_FF, D_MODEL), D_FF),
    }


def reference(x, mem, ffn1_norm, ffn1_w_gu, ffn1_w_down, mix_norm,
              ev_w_in, ev_q_gain, ev_k_gain, ev_sinks, ev_w_out,
              od_w_in, od_q_gain, od_k_gain, od_w_out,
              xa_norm, xa_mem_norm, xa_w_q, xa_w_kv, xa_q_gain, xa_k_gain, xa_w_o,
              ffn2_norm, ffn2_w_gu, ffn2_w_down):
    for layer in range(DEPTH):
        x = x + 0.5 * swiglu_ffn(rms_norm(x, ffn1_norm[layer]), ffn1_w_gu[layer], ffn1_w_down[layer])
        h = rms_norm(x, mix_norm[layer])
        if layer % 2 == 0:
            j = layer // 2
            x = x + even_mixer(h, ev_w_in[j], ev_q_gain[j], ev_k_gain[j], ev_sinks[j], ev_w_out[j])
        else:
            j = layer // 2
            x = x + odd_mixer(h, od_w_in[j], od_q_gain[j], od_k_gain[j], od_w_out[j])
        x = x + memory_cross_attention(rms_norm(x, xa_norm[layer]), rms_norm(mem, xa_mem_norm[layer]),
                                       xa_w_q[layer], xa_w_kv[layer], xa_q_gain[layer],
                                       xa_k_gain[layer], xa_w_o[layer])
        x = x + 0.5 * swiglu_ffn(rms_norm(x, ffn2_norm[layer]), ffn2_w_gu[layer], ffn2_w_down[layer])
    return x
```

```python
from contextlib import ExitStack
import numpy as np
import concourse.bass as bass
import concourse.mybir as mybir
from concourse.bass_utils import run_bass_kernel_spmd

F32 = mybir.dt.float32
BF16 = mybir.dt.bfloat16
AF = mybir.ActivationFunctionType
ALU = mybir.AluOpType

D = 1024
NCH = 8
TOK = 2048
TC = 512
NTC = TOK // TC
DFF = 2816
NFC = DFF // 128
EPS = 1e-6
NCORES = 8
NGROUPS = [4]


class Chan:
    def __init__(self, name):
        self.name = name
        self.sem = None
        self.cnt = 0


class Buf:
    def __init__(self, name, after=()):
        self.name = name
        self.writers = {}
        self.readers = {}
        self.chan = None
        for i, o in enumerate(after):
            self.readers[("init", i)] = o


class Op:
    __slots__ = ("eng", "fn", "deps", "dma", "chan", "val", "flag", "key", "inc")

    def __init__(self, eng, fn, dma=False, chan=None):
        self.eng = eng
        self.fn = fn
        self.deps = []
        self.dma = dma
        self.chan = chan
        self.val = 0
        self.flag = False
        self.key = None
        self.inc = 16


ENGS = ("pe", "act", "dve", "pool", "sp")


class Sched:
    def __init__(self):
        self.q = {e: [] for e in ENGS}
        self.chans = []
        self.nbuf = 0

    def buf(self, name, after=()):
        return Buf(name, after)

    def bufs(self, name, n, after=()):
        return [Buf(f"{name}{i}", after) for i in range(n)]

    def chan(self, name):
        c = Chan(f"{name}_{len(self.chans)}")
        self.chans.append(c)
        return c

    def _add(self, op, reads, writes):
        deps = {}

        def need(o):
            if o is op:
                return
            if (not o.dma) and (not op.dma) and o.eng == "pe" and op.eng == "pe":
                return
            deps[id(o)] = o

        for b in reads:
            for k, o in b.writers.items():
                need(o)
        for b in writes:
            for k, o in b.writers.items():
                if op.dma and o.dma:
                    continue
                need(o)
            for k, o in b.readers.items():
                need(o)
        op.deps = list(deps.values())
        for o in op.deps:
            o.flag = True
        for b in writes:
            if op.dma:
                b.writers = {k: o for k, o in b.writers.items() if o.dma}
                b.writers[op.key] = op
            else:
                b.writers = {op.key: op}
            b.readers = {}
        for b in reads:
            if b not in writes:
                b.readers[op.key] = op
        self.q[op.eng].append(op)
        return op

    def op(self, eng, fn, reads=(), writes=()):
        o = Op(eng, fn)
        o.key = eng
        return self._add(o, reads, writes)

    def dma(self, eng, out, in_, reads=(), writes=(), chan=None):
        if chan is None:
            b = writes[0]
            if b.chan is None:
                b.chan = self.chan("c_" + b.name)
            chan = b.chan
        o = Op(eng, lambda e: e.dma_start(out=out, in_=in_), dma=True, chan=chan)
        chan.cnt += 16
        o.val = chan.cnt
        o.key = ("ch", id(chan))
        return self._add(o, reads, writes)

    def coll(self, fn, reads=(), writes=()):
        ch = self.chan(f"cc{len(self.chans)}")
        o = Op("pool", fn, dma=True, chan=ch)
        o.inc = 1
        ch.cnt += 1
        o.val = ch.cnt
        o.key = ("ch", id(ch))
        return self._add(o, reads, writes)

    def barrier(self):
        res = []
        for e in ENGS:
            for o in reversed(self.q[e]):
                if not o.dma:
                    o.flag = True
                    res.append(o)
                    break
        seen = set()
        for e in ENGS:
            for o in reversed(self.q[e]):
                if o.dma and id(o.chan) not in seen:
                    seen.add(id(o.chan))
                    res.append(o)
        return res

    def emit(self, nc, es):
        for e in ENGS:
            c = 0
            for o in self.q[e]:
                if not o.dma and o.flag:
                    c += 1
                    o.val = c
        esem = {e: es.enter_context(nc.semaphore("sem_" + e)) for e in ENGS}
        for ch in self.chans:
            ch.sem = es.enter_context(nc.semaphore(ch.name))
        block = es.enter_context(nc.Block())

        def body_for(ename):
            def body(e):
                waited = {}
                for o in self.q[ename]:
                    need = {}
                    for d in o.deps:
                        sem = d.chan.sem if d.dma else esem[d.eng]
                        k = id(sem)
                        if waited.get(k, 0) >= d.val:
                            continue
                        if k not in need or need[k][1] < d.val:
                            need[k] = (sem, d.val)
                    for k, (sem, val) in need.items():
                        e.wait_ge(sem, val)
                        waited[k] = val
                    ins = o.fn(e)
                    if ins is None:
                        continue
                    if o.dma:
                        ins.then_inc(o.chan.sem, o.inc)
                    elif o.flag:
                        ins.then_inc(esem[ename], 1)
            return body

        block.tensor(body_for("pe"))
        block.scalar(body_for("act"))
        block.vector(body_for("dve"))
        block.gpsimd(body_for("pool"))
        block.sync(body_for("sp"))


def _col_chunks(v):
    v = np.asarray(v, np.float32)
    return np.ascontiguousarray(v.reshape(-1, 128).T)


def _gain_layout():
    cols = {}
    n = 0
    for l in range(2):
        for nm, w in ((f"ffn1_{l}", 8), (f"mix_{l}", 8), (f"xa_{l}", 8), (f"xam_{l}", 8), (f"ffn2_{l}", 8),
                      (f"xaq_{l}", 2), (f"xak_{l}", 2)):
            cols[nm] = n
            n += w
    for nm, w in (("evq", 1), ("evk", 1), ("odq", 1), ("odk", 1), ("sinks", 8)):
        cols[nm] = n
        n += w
    return cols, n


GCOL, NGCOL = _gain_layout()


def build_gains(inp):
    g = np.zeros((128, NGCOL), np.float32)
    for l in range(2):
        g[:, GCOL[f"ffn1_{l}"]:][:, :8] = _col_chunks(inp["ffn1_norm"][l])
        g[:, GCOL[f"mix_{l}"]:][:, :8] = _col_chunks(inp["mix_norm"][l])
        g[:, GCOL[f"xa_{l}"]:][:, :8] = _col_chunks(inp["xa_norm"][l])
        g[:, GCOL[f"xam_{l}"]:][:, :8] = _col_chunks(inp["xa_mem_norm"][l])
        g[:, GCOL[f"ffn2_{l}"]:][:, :8] = _col_chunks(inp["ffn2_norm"][l])
        g[:, GCOL[f"xaq_{l}"]:][:, :2] = _col_chunks(inp["xa_q_gain"][l])
        g[:, GCOL[f"xak_{l}"]:][:, :2] = _col_chunks(inp["xa_k_gain"][l])
    g[:, GCOL["evq"]] = np.tile(np.asarray(inp["ev_q_gain"][0], np.float32), 2)
    g[:, GCOL["evk"]] = np.tile(np.asarray(inp["ev_k_gain"][0], np.float32), 2)
    g[:, GCOL["odq"]] = np.tile(np.asarray(inp["od_q_gain"][0], np.float32), 2)
    g[:, GCOL["odk"]] = np.tile(np.asarray(inp["od_k_gain"][0], np.float32), 2)
    g[:, GCOL["sinks"]:GCOL["sinks"] + 8] = np.asarray(inp["ev_sinks"][0], np.float32)[None, :]
    return g


NTC_MASK = 23
CB_NEGU, CB_NEGI, CB_TRI7 = 0, 128, 256
CB_MULTC = CB_TRI7 + 7 * 128
CB_MULTA = CB_MULTC + NTC_MASK * 128
CB_N = CB_MULTA + 8 * 128
CF_D0, CF_D0C, CF_NSLC, CF_NSLA, CF_LNS = 0, 128, 256, 256 + 16 * 17, 256 + 16 * 17 + 8 * 2
CF_N = CF_LNS + 4


def slopes(n):
    return [2.0 ** (-8.0 * (i + 1) / n) for i in range(n)]


def build_consts():
    s = np.arange(128)[:, None]
    t = np.arange(128)[None, :]
    cb = np.zeros((128, CB_N), np.float32)
    cb[:, CB_NEGU:CB_NEGU + 128] = -(s >= t).astype(np.float32)
    cb[:, CB_NEGI:CB_NEGI + 128] = -(s == t).astype(np.float32)
    cb[:, CB_TRI7 + 3 * 128:CB_TRI7 + 4 * 128] = (s < t)
    cb[:, CB_TRI7 + 4 * 128:CB_TRI7 + 7 * 128] = 1.0
    for j in range(17):
        d = 128 * j + t - s
        m = ((d >= 0) & (d <= 128)).astype(np.float32) + ((d >= 0) & (d % 4 == 0) & (d <= 512)) \
            + ((d >= 0) & (d % 16 == 0) & (d <= 2048))
        cb[:, CB_MULTC + (3 + j) * 128:CB_MULTC + (4 + j) * 128] = m
    for j in range(2):
        d = 128 * j + t - s
        cb[:, CB_MULTA + (3 + j) * 128:CB_MULTA + (4 + j) * 128] = ((d >= 0) & (d <= 127))
    cf = np.zeros((128, CF_N), np.float32)
    cf[:, CF_D0:CF_D0 + 128] = t - s
    cf[:, CF_D0C:CF_D0C + 128] = np.maximum(t - s, 0)
    sc = slopes(16)
    for h in range(16):
        for j in range(17):
            cf[:, CF_NSLC + h * 17 + j] = -sc[h] * 128.0 * j
    sa = slopes(8)
    for h in range(8):
        for j in range(2):
            cf[:, CF_NSLA + h * 2 + j] = -sa[h] * 128.0 * j
    cf[:, CF_LNS + 0] = 0.0
    cf[:, CF_LNS + 1] = np.log(0.125)
    cf[:, CF_LNS + 2] = np.log(1.0 / 16.0)
    cf[:, CF_LNS + 3] = 1.0
    return cb, cf


W_SHAPES = {
    "ffn1_w_gu": [2, D, 2 * DFF], "ffn1_w_down": [2, DFF, D],
    "ffn2_w_gu": [2, D, 2 * DFF], "ffn2_w_down": [2, DFF, D],
    "ev_w_in": [1, D, 2304], "ev_w_out": [1, D, D], "od_w_in": [1, D, 3072], "od_w_out": [1, D, D],
    "xa_w_q": [2, D, D], "xa_w_kv": [2, D, 2 * D], "xa_w_o": [2, D, D],
}
UNIT = 2048
NUNITS = 14


class Ring:
    def __init__(self, S, tile, n):
        self.tile = tile
        self.n = n
        self.bufs = [S.buf(f"ring{i}") for i in range(n)]
        self.pos = 0

    def alloc(self, k=1):
        if self.pos + k > self.n:
            self.pos = 0
        u = self.pos
        self.pos += k
        return self.tile[:, u * UNIT:(u + k) * UNIT], self.bufs[u:u + k]


class Prog:
    def __init__(self, nc, es, phases, dbg=False):
        self.nc = nc
        self.es = es
        self.S = S = Sched()
        self.phases = phases
        dt = nc.dram_tensor
        self.xT_d = dt("xT", [D, TOK], F32, kind="ExternalInput")
        self.memT_d = dt("memT", [D, 256], F32, kind="ExternalInput")
        self.gains_d = dt("gains", [128, NGCOL], F32, kind="ExternalInput")
        self.cb_d = dt("cb", [128, CB_N], F32, kind="ExternalInput")
        self.cf_d = dt("cf", [128, CF_N], F32, kind="ExternalInput")
        self.flag_d = dt("flag", [128, 64], F32, kind="ExternalInput")
        class _LazyW(dict):
            def __missing__(d, nm):
                d[nm] = dt(nm, W_SHAPES[nm], F32, kind="ExternalInput")
                return d[nm]
        self.w = _LazyW()
        self.dbg = dbg
        self.out_d = dt("outT", [D, TOK], F32, kind="ExternalOutput")
        self.out_buf = S.buf("outT")
        self.scr = {}
        dk = dict(kind="ExternalOutput") if dbg else {}
        if dbg:
            self.xn_d = dt("xn_dump", [D, TOK], BF16, kind="ExternalOutput")
        for l, (nk, vc) in enumerate(((6, 640), (8, 1024))):
            self.scr[l] = dict(
                q=dt(f"q{l}_d", [8 * 128, TOK], BF16, **dk), q_b=S.buf(f"q{l}_d"),
                kloc=[dt(f"kloc{l}_{c}", [128, TOK], BF16) for c in range(nk)],
                kloc_b=[S.buf(f"kloc{l}_{c}") for c in range(nk)],
                kall=[dt(f"kall{l}_{c}", [256, TOK], BF16) for c in range(nk)],
                kall_b=[S.buf(f"kall{l}_{c}") for c in range(nk)],
                vloc=[dt(f"vloc{l}_{j}", [256, vc], BF16) for j in range(8)],
                vloc_b=[S.buf(f"vloc{l}_{j}") for j in range(8)],
                vall=[dt(f"vall{l}_{j}", [512, vc], BF16) for j in range(8)],
                vall_b=[S.buf(f"vall{l}_{j}") for j in range(8)],
                nk=nk, vc=vc)

        sb = lambda name, shape, dtype: es.enter_context(nc.sbuf_tensor(name, shape, dtype))
        ps = lambda name: es.enter_context(nc.psum_tensor(name, [128, 512], F32))
        self.xT = sb("xT_sb", [128, NCH, TOK], F32)
        self.xT_b = [[S.buf(f"xT{c}_{t}") for t in range(NTC)] for c in range(NCH)]
        self.xn = sb("xn_sb", [128, NCH, TOK], BF16)
        self.xn_b = [[S.buf(f"xn{c}_{t}") for t in range(NTC)] for c in range(NCH)]
        self.gains = sb("gains_sb", [128, NGCOL], F32)
        self.cb = sb("cb_sb", [128, CB_N], BF16)
        self.cf = sb("cf_sb", [128, CF_N], F32)
        self.const_b = S.buf("consts")
        self.onesm = sb("onesm", [128, 128], BF16)
        self.ones256 = sb("ones256", [128, 128], BF16)
        self.ones1 = sb("ones1", [128, 128], BF16)
        self.bd64 = sb("bd64", [128, 128], BF16)
        self.flagt = sb("flagt", [128, 64], F32)
        self.flagb = sb("flagb", [128, 64], BF16)
        self.epsc = sb("epsc", [128, 2], F32)
        self.exps = sb("exps", [128, 8], F32)
        self.sq = [sb(f"sq{i}", [128, TC], BF16) for i in range(4)]
        self.sq_b = S.bufs("sq", 4)
        self.rstd = [sb(f"rstd{i}", [128, TC], F32) for i in range(2)]
        self.rstd_b = S.bufs("rstd", 2)
        self.ring_t = sb("ring", [128, NUNITS * UNIT], BF16)
        self.ring = Ring(S, self.ring_t, NUNITS)
        self.sf = sb("scr_f32", [128, 3072], F32)
        self.sbf = sb("scr_bf", [128, 8192], BF16)
        self.ps = [ps(f"ps{i}") for i in range(8)]
        self.ps_b = S.bufs("ps", 8)
        self.norm_i = 0
        self.sq_i = 0

    def gcol(self, name, i=0):
        c = GCOL[name] + i
        return self.gains[:, c:c + 1]

    def setup(self):
        S = self.S
        cb_ = self.const_b
        S.dma("sp", self.gains[:, :], self.gains_d[:, :], writes=[cb_])
        S.dma("sp", self.cf[:, :], self.cf_d[:, :], writes=[cb_])
        S.dma("sp", self.flagt[:, :], self.flag_d[:, :], writes=[cb_])
        for i in range(0, CB_N, 1024):
            j = min(CB_N, i + 1024)
            S.dma("pool", self.cb[:, i:j], self.cb_d[:, i:j], writes=[cb_])
        S.op("pool", lambda e: e.memset(self.onesm[:, :], 1.0 / D), writes=[cb_])
        S.op("pool", lambda e: e.memset(self.ones256[:, :], 1.0 / 256), writes=[cb_])
        S.op("pool", lambda e: e.memset(self.ones1[:, :], 1.0), writes=[cb_])
        S.op("pool", lambda e: e.memset(self.bd64[:, :], 0.0), writes=[cb_])
        S.op("pool", lambda e: e.memset(self.bd64[0:64, 0:64], 1.0 / 64), writes=[cb_])
        S.op("pool", lambda e: e.memset(self.bd64[64:128, 64:128], 1.0 / 64), writes=[cb_])
        S.op("pool", lambda e: e.memset(self.epsc[:, :], EPS), writes=[cb_])
        S.op("pool", lambda e: e.memset(self.sbf[:, :], 0.0), writes=[cb_])
        S.op("dve", lambda e: e.tensor_copy(out=self.flagb[:, :], in_=self.flagt[:, :]), reads=[cb_], writes=[cb_])
        sc = GCOL["sinks"]
        S.op("act", lambda e: e.activation(out=self.exps[:, :], in_=self.gains[:, sc:sc + 8], func=AF.Exp),
             reads=[cb_], writes=[cb_])
        xv = self.xT_d.ap().rearrange("(c p) t -> p c t", p=128)
        for t in range(NTC):
            for c in range(NCH):
                S.dma("sp", self.xT[:, c, t * TC:(t + 1) * TC], xv[:, c, t * TC:(t + 1) * TC],
                      writes=[self.xT_b[c][t]])

    def finish(self):
        S = self.S
        ov = self.out_d.ap().rearrange("(c p) t -> p c t", p=128)
        for t in range(NTC):
            for c in range(NCH):
                S.dma("sp", ov[:, c, t * TC:(t + 1) * TC], self.xT[:, c, t * TC:(t + 1) * TC],
                      reads=[self.xT_b[c][t]], writes=[self.out_buf])
        if self.dbg:
            xv = self.xn_d.ap().rearrange("(c p) t -> p c t", p=128)
            for t in range(NTC):
                for c in range(NCH):
                    S.dma("sp", xv[:, c, t * TC:(t + 1) * TC], self.xn[:, c, t * TC:(t + 1) * TC],
                          reads=[self.xn_b[c][t]], writes=[self.out_buf])
        S.op("sp", lambda e: None, reads=[self.out_buf])

    def rstd_from(self, srcs, ones, n, lnbias, out_rs, out_rsb, reads_extra=()):
        S = self.S
        i = self.norm_i % 2
        self.norm_i += 1
        pss, pssb = self.ps[6 + i], self.ps_b[6 + i]
        k = len(srcs)
        for c, (ap, bl) in enumerate(srcs):
            q = self.sq_i % 4
            self.sq_i += 1
            sq, sqb = self.sq[q], self.sq_b[q]
            S.op("act", lambda e, sq=sq, ap=ap: e.activation(out=sq[:, 0:n], in_=ap, func=AF.Square),
                 reads=list(bl), writes=[sqb])
            S.op("pe", lambda e, sq=sq, pss=pss, c=c: e.matmul(pss[:, 0:n], lhsT=ones[:, :], rhs=sq[:, 0:n],
                                                             start=(c == 0), stop=(c == k - 1)),
                 reads=[self.const_b, sqb], writes=[pssb])
        S.op("act", lambda e: e.activation(out=out_rs, in_=pss[:, 0:n], func=AF.Ln, bias=self.epsc[:, 0:1]),
             reads=[pssb, self.const_b], writes=[out_rsb])
        S.op("act", lambda e: e.activation(out=out_rs, in_=out_rs, func=AF.Exp, scale=-0.5, bias=lnbias),
             reads=[out_rsb, self.const_b], writes=[out_rsb])

    def lnb(self, i):
        return self.cf[:, CF_LNS + i:CF_LNS + i + 1]

    def rmsnorm_x(self, gname):
        S = self.S
        xT, xn = self.xT, self.xn
        for t in range(NTC):
            sl = slice(t * TC, (t + 1) * TC)
            i = self.norm_i % 2
            rs, rsb = self.rstd[i], self.rstd_b[i]
            self.rstd_from([(xT[:, c, sl], [self.xT_b[c][t]]) for c in range(NCH)], self.onesm, TC,
                           self.lnb(0), rs[:, :], rsb)
            for c in range(NCH):
                S.op("dve", lambda e, c=c, rs=rs, sl=sl: e.scalar_tensor_tensor(
                        out=xn[:, c, sl], in0=xT[:, c, sl], scalar=self.gcol(gname, c),
                        in1=rs[:, :], op0=ALU.mult, op1=ALU.mult),
                     reads=[self.xT_b[c][t], rsb, self.const_b], writes=[self.xn_b[c][t]])

    def load_cols(self, w_ap2d, c0, ncols=256):
        ap, bufs = self.ring.alloc(1)
        v = ap.rearrange("p (c f) -> p c f", c=NCH)
        src = w_ap2d.rearrange("(c p) f -> p c f", p=128)
        self.S.dma("pool", v[:, :, 0:ncols], src[:, :, c0:c0 + ncols], writes=bufs)
        return v, bufs

    def load_rows(self, w_ap2d, r0):
        ap, bufs = self.ring.alloc(1)
        v = ap.rearrange("p (j m) -> p j m", j=2)
        src = w_ap2d[r0:r0 + 256, :].rearrange("(j p) m -> p j m", p=128)
        self.S.dma("pool", v[:, :, :], src, writes=bufs)
        return v, bufs

    def ffn(self, wgu_d, wdn_d, layer):
        S = self.S
        after = S.barrier()
        sg = [self.sf[:, i * TC:(i + 1) * TC] for i in range(2)]
        sg_b = S.bufs("sg", 2, after)
        act = [self.sbf[:, i * 2 * TC:(i + 1) * 2 * TC].rearrange("p (j t) -> p j t", j=2) for i in range(3)]
        act_b = [[S.buf(f"act{i}_{j}", after) for j in range(2)] for i in range(3)]
        NG = NFC // 2
        gu2 = wgu_d.ap()[layer]
        dn2 = wdn_d.ap()[layer]
        W = {}

        def load(g):
            f0 = g * 256
            W[g] = (self.load_cols(gu2, f0), self.load_cols(gu2, DFF + f0), self.load_rows(dn2, f0))

        items = [(g, t) for g in range(NG) for t in range(NTC)]
        xn = self.xn

        def emit_gu(i):
            g, t = items[i]
            sl = slice(t * TC, (t + 1) * TC)
            a = i % 3
            for j in range(2):
                k = (2 * i + j) % 2
                pg, pgb = self.ps[k], self.ps_b[k]
                pu, pub = self.ps[2 + k], self.ps_b[2 + k]
                for (pp, ppb, gi) in ((pg, pgb, 0), (pu, pub, 1)):
                    wv, wb = W[g][gi]
                    for c in range(NCH):
                        S.op("pe", lambda e, pp=pp, wv=wv, c=c, j=j, sl=sl: e.matmul(
                                pp[:, :], lhsT=wv[:, c, j * 128:(j + 1) * 128], rhs=xn[:, c, sl],
                                start=(c == 0), stop=(c == NCH - 1)),
                             reads=wb + [self.xn_b[c][t]], writes=[ppb])
                S.op("act", lambda e, k=k, pg=pg: e.activation(out=sg[k], in_=pg[:, :], func=AF.Silu),
                     reads=[pgb], writes=[sg_b[k]])
                S.op("dve", lambda e, k=k, pu=pu, a=a, j=j: e.tensor_tensor(
                        out=act[a][:, j, :], in0=pu[:, :], in1=sg[k], op=ALU.mult),
                     reads=[pub, sg_b[k]], writes=[act_b[a][j]])

        def emit_down(i):
            g, t = items[i]
            sl = slice(t * TC, (t + 1) * TC)
            a = i % 3
            wv, wb = W[g][2]
            for m in range(NCH):
                k = 4 + (m % 2)
                py, pyb = self.ps[k], self.ps_b[k]
                for j in range(2):
                    S.op("pe", lambda e, py=py, j=j, m=m, wv=wv, a=a: e.matmul(
                            py[:, :], lhsT=wv[:, j, m * 128:(m + 1) * 128], rhs=act[a][:, j, :],
                            start=(j == 0), stop=(j == 1)),
                         reads=wb + [act_b[a][j]], writes=[pyb])
                S.op("dve", lambda e, py=py, m=m, sl=sl: e.scalar_tensor_tensor(
                        out=self.xT[:, m, sl], in0=py[:, :], scalar=0.5, in1=self.xT[:, m, sl],
                        op0=ALU.mult, op1=ALU.add),
                     reads=[pyb, self.xT_b[m][t]], writes=[self.xT_b[m][t]])

        PF = 3
        for g in range(min(PF, NG)):
            load(g)
        n = len(items)
        for i in range(n + 1):
            if i < n:
                g, t = items[i]
                if t == 1 and g + PF < NG:
                    load(g + PF)
                emit_gu(i)
            if i >= 1:
                emit_down(i - 1)

    def out_proj(self, w2d):
        S = self.S
        units = [self.load_rows(w2d, r * 256) for r in range(4)]
        for t in range(NTC):
            sl = slice(t * TC, (t + 1) * TC)
            for m in range(NCH):
                k = 4 + (m % 2)
                py, pyb = self.ps[k], self.ps_b[k]
                for oc in range(NCH):
                    wv, wb = units[oc // 2]
                    S.op("pe", lambda e, py=py, wv=wv, oc=oc, m=m, sl=sl: e.matmul(
                            py[:, :], lhsT=wv[:, oc % 2, m * 128:(m + 1) * 128], rhs=self.xn[:, oc, sl],
                            start=(oc == 0), stop=(oc == NCH - 1)),
                         reads=wb + [self.xn_b[oc][t]], writes=[pyb])
                S.op("dve", lambda e, py=py, m=m, sl=sl: e.tensor_tensor(
                        out=self.xT[:, m, sl], in0=py[:, :], in1=self.xT[:, m, sl], op=ALU.add),
                     reads=[pyb, self.xT_b[m][t]], writes=[self.xT_b[m][t]])

    def mixer_proj(self, layer):
        S = self.S
        after = S.barrier()
        sc = self.scr[layer]
        even = layer == 0
        w2d = (self.w["ev_w_in"] if even else self.w["od_w_in"]).ap()[0]
        nun = 9 if even else 12
        stg = [self.sbf[:, i * TC:(i + 1) * TC] for i in range(4)]
        stg_b = S.bufs("stg", 4, after)
        stg_ch = [S.chan(f"stgch{layer}_{i}") for i in range(4)]
        sqh = [self.sbf[:, (4 + i) * TC:(5 + i) * TC] for i in range(2)]
        sqh_b = S.bufs("sqh", 2, after)
        rr = [self.sf[:, i * TC:(i + 1) * TC] for i in range(2)]
        rr_b = S.bufs("rr", 2, after)
        vst = [self.sbf[:, 3072 + i * 1024:3072 + (i + 1) * 1024] for i in range(2)]
        vst_b = S.bufs("vst", 2, after)
        vst_ch = [S.chan(f"vstch{layer}_{i}") for i in range(2)]
        cnt = {"s": 0, "q": 0, "v": 0}
        gq, gk = ("evq", "evk") if even else ("odq", "odk")
        jobs = []
        if even:
            for j in range(4):
                jobs.append((j // 2, (j % 2) * 128, "norm", gq, 1, sc["q"], sc["q_b"], j))
            for g in range(2):
                jobs.append((2, ("dup", g * 64), "norm", gk, 0, sc["kloc"], sc["kloc_b"], g))
            for j in range(4):
                jobs.append((3 + j // 2, (j % 2) * 128, "scale", None, 0.125, sc["q"], sc["q_b"], 4 + j))
            for j in range(4):
                jobs.append((5 + j // 2, (j % 2) * 128, "scale", None, 1.0, sc["kloc"], sc["kloc_b"], 2 + j))
            vjobs = [[(2, 128, 128)], [(7, 0, 256), (8, 0, 256)]]
        else:
            for j in range(8):
                jobs.append((j // 2, (j % 2) * 128, "norm", gq, 1, sc["q"], sc["q_b"], j))
            for j in range(8):
                jobs.append((4 + j // 2, (j % 2) * 128, "norm", gk, 0, sc["kloc"], sc["kloc_b"], j))
            vjobs = [[(8, 0, 256), (9, 0, 256)], [(10, 0, 256), (11, 0, 256)]]
        U = {}

        def need(u):
            if u not in U:
                U[u] = self.load_cols(w2d, u * 256)
            return U[u]

        for u in range(nun):
            need(u)
        xn = self.xn
        for (u, off, kind, gname, par, dten, dbuf, drow) in jobs:
            wv, wb = need(u)
            for t in range(NTC):
                sl = slice(t * TC, (t + 1) * TC)
                k = cnt["q"] % 2
                cnt["q"] += 1
                pq, pqb = self.ps[k], self.ps_b[k]
                if isinstance(off, tuple):
                    c0 = off[1]
                    for half in range(2):
                        for c in range(NCH):
                            S.op("pe", lambda e, pq=pq, wv=wv, c=c, c0=c0, half=half, sl=sl: e.matmul(
                                    pq[half * 64:(half + 1) * 64, :], lhsT=wv[:, c, c0:c0 + 64], rhs=xn[:, c, sl],
                                    start=(c == 0), stop=(c == NCH - 1)),
                                 reads=wb + [self.xn_b[c][t]], writes=[pqb])
                else:
                    for c in range(NCH):
                        S.op("pe", lambda e, pq=pq, wv=wv, c=c, off=off, sl=sl: e.matmul(
                                pq[:, :], lhsT=wv[:, c, off:off + 128], rhs=xn[:, c, sl],
                                start=(c == 0), stop=(c == NCH - 1)),
                             reads=wb + [self.xn_b[c][t]], writes=[pqb])
                si = cnt["s"] % 4
                cnt["s"] += 1
                if kind == "norm":
                    r, rb = rr[k], rr_b[k]
                    self.rstd_from([(pq[:, :], [pqb])], self.bd64, TC, self.lnb(par), r, rb)
                    S.op("dve", lambda e, si=si, pq=pq, r=r, gname=gname: e.scalar_tensor_tensor(
                            out=stg[si], in0=pq[:, :], scalar=self.gcol(gname), in1=r, op0=ALU.mult, op1=ALU.mult),
                         reads=[pqb, rb, self.const_b], writes=[stg_b[si]])
                else:
                    S.op("act", lambda e, si=si, pq=pq, par=par: e.activation(
                            out=stg[si], in_=pq[:, :], func=AF.Copy, scale=float(par)),
                         reads=[pqb], writes=[stg_b[si]])
                if isinstance(dten, list):
                    S.dma("sp", dten[drow][:, sl], stg[si], reads=[stg_b[si]], writes=[dbuf[drow]], chan=stg_ch[si])
                    if t == NTC - 1:
                        self.gather(sc["kloc"][drow], sc["kloc_b"][drow], sc["kall"][drow], sc["kall_b"][drow])
                else:
                    S.dma("sp", dten[drow * 128:(drow + 1) * 128, sl], stg[si], reads=[stg_b[si]], writes=[dbuf],
                          chan=stg_ch[si])
        vc = sc["vc"]
        for nb in range(TOK // 128):
            tb, tsl = nb // 4, slice(nb * 128, (nb + 1) * 128)
            vi = cnt["v"] % 2
            cnt["v"] += 1
            col = 0
            for bi, grp in enumerate(vjobs):
                pv, pvb = self.ps[2 + 2 * vi + bi], self.ps_b[2 + 2 * vi + bi]
                pc = 0
                for (u, c0, n) in grp:
                    wv, wb = need(u)
                    for c in range(NCH):
                        S.op("pe", lambda e, pv=pv, wv=wv, c=c, c0=c0, n=n, pc=pc, tsl=tsl: e.matmul(
                                pv[:, pc:pc + n], lhsT=xn[:, c, tsl], rhs=wv[:, c, c0:c0 + n],
                                start=(c == 0), stop=(c == NCH - 1)),
                             reads=wb + [self.xn_b[c][tb]], writes=[pvb])
                    pc += n
                eng = "act" if bi == 0 else "dve"
                if eng == "act":
                    S.op("act", lambda e, pv=pv, vi=vi, col=col, pc=pc: e.activation(
                            out=vst[vi][:, col:col + pc], in_=pv[:, 0:pc], func=AF.Copy),
                         reads=[pvb], writes=[vst_b[vi]])
                else:
                    S.op("dve", lambda e, pv=pv, vi=vi, col=col, pc=pc: e.tensor_copy(
                            out=vst[vi][:, col:col + pc], in_=pv[:, 0:pc]),
                         reads=[pvb], writes=[vst_b[vi]])
                col += pc
            jv = nb // 2
            S.dma("sp", sc["vloc"][jv][(nb % 2) * 128:(nb % 2 + 1) * 128, :], vst[vi][:, 0:vc], reads=[vst_b[vi]],
                  writes=[sc["vloc_b"][jv]], chan=vst_ch[vi])
            if nb % 2 == 1:
                self.gather(sc["vloc"][jv], sc["vloc_b"][jv], sc["vall"][jv], sc["vall_b"][jv])

    def gather(self, src, src_b, dst, dst_b):
        S = self.S
        import os
        if os.environ.get("KDBG_NOCOLL"):
            n = src.shape[0]
            S.dma("sp", dst[0:n, :], src[:, :], reads=[src_b], writes=[dst_b])
            return
        groups = [[2 * i, 2 * i + 1] for i in range(NGROUPS[0])]
        S.coll(lambda e: e.collective_compute("AllGather", ALU.bypass, replica_groups=groups,
                                              ins=[src.ap().opt()], outs=[dst.ap().opt()]),
               reads=[src_b], writes=[dst_b])

    def load_attn_tiles(self, layer, qrow, krow, vcol, nprev):
        S = self.S
        sc = self.scr[layer]
        nk = sc["nk"]
        qa, qb_ = self.ring.alloc(1)
        S.dma("sp", qa, sc["q"][qrow * 128:(qrow + 1) * 128, :], reads=[sc["q_b"]], writes=qb_)
        ka, kb_ = self.ring.alloc(2)
        npc = nprev * 128
        S.dma("sp", ka[:, 0:npc], sc["kall"][krow][0:128, TOK - npc:TOK], reads=[sc["kall_b"][krow]], writes=kb_)
        S.dma("sp", ka[:, npc:npc + TOK], sc["kloc"][krow][:, :], reads=[sc["kloc_b"][krow]], writes=kb_)
        va, vb_ = self.ring.alloc(2)
        v3 = va.rearrange("p (n c) -> p n c", c=128)
        if nprev == 16:
            for j in range(8):
                S.dma("sp", v3[:, 2 * j:2 * j + 2, :],
                      sc["vall"][j][0:256, vcol:vcol + 128].rearrange("(n p) c -> p n c", p=128),
                      reads=[sc["vall_b"][j]], writes=vb_)
        else:
            assert nprev == 1
            S.dma("sp", v3[:, 0:1, :], sc["vall"][7][128:256, vcol:vcol + 128].rearrange("(n p) c -> p n c", p=128),
                  reads=[sc["vall_b"][7]], writes=vb_)
        for j in range(8):
            S.dma("sp", v3[:, nprev + 2 * j:nprev + 2 * j + 2, :],
                  sc["vloc"][j][:, vcol:vcol + 128].rearrange("(n p) c -> p n c", p=128),
                  reads=[sc["vloc_b"][j]], writes=vb_)
        S.op("dve", lambda e: e.tensor_scalar(out=va[:, 0:npc], in0=va[:, 0:npc], scalar1=self.flagt[:, 0:1],
                                              scalar2=1.0, op0=ALU.mult, op1=ALU.mult),
             reads=vb_ + [self.const_b], writes=vb_)
        return (qa, qb_), (ka, kb_), (v3, vb_)

    def banded(self, layer, pairs, ND, nprev, mult_off, nsl_off, sl_list, sink):
        S = self.S
        after = S.barrier()
        NT = ND + 6
        msk = [self.sbf[:, i * 3072:i * 3072 + NT * 128] for i in range(2)]
        msk_b = S.bufs("msk", 2, after)
        P = [self.sbf[:, 6144 + i * TC:6144 + (i + 1) * TC] for i in range(3)]
        P_b = S.bufs("P", 3, after)
        tmp = [self.sf[:, i * TC:(i + 1) * TC] for i in range(2)]
        tmp_b = S.bufs("tmp", 2, after)
        cb_ = self.const_b
        hcount = 0
        ucount = 0
        ccount = 0
        for pi, (qrow, krow, vcol, oc, vh) in enumerate(pairs):
            (qa, qb_), (ka, kb_), (v3, vb_) = self.load_attn_tiles(layer, qrow, krow, vcol, nprev)
            for hh in range(2):
                h = 2 * pi + hh
                hp = slice(hh * 64, (hh + 1) * 64)
                vs = hp if vh is None else slice(vh, vh + 64)
                mi = hcount % 2
                hcount += 1
                m, mb = msk[mi], msk_b[mi]
                S.op("pool", lambda e, m=m: e.memset(m[:, 0:384], 0.0), writes=[mb])
                S.op("pool", lambda e, m=m: e.memset(m[:, (ND + 3) * 128:(ND + 6) * 128], 0.0), writes=[mb])
                for j in range(ND):
                    src = CF_D0C if j == 0 else CF_D0
                    S.op("act", lambda e, m=m, j=j, src=src, h=h: e.activation(
                            out=m[:, (3 + j) * 128:(4 + j) * 128], in_=self.cf[:, src:src + 128], func=AF.Exp,
                            scale=float(-sl_list[h]), bias=self.cf[:, nsl_off + h * ND + j:nsl_off + h * ND + j + 1]),
                         reads=[cb_], writes=[mb])
                S.op("dve", lambda e, m=m: e.tensor_tensor(out=m, in0=m, in1=self.cb[:, mult_off:mult_off + NT * 128],
                                                          op=ALU.mult),
                     reads=[cb_, mb], writes=[mb])
                for t in range(NTC):
                    ucount = self._banded_chunk(t, ccount, ucount, nprev, ND, hp, vs, h, oc, sink, m, mb, P, P_b, tmp, tmp_b,
                                                qa, qb_, ka, kb_, v3, vb_)
                    ccount += 1

    def _banded_chunk(self, t, ccount, ucount, nprev, ND, hp, vs, h, oc, sink, m, mb, P, P_b, tmp, tmp_b,
                      qa, qb_, ka, kb_, v3, vb_):
        S = self.S
        cb_ = self.const_b
        ci = ccount % 2
        po, pob = self.ps[2 + ci], self.ps_b[2 + ci]
        pd, pdb = self.ps[4 + ci], self.ps_b[4 + ci]
        Lq0 = nprev + 4 * t
        units = [(L, Lq0 - L) for L in range(max(0, Lq0 - (ND - 1)), Lq0 + 4)]
        n = len(units)
        qsl = slice(t * TC, (t + 1) * TC)

        def st_z(i):
            L, j0 = units[i]
            k = (ucount + i) % 2
            S.op("pe", lambda e, k=k, L=L: e.matmul(self.ps[k][:, :], lhsT=ka[hp, L * 128:(L + 1) * 128],
                                                    rhs=qa[hp, qsl], start=True, stop=True),
                 reads=kb_ + qb_, writes=[self.ps_b[k]])

        def st_p(i):
            L, j0 = units[i]
            k = (ucount + i) % 2
            p = (ucount + i) % 3
            S.op("act", lambda e, k=k, p=p: e.activation(out=P[p], in_=self.ps[k][:, :], func=AF.Exp),
                 reads=[self.ps_b[k]], writes=[P_b[p]])
            S.op("dve", lambda e, p=p, j0=j0: e.tensor_tensor(
                    out=P[p], in0=P[p], in1=m[:, (j0 + 3) * 128:(j0 + 7) * 128], op=ALU.mult),
                 reads=[P_b[p], mb], writes=[P_b[p]])

        def st_o(i):
            L, j0 = units[i]
            p = (ucount + i) % 3
            S.op("pe", lambda e, p=p, L=L, i=i: e.matmul(po[hp, :], lhsT=v3[:, L, vs], rhs=P[p],
                                                         start=(i == 0), stop=(i == n - 1)),
                 reads=vb_ + [P_b[p]], writes=[pob])
            ones = self.flagb[:, 0:64] if L < nprev else self.ones1[:, 0:64]
            S.op("pe", lambda e, p=p, ones=ones, i=i: e.matmul(pd[hp, :], lhsT=ones, rhs=P[p],
                                                               start=(i == 0), stop=(i == n - 1)),
                 reads=[cb_, P_b[p]], writes=[pdb])

        for step in range(n + 2):
            if step < n:
                st_z(step)
            if 0 <= step - 1 < n:
                st_p(step - 1)
            if 0 <= step - 2 < n:
                st_o(step - 2)
        ti = ccount % 2
        tm, tmb = tmp[ti], tmp_b[ti]
        if sink:
            S.op("dve", lambda e: e.tensor_scalar(
                    out=tm[hp, :], in0=pd[hp, :], scalar1=self.exps[hp, h:h + 1], scalar2=0.0, op0=ALU.add, op1=ALU.add),
                 reads=[pdb, cb_], writes=[tmb])
            S.op("dve", lambda e: e.reciprocal(out=tm[hp, :], in_=tm[hp, :]),
                 reads=[tmb], writes=[tmb])
        else:
            S.op("dve", lambda e: e.reciprocal(out=tm[hp, :], in_=pd[hp, :]),
                 reads=[pdb], writes=[tmb])
        S.op("dve", lambda e: e.tensor_tensor(out=self.xn[hp, oc, qsl], in0=po[hp, :], in1=tm[hp, :], op=ALU.mult),
             reads=[pob, tmb], writes=[self.xn_b[oc][t]])
        return ucount + n

    def stick(self, layer, pairs):
        S = self.S
        after = S.barrier()
        nprev = 16
        ef = [self.sf[:, i * TC:(i + 1) * TC] for i in range(2)]
        ef_b = S.bufs("ef", 2, after)
        sp = [self.sbf[:, i * TC:(i + 1) * TC] for i in range(2)]
        sp_b = S.bufs("sp", 2, after)
        A = [self.sbf[:, (2 + i) * TC:(3 + i) * TC] for i in range(2)]
        A_b = S.bufs("A", 2, after)
        Rs = [self.sbf[:, (4 + i) * TC:(5 + i) * TC] for i in range(2)]
        Rs_b = S.bufs("Rs", 2, after)
        cb_ = self.const_b
        negU = self.cb[:, CB_NEGU:CB_NEGU + 128]
        negI = self.cb[:, CB_NEGI:CB_NEGI + 128]
        one_c = self.cf[:, CF_LNS + 3:CF_LNS + 4]
        ccount = 0
        ucount = 0
        for pi, (qrow, krow, vcol, oc) in enumerate(pairs):
            (qa, qb_), (ka, kb_), (v3, vb_) = self.load_attn_tiles(layer, qrow, krow, vcol, nprev)
            for hh in range(2):
                hp = slice(hh * 64, (hh + 1) * 64)
                for t in range(NTC):
                    ucount = self._stick_chunk(t, ccount, ucount, nprev, hp, oc, ef, ef_b, sp, sp_b, A, A_b, Rs, Rs_b,
                                               qa, qb_, ka, kb_, v3, vb_)
                    ccount += 1

    def _stick_chunk(self, t, ccount, ucount, nprev, hp, oc, ef, ef_b, sp, sp_b, A, A_b, Rs, Rs_b,
                     qa, qb_, ka, kb_, v3, vb_):
        S = self.S
        cb_ = self.const_b
        negU = self.cb[:, CB_NEGU:CB_NEGU + 128]
        negI = self.cb[:, CB_NEGI:CB_NEGI + 128]
        one_c = self.cf[:, CF_LNS + 3:CF_LNS + 4]
        ci = ccount % 2
        pR, pRb = self.ps[4 + ci], self.ps_b[4 + ci]
        pO, pOb = self.ps[6 + ci], self.ps_b[6 + ci]
        qsl = slice(t * TC, (t + 1) * TC)
        Ldiag0 = nprev + 4 * t
        Ls = list(range(Ldiag0 + 3, -1, -1))
        n = len(Ls)

        def s0(i):
            L = Ls[i]
            k = (ucount + i) % 2
            S.op("pe", lambda e, k=k, L=L: e.matmul(self.ps[k][:, :], lhsT=ka[hp, L * 128:(L + 1) * 128], rhs=qa[hp, qsl],
                                                    start=True, stop=True),
                 reads=kb_ + qb_, writes=[self.ps_b[k]])

        def s1(i):
            L = Ls[i]
            k = (ucount + i) % 2
            S.op("act", lambda e, k=k: e.activation(out=ef[k], in_=self.ps[k][:, :], func=AF.Exp),
                 reads=[self.ps_b[k]], writes=[ef_b[k]])
            S.op("act", lambda e, k=k: e.activation(out=sp[k], in_=ef[k], func=AF.Ln, bias=one_c),
                 reads=[ef_b[k], cb_], writes=[sp_b[k]])
            if L >= Ldiag0:
                ib = L - Ldiag0
                S.op("dve", lambda e, k=k, ib=ib: e.tensor_tensor(
                        out=sp[k], in0=sp[k], in1=self.cb[:, CB_TRI7 + (3 - ib) * 128:CB_TRI7 + (7 - ib) * 128],
                        op=ALU.mult),
                     reads=[sp_b[k], cb_], writes=[sp_b[k]])

        def s2(i):
            L = Ls[i]
            k = (ucount + i) % 2
            pC, pCb = self.ps[2 + k], self.ps_b[2 + k]
            S.op("pe", lambda e, k=k, pC=pC: e.matmul(pC[:, :], lhsT=negU, rhs=sp[k], start=True, stop=False),
                 reads=[cb_, sp_b[k]], writes=[pCb])
            last = (i == 0)
            S.op("pe", lambda e, pC=pC, L=L, last=last: e.matmul(
                    pC[:, :], lhsT=ka[hp, L * 128:(L + 1) * 128], rhs=qa[hp, qsl], start=False, stop=last),
                 reads=kb_ + qb_, writes=[pCb])
            if i > 0:
                kp = (ucount + i - 1) % 2
                S.op("pe", lambda e, pC=pC, kp=kp: e.matmul(pC[:, :], lhsT=negI, rhs=Rs[kp], start=False, stop=True),
                     reads=[cb_, Rs_b[kp]], writes=[pCb])
            S.op("pe", lambda e, k=k, i=i: e.matmul(pR[:, :], lhsT=self.ones1[:, :], rhs=sp[k],
                                                   start=(i == 0), stop=(i == n - 1)),
                 reads=[cb_, sp_b[k]], writes=[pRb])
            if i < n - 1:
                S.op("dve", lambda e, k=k: e.tensor_copy(out=Rs[k], in_=pR[:, :]),
                     reads=[pRb], writes=[Rs_b[k]])

        def s3(i):
            L = Ls[i]
            k = (ucount + i) % 2
            pC, pCb = self.ps[2 + k], self.ps_b[2 + k]
            S.op("act", lambda e, k=k, pC=pC: e.activation(out=A[k], in_=pC[:, :], func=AF.Exp),
                 reads=[pCb], writes=[A_b[k]])
            if L >= Ldiag0:
                ib = L - Ldiag0
                S.op("dve", lambda e, k=k, ib=ib: e.tensor_tensor(
                        out=A[k], in0=A[k], in1=self.cb[:, CB_TRI7 + (3 - ib) * 128:CB_TRI7 + (7 - ib) * 128],
                        op=ALU.mult),
                     reads=[A_b[k], cb_], writes=[A_b[k]])

        def s4(i):
            L = Ls[i]
            k = (ucount + i) % 2
            S.op("pe", lambda e, k=k, L=L, i=i: e.matmul(pO[hp, :], lhsT=v3[:, L, hp], rhs=A[k],
                                                         start=(i == 0), stop=(i == n - 1)),
                 reads=vb_ + [A_b[k]], writes=[pOb])

        for step in range(n + 3):
            if step < n:
                s0(step)
            if 0 <= step - 1 < n:
                s1(step - 1)
            if 0 <= step - 2 < n:
                s2(step - 2)
                s3(step - 2)
            if 0 <= step - 3 < n:
                s4(step - 3)
        S.op("dve", lambda e: e.tensor_copy(out=self.xn[hp, oc, qsl], in_=pO[hp, :]),
             reads=[pOb], writes=[self.xn_b[oc][t]])
        return ucount + n

    def xattn(self, layer):
        S = self.S
        after = S.barrier()
        cb_ = self.const_b
        memT = self.sf[:, 0:2048].rearrange("p (c m) -> p c m", c=NCH)
        memT_b = S.buf("memT", after)
        rr = [self.sf[:, 2048 + i * TC:2048 + (i + 1) * TC] for i in range(2)]
        rr_b = S.bufs("xrr", 2, after)
        qn = self.sbf[:, 0:4096].rearrange("p (c t) -> p c t", c=NCH)
        qn_b = S.bufs("qn", NCH, after)
        Pm = [self.sbf[:, 4096 + i * 1024:4096 + (i + 1) * 1024].rearrange("p (m t) -> p m t", m=2) for i in range(2)]
        Pm_b = S.bufs("Pm", 2, after)
        rd = [self.sbf[:, 6144 + i * TC:6144 + (i + 1) * TC] for i in range(2)]
        S.dma("sp", memT, self.memT_d.ap().rearrange("(c p) m -> p c m", p=128), writes=[memT_b])
        mn_ap, mn_b = self.ring.alloc(1)
        memn = mn_ap.rearrange("p (c m) -> p c m", c=NCH)
        self.rstd_from([(memT[:, c, :], [memT_b]) for c in range(NCH)], self.onesm, 256, self.lnb(0), rr[0][:, 0:256], rr_b[0])
        for c in range(NCH):
            S.op("dve", lambda e, c=c: e.scalar_tensor_tensor(
                    out=memn[:, c, :], in0=memT[:, c, :], scalar=self.gcol(f"xam_{layer}", c), in1=rr[0][:, 0:256],
                    op0=ALU.mult, op1=ALU.mult),
                 reads=[memT_b, rr_b[0], cb_], writes=mn_b)
        wkv = self.w["xa_w_kv"].ap()[layer]
        kt_ap, kt_b = self.ring.alloc(1)
        KT = kt_ap.rearrange("p (c m) -> p c m", c=NCH)
        vm_ap, vm_b = self.ring.alloc(1)
        Vm = vm_ap.rearrange("p (m c) -> p m c", m=2)
        for hd in range(4):
            wv, wb = self.load_cols(wkv, hd * 256)
            pk = [self.ps[0], self.ps[1]]
            pkb = [self.ps_b[0], self.ps_b[1]]
            for cc in range(2):
                for c in range(NCH):
                    S.op("pe", lambda e, cc=cc, c=c, wv=wv: e.matmul(pk[cc][:, 0:256], lhsT=wv[:, c, cc * 128:(cc + 1) * 128],
                                                                     rhs=memn[:, c, :], start=(c == 0), stop=(c == NCH - 1)),
                         reads=wb + mn_b, writes=[pkb[cc]])
            r, rb = rr[1][:, 0:256], rr_b[1]
            self.rstd_from([(pk[cc][:, 0:256], [pkb[cc]]) for cc in range(2)], self.ones256, 256, self.lnb(0), r, rb)
            for cc in range(2):
                S.op("dve", lambda e, cc=cc, hd=hd, r=r: e.scalar_tensor_tensor(
                        out=KT[:, 2 * hd + cc, :], in0=pk[cc][:, 0:256], scalar=self.gcol(f"xak_{layer}", cc), in1=r,
                        op0=ALU.mult, op1=ALU.mult),
                     reads=[pkb[cc], rb, cb_], writes=kt_b)
        for g in range(4):
            wv, wb = self.load_cols(wkv, D + g * 256)
            for mb in range(2):
                pv, pvb = self.ps[2 + mb], self.ps_b[2 + mb]
                for c in range(NCH):
                    S.op("pe", lambda e, pv=pv, c=c, mb=mb, wv=wv: e.matmul(
                            pv[:, 0:256], lhsT=memn[:, c, mb * 128:(mb + 1) * 128], rhs=wv[:, c, :],
                            start=(c == 0), stop=(c == NCH - 1)),
                         reads=wb + mn_b, writes=[pvb])
                S.op("act", lambda e, pv=pv, mb=mb, g=g: e.activation(out=Vm[:, mb, g * 256:(g + 1) * 256], in_=pv[:, 0:256],
                                                                      func=AF.Copy),
                     reads=[pvb], writes=vm_b)
        wq = self.w["xa_w_q"].ap()[layer]
        wo = self.w["xa_w_o"].ap()[layer]
        WQ = [self.load_cols(wq, g * 256) for g in range(4)]
        xn = self.xn
        for t in range(NTC):
            sl = slice(t * TC, (t + 1) * TC)
            for hd in range(4):
                wv, wb = WQ[hd]
                pq = [self.ps[0], self.ps[1]]
                pqb = [self.ps_b[0], self.ps_b[1]]
                for cc in range(2):
                    for c in range(NCH):
                        S.op("pe", lambda e, cc=cc, c=c, wv=wv, sl=sl: e.matmul(
                                pq[cc][:, :], lhsT=wv[:, c, cc * 128:(cc + 1) * 128], rhs=xn[:, c, sl],
                                start=(c == 0), stop=(c == NCH - 1)),
                             reads=wb + [self.xn_b[c][t]], writes=[pqb[cc]])
                ri = (t * 4 + hd) % 2
                r, rb = rr[ri], rr_b[ri]
                self.rstd_from([(pq[cc][:, :], [pqb[cc]]) for cc in range(2)], self.ones256, TC, self.lnb(2), r, rb)
                for cc in range(2):
                    S.op("dve", lambda e, cc=cc, hd=hd, r=r: e.scalar_tensor_tensor(
                            out=qn[:, 2 * hd + cc, :], in0=pq[cc][:, :], scalar=self.gcol(f"xaq_{layer}", cc), in1=r,
                            op0=ALU.mult, op1=ALU.mult),
                         reads=[pqb[cc], rb, cb_], writes=[qn_b[2 * hd + cc]])
            for hd in range(4):
                pi = (t * 4 + hd) % 2
                Pt, Ptb = Pm[pi], Pm_b[pi]
                for mb in range(2):
                    pz, pzb = self.ps[2 + mb], self.ps_b[2 + mb]
                    for cc in range(2):
                        S.op("pe", lambda e, pz=pz, mb=mb, cc=cc, hd=hd: e.matmul(
                                pz[:, :], lhsT=KT[:, 2 * hd + cc, mb * 128:(mb + 1) * 128], rhs=qn[:, 2 * hd + cc, :],
                                start=(cc == 0), stop=(cc == 1)),
                             reads=kt_b + [qn_b[2 * hd + cc]], writes=[pzb])
                    S.op("act", lambda e, pz=pz, mb=mb, Pt=Pt: e.activation(out=Pt[:, mb, :], in_=pz[:, :], func=AF.Exp),
                         reads=[pzb], writes=[Ptb])
                pd, pdb = self.ps[4], self.ps_b[4]
                for mb in range(2):
                    S.op("pe", lambda e, mb=mb, Pt=Pt: e.matmul(pd[:, :], lhsT=self.ones1[:, :], rhs=Pt[:, mb, :],
                                                              start=(mb == 0), stop=(mb == 1)),
                         reads=[cb_, Ptb], writes=[pdb])
                r, rb = rr[pi], rr_b[pi]
                S.op("dve", lambda e, r=r: e.reciprocal(out=r, in_=pd[:, :]), reads=[pdb], writes=[rb])
                for dv in range(2):
                    po, pob = self.ps[5 + dv], self.ps_b[5 + dv]
                    for mb in range(2):
                        S.op("pe", lambda e, po=po, mb=mb, dv=dv, hd=hd, Pt=Pt: e.matmul(
                                po[:, :], lhsT=Vm[:, mb, hd * 256 + dv * 128:hd * 256 + (dv + 1) * 128], rhs=Pt[:, mb, :],
                                start=(mb == 0), stop=(mb == 1)),
                             reads=vm_b + [Ptb], writes=[pob])
                    S.op("dve", lambda e, po=po, dv=dv, hd=hd, sl=sl, r=r: e.tensor_tensor(
                            out=xn[:, 2 * hd + dv, sl], in0=po[:, :], in1=r, op=ALU.mult),
                         reads=[pob, rb], writes=[self.xn_b[2 * hd + dv][t]])
        self.out_proj(wo)

    def build(self):
        self.setup()
        ph = self.phases
        for l in range(2):
            if f"ffn1_{l}" in ph:
                self.rmsnorm_x(f"ffn1_{l}")
                self.ffn(self.w["ffn1_w_gu"], self.w["ffn1_w_down"], l)
            if f"mix_{l}" in ph:
                self.rmsnorm_x(f"mix_{l}")
                self.mixer_proj(l)
                if l == 0:
                    self.banded(0, [(j, j // 2, 0, j, (j // 2) * 64) for j in range(4)], 2, 1,
                                CB_MULTA, CF_NSLA, slopes(8), True)
                    self.stick(0, [(4 + j, 2 + j, 128 + j * 128, 4 + j) for j in range(4)])
                    if "noout" not in ph:
                        self.out_proj(self.w["ev_w_out"].ap()[0])
                else:
                    self.banded(1, [(j, j, j * 128, j, None) for j in range(8)], 17, 16,
                                CB_MULTC, CF_NSLC, slopes(16), False)
                    if "noout" not in ph:
                        self.out_proj(self.w["od_w_out"].ap()[0])
            if f"xa_{l}" in ph:
                self.rmsnorm_x(f"xa_{l}")
                self.xattn(l)
            if f"ffn2_{l}" in ph:
                self.rmsnorm_x(f"ffn2_{l}")
                self.ffn(self.w["ffn2_w_gu"], self.w["ffn2_w_down"], l)
        self.finish()
        self.S.emit(self.nc, self.es)


ALL_PHASES = tuple(f"{p}_{l}" for l in range(2) for p in ("ffn1", "mix", "xa", "ffn2"))


def build_nc(phases=ALL_PHASES, dbg=False):
    nc = bass.Bass("TRN2", target_bir_lowering=False)
    es = ExitStack()
    with es:
        p = Prog(nc, es, phases, dbg)
        p.build()
    nc.used_w = set(p.w.keys())
    return nc


def make_in_maps(inp, x_override=None, used_w=None, ncores=NCORES):
    gains = build_gains(inp)
    cb, cf = build_consts()
    x = np.asarray(inp["x"], np.float32) if x_override is None else x_override
    mem = np.asarray(inp["mem"], np.float32)
    shared = {k: np.ascontiguousarray(np.asarray(inp[k], np.float32)) for k in W_SHAPES
              if used_w is None or k in used_w}
    maps = []
    for core in range(ncores):
        b, h = core // 2, core % 2
        m = dict(shared)
        m["xT"] = np.ascontiguousarray(x[b, h * TOK:(h + 1) * TOK, :].T)
        m["memT"] = np.ascontiguousarray(mem[b].T)
        m["gains"] = gains
        m["cb"] = cb
        m["cf"] = cf
        m["flag"] = np.full((128, 64), float(h), np.float32)
        maps.append(m)
    return maps


def kernel(**inputs):
    nc = build_nc()
    maps = make_in_maps(inputs, used_w=nc.used_w)
    res = run_bass_kernel_spmd(nc, maps, core_ids=list(range(NCORES)))
    out = np.empty((4, 4096, D), np.float32)
    for core in range(NCORES):
        b, h = core // 2, core % 2
        out[b, h * TOK:(h + 1) * TOK, :] = np.asarray(res.results[core]["outT"]).T
    return out
```

```python
from contextlib import ExitStack
import numpy as np
import concourse.bass as bass
import concourse.mybir as mybir
from concourse.bass_utils import run_bass_kernel_spmd

F32 = mybir.dt.float32
BF16 = mybir.dt.bfloat16
AF = mybir.ActivationFunctionType
ALU = mybir.AluOpType

D = 1024
NCH = 8
TOK = 2048
TC = 512
NTC = TOK // TC
DFF = 2816
NFC = DFF // 128
EPS = 1e-6
NCORES = 8
NGROUPS = [4]


class Chan:
    def __init__(self, name):
        self.name = name
        self.sem = None
        self.cnt = 0


class Buf:
    def __init__(self, name, after=()):
        self.name = name
        self.writers = {}
        self.readers = {}
        self.chan = None
        for i, o in enumerate(after):
            self.readers[("init", i)] = o


class Op:
    __slots__ = ("eng", "fn", "deps", "dma", "chan", "val", "flag", "key", "inc")

    def __init__(self, eng, fn, dma=False, chan=None):
        self.eng = eng
        self.fn = fn
        self.deps = []
        self.dma = dma
        self.chan = chan
        self.val = 0
        self.flag = False
        self.key = None
        self.inc = 16


ENGS = ("pe", "act", "dve", "pool", "sp")


class Sched:
    def __init__(self):
        self.q = {e: [] for e in ENGS}
        self.chans = []
        self.nbuf = 0

    def buf(self, name, after=()):
        return Buf(name, after)

    def bufs(self, name, n, after=()):
        return [Buf(f"{name}{i}", after) for i in range(n)]

    def chan(self, name):
        c = Chan(f"{name}_{len(self.chans)}")
        self.chans.append(c)
        return c

    def _add(self, op, reads, writes):
        deps = {}

        def need(o):
            if o is op:
                return
            if (not o.dma) and (not op.dma) and o.eng == "pe" and op.eng == "pe":
                return
            deps[id(o)] = o

        for b in reads:
            for k, o in b.writers.items():
                need(o)
        for b in writes:
            for k, o in b.writers.items():
                if op.dma and o.dma:
                    continue
                need(o)
            for k, o in b.readers.items():
                need(o)
        op.deps = list(deps.values())
        for o in op.deps:
            o.flag = True
        for b in writes:
            if op.dma:
                b.writers = {k: o for k, o in b.writers.items() if o.dma}
                b.writers[op.key] = op
            else:
                b.writers = {op.key: op}
            b.readers = {}
        for b in reads:
            if b not in writes:
                b.readers[op.key] = op
        self.q[op.eng].append(op)
        return op

    def op(self, eng, fn, reads=(), writes=()):
        o = Op(eng, fn)
        o.key = eng
        return self._add(o, reads, writes)

    def dma(self, eng, out, in_, reads=(), writes=(), chan=None):
        if chan is None:
            b = writes[0]
            if b.chan is None:
                b.chan = self.chan("c_" + b.name)
            chan = b.chan
        o = Op(eng, lambda e: e.dma_start(out=out, in_=in_), dma=True, chan=chan)
        chan.cnt += 16
        o.val = chan.cnt
        o.key = ("ch", id(chan))
        return self._add(o, reads, writes)

    def coll(self, fn, reads=(), writes=()):
        ch = self.chan(f"cc{len(self.chans)}")
        o = Op("pool", fn, dma=True, chan=ch)
        o.inc = 1
        ch.cnt += 1
        o.val = ch.cnt
        o.key = ("ch", id(ch))
        return self._add(o, reads, writes)

    def barrier(self):
        res = []
        for e in ENGS:
            for o in reversed(self.q[e]):
                if not o.dma:
                    o.flag = True
                    res.append(o)
                    break
        seen = set()
        for e in ENGS:
            for o in reversed(self.q[e]):
                if o.dma and id(o.chan) not in seen:
                    seen.add(id(o.chan))
                    res.append(o)
        return res

    def emit(self, nc, es):
        for e in ENGS:
            c = 0
            for o in self.q[e]:
                if not o.dma and o.flag:
                    c += 1
                    o.val = c
        esem = {e: es.enter_context(nc.semaphore("sem_" + e)) for e in ENGS}
        for ch in self.chans:
            ch.sem = es.enter_context(nc.semaphore(ch.name))
        block = es.enter_context(nc.Block())

        def body_for(ename):
            def body(e):
                waited = {}
                for o in self.q[ename]:
                    need = {}
                    for d in o.deps:
                        sem = d.chan.sem if d.dma else esem[d.eng]
                        k = id(sem)
                        if waited.get(k, 0) >= d.val:
                            continue
                        if k not in need or need[k][1] < d.val:
                            need[k] = (sem, d.val)
                    for k, (sem, val) in need.items():
                        e.wait_ge(sem, val)
                        waited[k] = val
                    ins = o.fn(e)
                    if ins is None:
                        continue
                    if o.dma:
                        ins.then_inc(o.chan.sem, o.inc)
                    elif o.flag:
                        ins.then_inc(esem[ename], 1)
            return body

        block.tensor(body_for("pe"))
        block.scalar(body_for("act"))
        block.vector(body_for("dve"))
        block.gpsimd(body_for("pool"))
        block.sync(body_for("sp"))


def _col_chunks(v):
    v = np.asarray(v, np.float32)
    return np.ascontiguousarray(v.reshape(-1, 128).T)


def _gain_layout():
    cols = {}
    n = 0
    for l in range(2):
        for nm, w in ((f"ffn1_{l}", 8), (f"mix_{l}", 8), (f"xa_{l}", 8), (f"xam_{l}", 8), (f"ffn2_{l}", 8),
                      (f"xaq_{l}", 2), (f"xak_{l}", 2)):
            cols[nm] = n
            n += w
    for nm, w in (("evq", 1), ("evk", 1), ("odq", 1), ("odk", 1), ("sinks", 8)):
        cols[nm] = n
        n += w
    return cols, n


GCOL, NGCOL = _gain_layout()


def build_gains(inp):
    g = np.zeros((128, NGCOL), np.float32)
    for l in range(2):
        g[:, GCOL[f"ffn1_{l}"]:][:, :8] = _col_chunks(inp["ffn1_norm"][l])
        g[:, GCOL[f"mix_{l}"]:][:, :8] = _col_chunks(inp["mix_norm"][l])
        g[:, GCOL[f"xa_{l}"]:][:, :8] = _col_chunks(inp["xa_norm"][l])
        g[:, GCOL[f"xam_{l}"]:][:, :8] = _col_chunks(inp["xa_mem_norm"][l])
        g[:, GCOL[f"ffn2_{l}"]:][:, :8] = _col_chunks(inp["ffn2_norm"][l])
        g[:, GCOL[f"xaq_{l}"]:][:, :2] = _col_chunks(inp["xa_q_gain"][l])
        g[:, GCOL[f"xak_{l}"]:][:, :2] = _col_chunks(inp["xa_k_gain"][l])
    g[:, GCOL["evq"]] = np.tile(np.asarray(inp["ev_q_gain"][0], np.float32), 2)
    g[:, GCOL["evk"]] = np.tile(np.asarray(inp["ev_k_gain"][0], np.float32), 2)
    g[:, GCOL["odq"]] = np.tile(np.asarray(inp["od_q_gain"][0], np.float32), 2)
    g[:, GCOL["odk"]] = np.tile(np.asarray(inp["od_k_gain"][0], np.float32), 2)
    g[:, GCOL["sinks"]:GCOL["sinks"] + 8] = np.asarray(inp["ev_sinks"][0], np.float32)[None, :]
    return g


NTC_MASK = 23
CB_NEGU, CB_NEGI, CB_TRI7 = 0, 128, 256
CB_MULTC = CB_TRI7 + 7 * 128
CB_MULTA = CB_MULTC + NTC_MASK * 128
CB_N = CB_MULTA + 8 * 128
CF_D0, CF_D0C, CF_NSLC, CF_NSLA, CF_LNS = 0, 128, 256, 256 + 16 * 17, 256 + 16 * 17 + 8 * 2
CF_N = CF_LNS + 4


def slopes(n):
    return [2.0 ** (-8.0 * (i + 1) / n) for i in range(n)]


def build_consts():
    s = np.arange(128)[:, None]
    t = np.arange(128)[None, :]
    cb = np.zeros((128, CB_N), np.float32)
    cb[:, CB_NEGU:CB_NEGU + 128] = -(s >= t).astype(np.float32)
    cb[:, CB_NEGI:CB_NEGI + 128] = -(s == t).astype(np.float32)
    cb[:, CB_TRI7 + 3 * 128:CB_TRI7 + 4 * 128] = (s < t)
    cb[:, CB_TRI7 + 4 * 128:CB_TRI7 + 7 * 128] = 1.0
    for j in range(17):
        d = 128 * j + t - s
        m = ((d >= 0) & (d <= 128)).astype(np.float32) + ((d >= 0) & (d % 4 == 0) & (d <= 512)) \
            + ((d >= 0) & (d % 16 == 0) & (d <= 2048))
        cb[:, CB_MULTC + (3 + j) * 128:CB_MULTC + (4 + j) * 128] = m
    for j in range(2):
        d = 128 * j + t - s
        cb[:, CB_MULTA + (3 + j) * 128:CB_MULTA + (4 + j) * 128] = ((d >= 0) & (d <= 127))
    cf = np.zeros((128, CF_N), np.float32)
    cf[:, CF_D0:CF_D0 + 128] = t - s
    cf[:, CF_D0C:CF_D0C + 128] = np.maximum(t - s, 0)
    sc = slopes(16)
    for h in range(16):
        for j in range(17):
            cf[:, CF_NSLC + h * 17 + j] = -sc[h] * 128.0 * j
    sa = slopes(8)
    for h in range(8):
        for j in range(2):
            cf[:, CF_NSLA + h * 2 + j] = -sa[h] * 128.0 * j
    cf[:, CF_LNS + 0] = 0.0
    cf[:, CF_LNS + 1] = np.log(0.125)
    cf[:, CF_LNS + 2] = np.log(1.0 / 16.0)
    cf[:, CF_LNS + 3] = 1.0
    return cb, cf


W_SHAPES = {
    "ffn1_w_gu": [2, D, 2 * DFF], "ffn1_w_down": [2, DFF, D],
    "ffn2_w_gu": [2, D, 2 * DFF], "ffn2_w_down": [2, DFF, D],
    "ev_w_in": [1, D, 2304], "ev_w_out": [1, D, D], "od_w_in": [1, D, 3072], "od_w_out": [1, D, D],
    "xa_w_q": [2, D, D], "xa_w_kv": [2, D, 2 * D], "xa_w_o": [2, D, D],
}
UNIT = 2048
NUNITS = 14


class Ring:
    def __init__(self, S, tile, n):
        self.tile = tile
        self.n = n
        self.bufs = [S.buf(f"ring{i}") for i in range(n)]
        self.pos = 0

    def alloc(self, k=1):
        if self.pos + k > self.n:
            self.pos = 0
        u = self.pos
        self.pos += k
        return self.tile[:, u * UNIT:(u + k) * UNIT], self.bufs[u:u + k]


class Prog:
    def __init__(self, nc, es, phases, dbg=False):
        self.nc = nc
        self.es = es
        self.S = S = Sched()
        self.phases = phases
        dt = nc.dram_tensor
        self.xT_d = dt("xT", [D, TOK], F32, kind="ExternalInput")
        self.memT_d = dt("memT", [D, 256], F32, kind="ExternalInput")
        self.gains_d = dt("gains", [128, NGCOL], F32, kind="ExternalInput")
        self.cb_d = dt("cb", [128, CB_N], F32, kind="ExternalInput")
        self.cf_d = dt("cf", [128, CF_N], F32, kind="ExternalInput")
        self.flag_d = dt("flag", [128, 128], F32, kind="ExternalInput")
        class _LazyW(dict):
            def __missing__(d, nm):
                d[nm] = dt(nm, W_SHAPES[nm], F32, kind="ExternalInput")
                return d[nm]
        self.w = _LazyW()
        self.dbg = dbg
        self.out_d = dt("outT", [D, TOK], F32, kind="ExternalOutput")
        self.out_buf = S.buf("outT")
        self.scr = {}
        dk = dict(kind="ExternalOutput") if dbg else {}
        if dbg:
            self.xn_d = dt("xn_dump", [D, TOK], BF16, kind="ExternalOutput")
        for l, (nk, vc) in enumerate(((6, 640), (8, 1024))):
            self.scr[l] = dict(
                q=dt(f"q{l}_d", [8 * 128, TOK], BF16, **dk), q_b=S.buf(f"q{l}_d"),
                kloc=[dt(f"kloc{l}_{c}", [128, TOK], BF16) for c in range(nk)],
                kloc_b=[S.buf(f"kloc{l}_{c}") for c in range(nk)],
                kall=[dt(f"kall{l}_{c}", [256, TOK], BF16) for c in range(nk)],
                kall_b=[S.buf(f"kall{l}_{c}") for c in range(nk)],
                vloc=[dt(f"vloc{l}_{j}", [256, vc], BF16) for j in range(8)],
                vloc_b=[S.buf(f"vloc{l}_{j}") for j in range(8)],
                vall=[dt(f"vall{l}_{j}", [512, vc], BF16) for j in range(8)],
                vall_b=[S.buf(f"vall{l}_{j}") for j in range(8)],
                nk=nk, vc=vc)

        sb = lambda name, shape, dtype: es.enter_context(nc.sbuf_tensor(name, shape, dtype))
        ps = lambda name: es.enter_context(nc.psum_tensor(name, [128, 512], F32))
        self.xT = sb("xT_sb", [128, NCH, TOK], F32)
        self.xT_b = [[S.buf(f"xT{c}_{t}") for t in range(NTC)] for c in range(NCH)]
        self.xn = sb("xn_sb", [128, NCH, TOK], BF16)
        self.xn_b = [[S.buf(f"xn{c}_{t}") for t in range(NTC)] for c in range(NCH)]
        self.gains = sb("gains_sb", [128, NGCOL], F32)
        self.cb = sb("cb_sb", [128, CB_N], BF16)
        self.cf = sb("cf_sb", [128, CF_N], F32)
        self.const_b = S.buf("consts")
        self.onesm = sb("onesm", [128, 128], BF16)
        self.ones256 = sb("ones256", [128, 128], BF16)
        self.ones1 = sb("ones1", [128, 128], BF16)
        self.bd64 = sb("bd64", [128, 128], BF16)
        self.flagt = sb("flagt", [128, 128], F32)
        self.flagb = sb("flagb", [128, 128], BF16)
        self.epsc = sb("epsc", [128, 2], F32)
        self.exps = sb("exps", [128, 8], F32)
        self.sq = [sb(f"sq{i}", [128, TC], BF16) for i in range(4)]
        self.sq_b = S.bufs("sq", 4)
        self.rstd = [sb(f"rstd{i}", [128, TC], F32) for i in range(2)]
        self.rstd_b = S.bufs("rstd", 2)
        self.ring_t = sb("ring", [128, NUNITS * UNIT], BF16)
        self.ring = Ring(S, self.ring_t, NUNITS)
        self.sf = sb("scr_f32", [128, 3072], F32)
        self.sbf = sb("scr_bf", [128, 8192], BF16)
        self.ps = [ps(f"ps{i}") for i in range(8)]
        self.ps_b = S.bufs("ps", 8)
        self.norm_i = 0
        self.sq_i = 0

    def gcol(self, name, i=0):
        c = GCOL[name] + i
        return self.gains[:, c:c + 1]

    def setup(self):
        S = self.S
        cb_ = self.const_b
        S.dma("sp", self.gains[:, :], self.gains_d[:, :], writes=[cb_])
        S.dma("sp", self.cf[:, :], self.cf_d[:, :], writes=[cb_])
        S.dma("sp", self.flagt[:, :], self.flag_d[:, :], writes=[cb_])
        for i in range(0, CB_N, 1024):
            j = min(CB_N, i + 1024)
            S.dma("pool", self.cb[:, i:j], self.cb_d[:, i:j], writes=[cb_])
        S.op("pool", lambda e: e.memset(self.onesm[:, :], 1.0 / D), writes=[cb_])
        S.op("pool", lambda e: e.memset(self.ones256[:, :], 1.0 / 256), writes=[cb_])
        S.op("pool", lambda e: e.memset(self.ones1[:, :], 1.0), writes=[cb_])
        S.op("pool", lambda e: e.memset(self.bd64[:, :], 0.0), writes=[cb_])
        S.op("pool", lambda e: e.memset(self.bd64[0:64, 0:64], 1.0 / 64), writes=[cb_])
        S.op("pool", lambda e: e.memset(self.bd64[64:128, 64:128], 1.0 / 64), writes=[cb_])
        S.op("pool", lambda e: e.memset(self.epsc[:, :], EPS), writes=[cb_])
        S.op("pool", lambda e: e.memset(self.sbf[:, :], 0.0), writes=[cb_])
        S.op("dve", lambda e: e.tensor_copy(out=self.flagb[:, :], in_=self.flagt[:, :]), reads=[cb_], writes=[cb_])
        sc = GCOL["sinks"]
        S.op("act", lambda e: e.activation(out=self.exps[:, :], in_=self.gains[:, sc:sc + 8], func=AF.Exp),
             reads=[cb_], writes=[cb_])
        xv = self.xT_d.ap().rearrange("(c p) t -> p c t", p=128)
        for t in range(NTC):
            for c in range(NCH):
                S.dma("sp", self.xT[:, c, t * TC:(t + 1) * TC], xv[:, c, t * TC:(t + 1) * TC],
                      writes=[self.xT_b[c][t]])

    def finish(self):
        S = self.S
        ov = self.out_d.ap().rearrange("(c p) t -> p c t", p=128)
        for t in range(NTC):
            for c in range(NCH):
                S.dma("sp", ov[:, c, t * TC:(t + 1) * TC], self.xT[:, c, t * TC:(t + 1) * TC],
                      reads=[self.xT_b[c][t]], writes=[self.out_buf])
        if self.dbg:
            xv = self.xn_d.ap().rearrange("(c p) t -> p c t", p=128)
            for t in range(NTC):
                for c in range(NCH):
                    S.dma("sp", xv[:, c, t * TC:(t + 1) * TC], self.xn[:, c, t * TC:(t + 1) * TC],
                          reads=[self.xn_b[c][t]], writes=[self.out_buf])
        S.op("sp", lambda e: None, reads=[self.out_buf])

    def rstd_from(self, srcs, ones, n, lnbias, out_rs, out_rsb, reads_extra=()):
        S = self.S
        i = self.norm_i % 2
        self.norm_i += 1
        pss, pssb = self.ps[6 + i], self.ps_b[6 + i]
        k = len(srcs)
        for c, (ap, bl) in enumerate(srcs):
            q = self.sq_i % 4
            self.sq_i += 1
            sq, sqb = self.sq[q], self.sq_b[q]
            S.op("act", lambda e, sq=sq, ap=ap: e.activation(out=sq[:, 0:n], in_=ap, func=AF.Square),
                 reads=list(bl), writes=[sqb])
            S.op("pe", lambda e, sq=sq, pss=pss, c=c: e.matmul(pss[:, 0:n], lhsT=ones[:, :], rhs=sq[:, 0:n],
                                                             start=(c == 0), stop=(c == k - 1)),
                 reads=[self.const_b, sqb], writes=[pssb])
        S.op("act", lambda e: e.activation(out=out_rs, in_=pss[:, 0:n], func=AF.Ln, bias=self.epsc[:, 0:1]),
             reads=[pssb, self.const_b], writes=[out_rsb])
        S.op("act", lambda e: e.activation(out=out_rs, in_=out_rs, func=AF.Exp, scale=-0.5, bias=lnbias),
             reads=[out_rsb, self.const_b], writes=[out_rsb])

    def lnb(self, i):
        return self.cf[:, CF_LNS + i:CF_LNS + i + 1]

    def rmsnorm_x(self, gname):
        S = self.S
        xT, xn = self.xT, self.xn
        for t in range(NTC):
            sl = slice(t * TC, (t + 1) * TC)
            i = self.norm_i % 2
            rs, rsb = self.rstd[i], self.rstd_b[i]
            self.rstd_from([(xT[:, c, sl], [self.xT_b[c][t]]) for c in range(NCH)], self.onesm, TC,
                           self.lnb(0), rs[:, :], rsb)
            for c in range(NCH):
                S.op("dve", lambda e, c=c, rs=rs, sl=sl: e.scalar_tensor_tensor(
                        out=xn[:, c, sl], in0=xT[:, c, sl], scalar=self.gcol(gname, c),
                        in1=rs[:, :], op0=ALU.mult, op1=ALU.mult),
                     reads=[self.xT_b[c][t], rsb, self.const_b], writes=[self.xn_b[c][t]])

    def load_cols(self, w_ap2d, c0, ncols=256):
        ap, bufs = self.ring.alloc(1)
        v = ap.rearrange("p (c f) -> p c f", c=NCH)
        src = w_ap2d.rearrange("(c p) f -> p c f", p=128)
        self.S.dma("pool", v[:, :, 0:ncols], src[:, :, c0:c0 + ncols], writes=bufs)
        return v, bufs

    def load_rows(self, w_ap2d, r0):
        ap, bufs = self.ring.alloc(1)
        v = ap.rearrange("p (j m) -> p j m", j=2)
        src = w_ap2d[r0:r0 + 256, :].rearrange("(j p) m -> p j m", p=128)
        self.S.dma("pool", v[:, :, :], src, writes=bufs)
        return v, bufs

    def ffn(self, wgu_d, wdn_d, layer):
        S = self.S
        after = S.barrier()
        sg = [self.sf[:, i * TC:(i + 1) * TC] for i in range(2)]
        sg_b = S.bufs("sg", 2, after)
        act = [self.sbf[:, i * 2 * TC:(i + 1) * 2 * TC].rearrange("p (j t) -> p j t", j=2) for i in range(3)]
        act_b = [[S.buf(f"act{i}_{j}", after) for j in range(2)] for i in range(3)]
        NG = NFC // 2
        gu2 = wgu_d.ap()[layer]
        dn2 = wdn_d.ap()[layer]
        W = {}

        def load(g):
            f0 = g * 256
            W[g] = (self.load_cols(gu2, f0), self.load_cols(gu2, DFF + f0), self.load_rows(dn2, f0))

        items = [(g, t) for g in range(NG) for t in range(NTC)]
        xn = self.xn

        def emit_gu(i):
            g, t = items[i]
            sl = slice(t * TC, (t + 1) * TC)
            a = i % 3
            for j in range(2):
                k = (2 * i + j) % 2
                pg, pgb = self.ps[k], self.ps_b[k]
                pu, pub = self.ps[2 + k], self.ps_b[2 + k]
                for (pp, ppb, gi) in ((pg, pgb, 0), (pu, pub, 1)):
                    wv, wb = W[g][gi]
                    for c in range(NCH):
                        S.op("pe", lambda e, pp=pp, wv=wv, c=c, j=j, sl=sl: e.matmul(
                                pp[:, :], lhsT=wv[:, c, j * 128:(j + 1) * 128], rhs=xn[:, c, sl],
                                start=(c == 0), stop=(c == NCH - 1)),
                             reads=wb + [self.xn_b[c][t]], writes=[ppb])
                S.op("act", lambda e, k=k, pg=pg: e.activation(out=sg[k], in_=pg[:, :], func=AF.Silu),
                     reads=[pgb], writes=[sg_b[k]])
                S.op("dve", lambda e, k=k, pu=pu, a=a, j=j: e.tensor_tensor(
                        out=act[a][:, j, :], in0=pu[:, :], in1=sg[k], op=ALU.mult),
                     reads=[pub, sg_b[k]], writes=[act_b[a][j]])

        def emit_down(i):
            g, t = items[i]
            sl = slice(t * TC, (t + 1) * TC)
            a = i % 3
            wv, wb = W[g][2]
            for m in range(NCH):
                k = 4 + (m % 2)
                py, pyb = self.ps[k], self.ps_b[k]
                for j in range(2):
                    S.op("pe", lambda e, py=py, j=j, m=m, wv=wv, a=a: e.matmul(
                            py[:, :], lhsT=wv[:, j, m * 128:(m + 1) * 128], rhs=act[a][:, j, :],
                            start=(j == 0), stop=(j == 1)),
                         reads=wb + [act_b[a][j]], writes=[pyb])
                S.op("dve", lambda e, py=py, m=m, sl=sl: e.scalar_tensor_tensor(
                        out=self.xT[:, m, sl], in0=py[:, :], scalar=0.5, in1=self.xT[:, m, sl],
                        op0=ALU.mult, op1=ALU.add),
                     reads=[pyb, self.xT_b[m][t]], writes=[self.xT_b[m][t]])

        PF = 3
        for g in range(min(PF, NG)):
            load(g)
        n = len(items)
        for i in range(n + 1):
            if i < n:
                g, t = items[i]
                if t == 1 and g + PF < NG:
                    load(g + PF)
                emit_gu(i)
            if i >= 1:
                emit_down(i - 1)

    def out_proj(self, w2d):
        S = self.S
        units = [self.load_rows(w2d, r * 256) for r in range(4)]
        for t in range(NTC):
            sl = slice(t * TC, (t + 1) * TC)
            for m in range(NCH):
                k = 4 + (m % 2)
                py, pyb = self.ps[k], self.ps_b[k]
                for oc in range(NCH):
                    wv, wb = units[oc // 2]
                    S.op("pe", lambda e, py=py, wv=wv, oc=oc, m=m, sl=sl: e.matmul(
                            py[:, :], lhsT=wv[:, oc % 2, m * 128:(m + 1) * 128], rhs=self.xn[:, oc, sl],
                            start=(oc == 0), stop=(oc == NCH - 1)),
                         reads=wb + [self.xn_b[oc][t]], writes=[pyb])
                S.op("dve", lambda e, py=py, m=m, sl=sl: e.tensor_tensor(
                        out=self.xT[:, m, sl], in0=py[:, :], in1=self.xT[:, m, sl], op=ALU.add),
                     reads=[pyb, self.xT_b[m][t]], writes=[self.xT_b[m][t]])

    def mixer_proj(self, layer):
        S = self.S
        after = S.barrier()
        sc = self.scr[layer]
        even = layer == 0
        w2d = (self.w["ev_w_in"] if even else self.w["od_w_in"]).ap()[0]
        nun = 9 if even else 12
        stg = [self.sbf[:, i * TC:(i + 1) * TC] for i in range(4)]
        stg_b = S.bufs("stg", 4, after)
        stg_ch = [S.chan(f"stgch{layer}_{i}") for i in range(4)]
        sqh = [self.sbf[:, (4 + i) * TC:(5 + i) * TC] for i in range(2)]
        sqh_b = S.bufs("sqh", 2, after)
        rr = [self.sf[:, i * TC:(i + 1) * TC] for i in range(2)]
        rr_b = S.bufs("rr", 2, after)
        vst = [self.sbf[:, 3072 + i * 1024:3072 + (i + 1) * 1024] for i in range(2)]
        vst_b = S.bufs("vst", 2, after)
        vst_ch = [S.chan(f"vstch{layer}_{i}") for i in range(2)]
        cnt = {"s": 0, "q": 0, "v": 0}
        gq, gk = ("evq", "evk") if even else ("odq", "odk")
        jobs = []
        if even:
            for j in range(4):
                jobs.append((j // 2, (j % 2) * 128, "norm", gq, 1, sc["q"], sc["q_b"], j))
            for g in range(2):
                jobs.append((2, ("dup", g * 64), "norm", gk, 0, sc["kloc"], sc["kloc_b"], g))
            for j in range(4):
                jobs.append((3 + j // 2, (j % 2) * 128, "scale", None, 0.125, sc["q"], sc["q_b"], 4 + j))
            for j in range(4):
                jobs.append((5 + j // 2, (j % 2) * 128, "scale", None, 1.0, sc["kloc"], sc["kloc_b"], 2 + j))
            vjobs = [[(2, 128, 128)], [(7, 0, 256), (8, 0, 256)]]
        else:
            for j in range(8):
                jobs.append((j // 2, (j % 2) * 128, "norm", gq, 1, sc["q"], sc["q_b"], j))
            for j in range(8):
                jobs.append((4 + j // 2, (j % 2) * 128, "norm", gk, 0, sc["kloc"], sc["kloc_b"], j))
            vjobs = [[(8, 0, 256), (9, 0, 256)], [(10, 0, 256), (11, 0, 256)]]
        U = {}

        def need(u):
            if u not in U:
                U[u] = self.load_cols(w2d, u * 256)
            return U[u]

        for u in range(nun):
            need(u)
        xn = self.xn
        for (u, off, kind, gname, par, dten, dbuf, drow) in jobs:
            wv, wb = need(u)
            for t in range(NTC):
                sl = slice(t * TC, (t + 1) * TC)
                k = cnt["q"] % 2
                cnt["q"] += 1
                pq, pqb = self.ps[k], self.ps_b[k]
                if isinstance(off, tuple):
                    c0 = off[1]
                    for half in range(2):
                        for c in range(NCH):
                            S.op("pe", lambda e, pq=pq, wv=wv, c=c, c0=c0, half=half, sl=sl: e.matmul(
                                    pq[half * 64:(half + 1) * 64, :], lhsT=wv[:, c, c0:c0 + 64], rhs=xn[:, c, sl],
                                    start=(c == 0), stop=(c == NCH - 1)),
                                 reads=wb + [self.xn_b[c][t]], writes=[pqb])
                else:
                    for c in range(NCH):
                        S.op("pe", lambda e, pq=pq, wv=wv, c=c, off=off, sl=sl: e.matmul(
                                pq[:, :], lhsT=wv[:, c, off:off + 128], rhs=xn[:, c, sl],
                                start=(c == 0), stop=(c == NCH - 1)),
                             reads=wb + [self.xn_b[c][t]], writes=[pqb])
                si = cnt["s"] % 4
                cnt["s"] += 1
                if kind == "norm":
                    r, rb = rr[k], rr_b[k]
                    self.rstd_from([(pq[:, :], [pqb])], self.bd64, TC, self.lnb(par), r, rb)
                    S.op("dve", lambda e, si=si, pq=pq, r=r, gname=gname: e.scalar_tensor_tensor(
                            out=stg[si], in0=pq[:, :], scalar=self.gcol(gname), in1=r, op0=ALU.mult, op1=ALU.mult),
                         reads=[pqb, rb, self.const_b], writes=[stg_b[si]])
                else:
                    S.op("act", lambda e, si=si, pq=pq, par=par: e.activation(
                            out=stg[si], in_=pq[:, :], func=AF.Copy, scale=float(par)),
                         reads=[pqb], writes=[stg_b[si]])
                if isinstance(dten, list):
                    S.dma("sp", dten[drow][:, sl], stg[si], reads=[stg_b[si]], writes=[dbuf[drow]], chan=stg_ch[si])
                    if t == NTC - 1:
                        self.gather(sc["kloc"][drow], sc["kloc_b"][drow], sc["kall"][drow], sc["kall_b"][drow])
                else:
                    S.dma("sp", dten[drow * 128:(drow + 1) * 128, sl], stg[si], reads=[stg_b[si]], writes=[dbuf],
                          chan=stg_ch[si])
        vc = sc["vc"]
        for nb in range(TOK // 128):
            tb, tsl = nb // 4, slice(nb * 128, (nb + 1) * 128)
            vi = cnt["v"] % 2
            cnt["v"] += 1
            col = 0
            for bi, grp in enumerate(vjobs):
                pv, pvb = self.ps[2 + 2 * vi + bi], self.ps_b[2 + 2 * vi + bi]
                pc = 0
                for (u, c0, n) in grp:
                    wv, wb = need(u)
                    for c in range(NCH):
                        S.op("pe", lambda e, pv=pv, wv=wv, c=c, c0=c0, n=n, pc=pc, tsl=tsl: e.matmul(
                                pv[:, pc:pc + n], lhsT=xn[:, c, tsl], rhs=wv[:, c, c0:c0 + n],
                                start=(c == 0), stop=(c == NCH - 1)),
                             reads=wb + [self.xn_b[c][tb]], writes=[pvb])
                    pc += n
                eng = "act" if bi == 0 else "dve"
                if eng == "act":
                    S.op("act", lambda e, pv=pv, vi=vi, col=col, pc=pc: e.activation(
                            out=vst[vi][:, col:col + pc], in_=pv[:, 0:pc], func=AF.Copy),
                         reads=[pvb], writes=[vst_b[vi]])
                else:
                    S.op("dve", lambda e, pv=pv, vi=vi, col=col, pc=pc: e.tensor_copy(
                            out=vst[vi][:, col:col + pc], in_=pv[:, 0:pc]),
                         reads=[pvb], writes=[vst_b[vi]])
                col += pc
            jv = nb // 2
            S.dma("sp", sc["vloc"][jv][(nb % 2) * 128:(nb % 2 + 1) * 128, :], vst[vi][:, 0:vc], reads=[vst_b[vi]],
                  writes=[sc["vloc_b"][jv]], chan=vst_ch[vi])
            if nb % 2 == 1:
                self.gather(sc["vloc"][jv], sc["vloc_b"][jv], sc["vall"][jv], sc["vall_b"][jv])

    def gather(self, src, src_b, dst, dst_b):
        S = self.S
        import os
        if os.environ.get("KDBG_NOCOLL"):
            n = src.shape[0]
            S.dma("sp", dst[0:n, :], src[:, :], reads=[src_b], writes=[dst_b])
            return
        groups = [[2 * i, 2 * i + 1] for i in range(NGROUPS[0])]
        S.coll(lambda e: e.collective_compute("AllGather", ALU.bypass, replica_groups=groups,
                                              ins=[src.ap().opt()], outs=[dst.ap().opt()]),
               reads=[src_b], writes=[dst_b])

    def load_attn_tiles(self, layer, qrow, krow, vcol, nprev):
        S = self.S
        sc = self.scr[layer]
        nk = sc["nk"]
        qh = []
        for hh in range(2):
            qa, qb_ = self.ring.alloc(1)
            lo, hi = hh * 64, (hh + 1) * 64
            zl, zh = (1 - hh) * 64, (2 - hh) * 64
            S.dma("sp", qa[lo:hi, :], sc["q"][qrow * 128 + lo:qrow * 128 + hi, :], reads=[sc["q_b"]], writes=qb_)
            S.op("pool", lambda e, qa=qa, zl=zl, zh=zh: e.memset(qa[zl:zh, :], 0.0), writes=qb_)
            qh.append((qa, qb_))
        ka, kb_ = self.ring.alloc(2)
        npc = nprev * 128
        S.dma("sp", ka[:, 0:npc], sc["kall"][krow][0:128, TOK - npc:TOK], reads=[sc["kall_b"][krow]], writes=kb_)
        S.dma("sp", ka[:, npc:npc + TOK], sc["kloc"][krow][:, :], reads=[sc["kloc_b"][krow]], writes=kb_)
        va, vb_ = self.ring.alloc(2)
        v3 = va.rearrange("p (n c) -> p n c", c=128)
        if nprev == 16:
            for j in range(8):
                S.dma("sp", v3[:, 2 * j:2 * j + 2, :],
                      sc["vall"][j][0:256, vcol:vcol + 128].rearrange("(n p) c -> p n c", p=128),
                      reads=[sc["vall_b"][j]], writes=vb_)
        else:
            assert nprev == 1
            S.dma("sp", v3[:, 0:1, :], sc["vall"][7][128:256, vcol:vcol + 128].rearrange("(n p) c -> p n c", p=128),
                  reads=[sc["vall_b"][7]], writes=vb_)
        for j in range(8):
            S.dma("sp", v3[:, nprev + 2 * j:nprev + 2 * j + 2, :],
                  sc["vloc"][j][:, vcol:vcol + 128].rearrange("(n p) c -> p n c", p=128),
                  reads=[sc["vloc_b"][j]], writes=vb_)
        S.op("dve", lambda e: e.tensor_scalar(out=va[:, 0:npc], in0=va[:, 0:npc], scalar1=self.flagt[:, 0:1],
                                              scalar2=1.0, op0=ALU.mult, op1=ALU.mult),
             reads=vb_ + [self.const_b], writes=vb_)
        return qh, (ka, kb_), (v3, vb_)

    def banded(self, layer, pairs, ND, nprev, mult_off, nsl_off, sl_list, sink):
        S = self.S
        after = S.barrier()
        NT = ND + 6
        msk = [self.sbf[:, i * 3072:i * 3072 + NT * 128] for i in range(2)]
        msk_b = S.bufs("msk", 2, after)
        P = [self.sbf[:, 6144 + i * TC:6144 + (i + 1) * TC] for i in range(3)]
        P_b = S.bufs("P", 3, after)
        tmp = [self.sf[:, i * TC:(i + 1) * TC] for i in range(2)]
        tmp_b = S.bufs("tmp", 2, after)
        cb_ = self.const_b
        hcount = 0
        ucount = 0
        ccount = 0
        for pi, (qrow, krow, vcol, oc, vh) in enumerate(pairs):
            qh, (ka, kb_), (v3, vb_) = self.load_attn_tiles(layer, qrow, krow, vcol, nprev)
            for hh in range(2):
                qa, qb_ = qh[hh]
                h = 2 * pi + hh
                hp = slice(hh * 64, (hh + 1) * 64)
                vs = hp if vh is None else slice(vh, vh + 64)
                mi = hcount % 2
                hcount += 1
                m, mb = msk[mi], msk_b[mi]
                S.op("pool", lambda e, m=m: e.memset(m[:, 0:384], 0.0), writes=[mb])
                S.op("pool", lambda e, m=m: e.memset(m[:, (ND + 3) * 128:(ND + 6) * 128], 0.0), writes=[mb])
                for j in range(ND):
                    src = CF_D0C if j == 0 else CF_D0
                    S.op("act", lambda e, m=m, j=j, src=src, h=h: e.activation(
                            out=m[:, (3 + j) * 128:(4 + j) * 128], in_=self.cf[:, src:src + 128], func=AF.Exp,
                            scale=float(-sl_list[h]), bias=self.cf[:, nsl_off + h * ND + j:nsl_off + h * ND + j + 1]),
                         reads=[cb_], writes=[mb])
                S.op("dve", lambda e, m=m: e.tensor_tensor(out=m, in0=m, in1=self.cb[:, mult_off:mult_off + NT * 128],
                                                          op=ALU.mult),
                     reads=[cb_, mb], writes=[mb])
                for t in range(NTC):
                    ucount = self._banded_chunk(t, ccount, ucount, nprev, ND, hp, vs, h, oc, sink, m, mb, P, P_b, tmp, tmp_b,
                                                qa, qb_, ka, kb_, v3, vb_)
                    ccount += 1

    def _banded_chunk(self, t, ccount, ucount, nprev, ND, hp, vs, h, oc, sink, m, mb, P, P_b, tmp, tmp_b,
                      qa, qb_, ka, kb_, v3, vb_):
        S = self.S
        cb_ = self.const_b
        full_v = (vs == hp)
        ci = ccount % 2
        po, pob = self.ps[2 + ci], self.ps_b[2 + ci]
        pd, pdb = self.ps[4 + ci], self.ps_b[4 + ci]
        Lq0 = nprev + 4 * t
        units = [(L, Lq0 - L) for L in range(max(0, Lq0 - (ND - 1)), Lq0 + 4)]
        n = len(units)
        qsl = slice(t * TC, (t + 1) * TC)

        def st_z(i):
            L, j0 = units[i]
            k = (ucount + i) % 2
            S.op("pe", lambda e, k=k, L=L: e.matmul(self.ps[k][:, :], lhsT=ka[:, L * 128:(L + 1) * 128],
                                                    rhs=qa[:, qsl], start=True, stop=True),
                 reads=kb_ + qb_, writes=[self.ps_b[k]])

        def st_p(i):
            L, j0 = units[i]
            k = (ucount + i) % 2
            p = (ucount + i) % 3
            S.op("act", lambda e, k=k, p=p: e.activation(out=P[p], in_=self.ps[k][:, :], func=AF.Exp),
                 reads=[self.ps_b[k]], writes=[P_b[p]])
            S.op("dve", lambda e, p=p, j0=j0: e.tensor_tensor(
                    out=P[p], in0=P[p], in1=m[:, (j0 + 3) * 128:(j0 + 7) * 128], op=ALU.mult),
                 reads=[P_b[p], mb], writes=[P_b[p]])

        def st_o(i):
            L, j0 = units[i]
            p = (ucount + i) % 3
            if full_v:
                S.op("pe", lambda e, p=p, L=L, i=i: e.matmul(po[:, :], lhsT=v3[:, L, :], rhs=P[p],
                                                             start=(i == 0), stop=(i == n - 1)),
                     reads=vb_ + [P_b[p]], writes=[pob])
            else:
                S.op("pe", lambda e, p=p, L=L, i=i: e.matmul(po[hp, :], lhsT=v3[:, L, vs], rhs=P[p],
                                                             start=(i == 0), stop=(i == n - 1)),
                     reads=vb_ + [P_b[p]], writes=[pob])
            ones = self.flagb[:, :] if L < nprev else self.ones1[:, :]
            S.op("pe", lambda e, p=p, ones=ones, i=i: e.matmul(pd[:, :], lhsT=ones, rhs=P[p],
                                                               start=(i == 0), stop=(i == n - 1)),
                 reads=[cb_, P_b[p]], writes=[pdb])

        for step in range(n + 2):
            if step < n:
                st_z(step)
            if 0 <= step - 1 < n:
                st_p(step - 1)
            if 0 <= step - 2 < n:
                st_o(step - 2)
        ti = ccount % 2
        tm, tmb = tmp[ti], tmp_b[ti]
        if sink:
            S.op("dve", lambda e: e.tensor_scalar(
                    out=tm[hp, :], in0=pd[hp, :], scalar1=self.exps[hp, h:h + 1], scalar2=0.0, op0=ALU.add, op1=ALU.add),
                 reads=[pdb, cb_], writes=[tmb])
            S.op("dve", lambda e: e.reciprocal(out=tm[hp, :], in_=tm[hp, :]),
                 reads=[tmb], writes=[tmb])
        else:
            S.op("dve", lambda e: e.reciprocal(out=tm[hp, :], in_=pd[hp, :]),
                 reads=[pdb], writes=[tmb])
        S.op("dve", lambda e: e.tensor_tensor(out=self.xn[hp, oc, qsl], in0=po[hp, :], in1=tm[hp, :], op=ALU.mult),
             reads=[pob, tmb], writes=[self.xn_b[oc][t]])
        return ucount + n

    def stick(self, layer, pairs):
        S = self.S
        after = S.barrier()
        nprev = 16
        ef = [self.sf[:, i * TC:(i + 1) * TC] for i in range(2)]
        ef_b = S.bufs("ef", 2, after)
        sp = [self.sbf[:, i * TC:(i + 1) * TC] for i in range(2)]
        sp_b = S.bufs("sp", 2, after)
        A = [self.sbf[:, (2 + i) * TC:(3 + i) * TC] for i in range(2)]
        A_b = S.bufs("A", 2, after)
        Rs = [self.sbf[:, (4 + i) * TC:(5 + i) * TC] for i in range(2)]
        Rs_b = S.bufs("Rs", 2, after)
        cb_ = self.const_b
        negU = self.cb[:, CB_NEGU:CB_NEGU + 128]
        negI = self.cb[:, CB_NEGI:CB_NEGI + 128]
        one_c = self.cf[:, CF_LNS + 3:CF_LNS + 4]
        ccount = 0
        ucount = 0
        for pi, (qrow, krow, vcol, oc) in enumerate(pairs):
            qh, (ka, kb_), (v3, vb_) = self.load_attn_tiles(layer, qrow, krow, vcol, nprev)
            for hh in range(2):
                qa, qb_ = qh[hh]
                hp = slice(hh * 64, (hh + 1) * 64)
                for t in range(NTC):
                    ucount = self._stick_chunk(t, ccount, ucount, nprev, hp, oc, ef, ef_b, sp, sp_b, A, A_b, Rs, Rs_b,
                                               qa, qb_, ka, kb_, v3, vb_)
                    ccount += 1

    def _stick_chunk(self, t, ccount, ucount, nprev, hp, oc, ef, ef_b, sp, sp_b, A, A_b, Rs, Rs_b,
                     qa, qb_, ka, kb_, v3, vb_):
        S = self.S
        cb_ = self.const_b
        negU = self.cb[:, CB_NEGU:CB_NEGU + 128]
        negI = self.cb[:, CB_NEGI:CB_NEGI + 128]
        one_c = self.cf[:, CF_LNS + 3:CF_LNS + 4]
        ci = ccount % 2
        pR, pRb = self.ps[4 + ci], self.ps_b[4 + ci]
        pO, pOb = self.ps[6 + ci], self.ps_b[6 + ci]
        qsl = slice(t * TC, (t + 1) * TC)
        Ldiag0 = nprev + 4 * t
        Ls = list(range(Ldiag0 + 3, -1, -1))
        n = len(Ls)

        def s0(i):
            L = Ls[i]
            k = (ucount + i) % 2
            S.op("pe", lambda e, k=k, L=L: e.matmul(self.ps[k][:, :], lhsT=ka[:, L * 128:(L + 1) * 128], rhs=qa[:, qsl],
                                                    start=True, stop=True),
                 reads=kb_ + qb_, writes=[self.ps_b[k]])

        def s1(i):
            L = Ls[i]
            k = (ucount + i) % 2
            S.op("act", lambda e, k=k: e.activation(out=ef[k], in_=self.ps[k][:, :], func=AF.Exp),
                 reads=[self.ps_b[k]], writes=[ef_b[k]])
            S.op("act", lambda e, k=k: e.activation(out=sp[k], in_=ef[k], func=AF.Ln, bias=one_c),
                 reads=[ef_b[k], cb_], writes=[sp_b[k]])
            if L >= Ldiag0:
                ib = L - Ldiag0
                S.op("dve", lambda e, k=k, ib=ib: e.tensor_tensor(
                        out=sp[k], in0=sp[k], in1=self.cb[:, CB_TRI7 + (3 - ib) * 128:CB_TRI7 + (7 - ib) * 128],
                        op=ALU.mult),
                     reads=[sp_b[k], cb_], writes=[sp_b[k]])

        def s2(i):
            L = Ls[i]
            k = (ucount + i) % 2
            pC, pCb = self.ps[2 + k], self.ps_b[2 + k]
            S.op("pe", lambda e, k=k, pC=pC: e.matmul(pC[:, :], lhsT=negU, rhs=sp[k], start=True, stop=False),
                 reads=[cb_, sp_b[k]], writes=[pCb])
            last = (i == 0)
            S.op("pe", lambda e, pC=pC, L=L, last=last: e.matmul(
                    pC[:, :], lhsT=ka[:, L * 128:(L + 1) * 128], rhs=qa[:, qsl], start=False, stop=last),
                 reads=kb_ + qb_, writes=[pCb])
            if i > 0:
                kp = (ucount + i - 1) % 2
                S.op("pe", lambda e, pC=pC, kp=kp: e.matmul(pC[:, :], lhsT=negI, rhs=Rs[kp], start=False, stop=True),
                     reads=[cb_, Rs_b[kp]], writes=[pCb])
            S.op("pe", lambda e, k=k, i=i: e.matmul(pR[:, :], lhsT=self.ones1[:, :], rhs=sp[k],
                                                   start=(i == 0), stop=(i == n - 1)),
                 reads=[cb_, sp_b[k]], writes=[pRb])
            if i < n - 1:
                S.op("dve", lambda e, k=k: e.tensor_copy(out=Rs[k], in_=pR[:, :]),
                     reads=[pRb], writes=[Rs_b[k]])

        def s3(i):
            L = Ls[i]
            k = (ucount + i) % 2
            pC, pCb = self.ps[2 + k], self.ps_b[2 + k]
            S.op("act", lambda e, k=k, pC=pC: e.activation(out=A[k], in_=pC[:, :], func=AF.Exp),
                 reads=[pCb], writes=[A_b[k]])
            if L >= Ldiag0:
                ib = L - Ldiag0
                S.op("dve", lambda e, k=k, ib=ib: e.tensor_tensor(
                        out=A[k], in0=A[k], in1=self.cb[:, CB_TRI7 + (3 - ib) * 128:CB_TRI7 + (7 - ib) * 128],
                        op=ALU.mult),
                     reads=[A_b[k], cb_], writes=[A_b[k]])

        def s4(i):
            L = Ls[i]
            k = (ucount + i) % 2
            S.op("pe", lambda e, k=k, L=L, i=i: e.matmul(pO[:, :], lhsT=v3[:, L, :], rhs=A[k],
                                                         start=(i == 0), stop=(i == n - 1)),
                 reads=vb_ + [A_b[k]], writes=[pOb])

        for step in range(n + 3):
            if step < n:
                s0(step)
            if 0 <= step - 1 < n:
                s1(step - 1)
            if 0 <= step - 2 < n:
                s2(step - 2)
                s3(step - 2)
            if 0 <= step - 3 < n:
                s4(step - 3)
        S.op("dve", lambda e: e.tensor_copy(out=self.xn[hp, oc, qsl], in_=pO[hp, :]),
             reads=[pOb], writes=[self.xn_b[oc][t]])
        return ucount + n

    def xattn(self, layer):
        S = self.S
        after = S.barrier()
        cb_ = self.const_b
        memT = self.sf[:, 0:2048].rearrange("p (c m) -> p c m", c=NCH)
        memT_b = S.buf("memT", after)
        rr = [self.sf[:, 2048 + i * TC:2048 + (i + 1) * TC] for i in range(2)]
        rr_b = S.bufs("xrr", 2, after)
        qn = self.sbf[:, 0:4096].rearrange("p (c t) -> p c t", c=NCH)
        qn_b = S.bufs("qn", NCH, after)
        Pm = [self.sbf[:, 4096 + i * 1024:4096 + (i + 1) * 1024].rearrange("p (m t) -> p m t", m=2) for i in range(2)]
        Pm_b = S.bufs("Pm", 2, after)
        rd = [self.sbf[:, 6144 + i * TC:6144 + (i + 1) * TC] for i in range(2)]
        S.dma("sp", memT, self.memT_d.ap().rearrange("(c p) m -> p c m", p=128), writes=[memT_b])
        mn_ap, mn_b = self.ring.alloc(1)
        memn = mn_ap.rearrange("p (c m) -> p c m", c=NCH)
        self.rstd_from([(memT[:, c, :], [memT_b]) for c in range(NCH)], self.onesm, 256, self.lnb(0), rr[0][:, 0:256], rr_b[0])
        for c in range(NCH):
            S.op("dve", lambda e, c=c: e.scalar_tensor_tensor(
                    out=memn[:, c, :], in0=memT[:, c, :], scalar=self.gcol(f"xam_{layer}", c), in1=rr[0][:, 0:256],
                    op0=ALU.mult, op1=ALU.mult),
                 reads=[memT_b, rr_b[0], cb_], writes=mn_b)
        wkv = self.w["xa_w_kv"].ap()[layer]
        kt_ap, kt_b = self.ring.alloc(1)
        KT = kt_ap.rearrange("p (c m) -> p c m", c=NCH)
        vm_ap, vm_b = self.ring.alloc(1)
        Vm = vm_ap.rearrange("p (m c) -> p m c", m=2)
        for hd in range(4):
            wv, wb = self.load_cols(wkv, hd * 256)
            pk = [self.ps[0], self.ps[1]]
            pkb = [self.ps_b[0], self.ps_b[1]]
            for cc in range(2):
                for c in range(NCH):
                    S.op("pe", lambda e, cc=cc, c=c, wv=wv: e.matmul(pk[cc][:, 0:256], lhsT=wv[:, c, cc * 128:(cc + 1) * 128],
                                                                     rhs=memn[:, c, :], start=(c == 0), stop=(c == NCH - 1)),
                         reads=wb + mn_b, writes=[pkb[cc]])
            r, rb = rr[1][:, 0:256], rr_b[1]
            self.rstd_from([(pk[cc][:, 0:256], [pkb[cc]]) for cc in range(2)], self.ones256, 256, self.lnb(0), r, rb)
            for cc in range(2):
                S.op("dve", lambda e, cc=cc, hd=hd, r=r: e.scalar_tensor_tensor(
                        out=KT[:, 2 * hd + cc, :], in0=pk[cc][:, 0:256], scalar=self.gcol(f"xak_{layer}", cc), in1=r,
                        op0=ALU.mult, op1=ALU.mult),
                     reads=[pkb[cc], rb, cb_], writes=kt_b)
        for g in range(4):
            wv, wb = self.load_cols(wkv, D + g * 256)
            for mb in range(2):
                pv, pvb = self.ps[2 + mb], self.ps_b[2 + mb]
                for c in range(NCH):
                    S.op("pe", lambda e, pv=pv, c=c, mb=mb, wv=wv: e.matmul(
                            pv[:, 0:256], lhsT=memn[:, c, mb * 128:(mb + 1) * 128], rhs=wv[:, c, :],
                            start=(c == 0), stop=(c == NCH - 1)),
                         reads=wb + mn_b, writes=[pvb])
                S.op("act", lambda e, pv=pv, mb=mb, g=g: e.activation(out=Vm[:, mb, g * 256:(g + 1) * 256], in_=pv[:, 0:256],
                                                                      func=AF.Copy),
                     reads=[pvb], writes=vm_b)
        wq = self.w["xa_w_q"].ap()[layer]
        wo = self.w["xa_w_o"].ap()[layer]
        WQ = [self.load_cols(wq, g * 256) for g in range(4)]
        xn = self.xn
        for t in range(NTC):
            sl = slice(t * TC, (t + 1) * TC)
            for hd in range(4):
                wv, wb = WQ[hd]
                pq = [self.ps[0], self.ps[1]]
                pqb = [self.ps_b[0], self.ps_b[1]]
                for cc in range(2):
                    for c in range(NCH):
                        S.op("pe", lambda e, cc=cc, c=c, wv=wv, sl=sl: e.matmul(
                                pq[cc][:, :], lhsT=wv[:, c, cc * 128:(cc + 1) * 128], rhs=xn[:, c, sl],
                                start=(c == 0), stop=(c == NCH - 1)),
                             reads=wb + [self.xn_b[c][t]], writes=[pqb[cc]])
                ri = (t * 4 + hd) % 2
                r, rb = rr[ri], rr_b[ri]
                self.rstd_from([(pq[cc][:, :], [pqb[cc]]) for cc in range(2)], self.ones256, TC, self.lnb(2), r, rb)
                for cc in range(2):
                    S.op("dve", lambda e, cc=cc, hd=hd, r=r: e.scalar_tensor_tensor(
                            out=qn[:, 2 * hd + cc, :], in0=pq[cc][:, :], scalar=self.gcol(f"xaq_{layer}", cc), in1=r,
                            op0=ALU.mult, op1=ALU.mult),
                         reads=[pqb[cc], rb, cb_], writes=[qn_b[2 * hd + cc]])
            for hd in range(4):
                pi = (t * 4 + hd) % 2
                Pt, Ptb = Pm[pi], Pm_b[pi]
                for mb in range(2):
                    pz, pzb = self.ps[2 + mb], self.ps_b[2 + mb]
                    for cc in range(2):
                        S.op("pe", lambda e, pz=pz, mb=mb, cc=cc, hd=hd: e.matmul(
                                pz[:, :], lhsT=KT[:, 2 * hd + cc, mb * 128:(mb + 1) * 128], rhs=qn[:, 2 * hd + cc, :],
                                start=(cc == 0), stop=(cc == 1)),
                             reads=kt_b + [qn_b[2 * hd + cc]], writes=[pzb])
                    S.op("act", lambda e, pz=pz, mb=mb, Pt=Pt: e.activation(out=Pt[:, mb, :], in_=pz[:, :], func=AF.Exp),
                         reads=[pzb], writes=[Ptb])
                pd, pdb = self.ps[4], self.ps_b[4]
                for mb in range(2):
                    S.op("pe", lambda e, mb=mb, Pt=Pt: e.matmul(pd[:, :], lhsT=self.ones1[:, :], rhs=Pt[:, mb, :],
                                                              start=(mb == 0), stop=(mb == 1)),
                         reads=[cb_, Ptb], writes=[pdb])
                r, rb = rr[pi], rr_b[pi]
                S.op("dve", lambda e, r=r: e.reciprocal(out=r, in_=pd[:, :]), reads=[pdb], writes=[rb])
                for dv in range(2):
                    po, pob = self.ps[5 + dv], self.ps_b[5 + dv]
                    for mb in range(2):
                        S.op("pe", lambda e, po=po, mb=mb, dv=dv, hd=hd, Pt=Pt: e.matmul(
                                po[:, :], lhsT=Vm[:, mb, hd * 256 + dv * 128:hd * 256 + (dv + 1) * 128], rhs=Pt[:, mb, :],
                                start=(mb == 0), stop=(mb == 1)),
                             reads=vm_b + [Ptb], writes=[pob])
                    S.op("dve", lambda e, po=po, dv=dv, hd=hd, sl=sl, r=r: e.tensor_tensor(
                            out=xn[:, 2 * hd + dv, sl], in0=po[:, :], in1=r, op=ALU.mult),
                         reads=[pob, rb], writes=[self.xn_b[2 * hd + dv][t]])
        self.out_proj(wo)

    def build(self):
        self.setup()
        ph = self.phases
        for l in range(2):
            if f"ffn1_{l}" in ph:
                self.rmsnorm_x(f"ffn1_{l}")
                self.ffn(self.w["ffn1_w_gu"], self.w["ffn1_w_down"], l)
            if f"mix_{l}" in ph:
                self.rmsnorm_x(f"mix_{l}")
                self.mixer_proj(l)
                if l == 0:
                    self.banded(0, [(j, j // 2, 0, j, (j // 2) * 64) for j in range(4)], 2, 1,
                                CB_MULTA, CF_NSLA, slopes(8), True)
                    self.stick(0, [(4 + j, 2 + j, 128 + j * 128, 4 + j) for j in range(4)])
                    if "noout" not in ph:
                        self.out_proj(self.w["ev_w_out"].ap()[0])
                else:
                    self.banded(1, [(j, j, j * 128, j, None) for j in range(8)], 17, 16,
                                CB_MULTC, CF_NSLC, slopes(16), False)
                    if "noout" not in ph:
                        self.out_proj(self.w["od_w_out"].ap()[0])
            if f"xa_{l}" in ph:
                self.rmsnorm_x(f"xa_{l}")
                self.xattn(l)
            if f"ffn2_{l}" in ph:
                self.rmsnorm_x(f"ffn2_{l}")
                self.ffn(self.w["ffn2_w_gu"], self.w["ffn2_w_down"], l)
        self.finish()
        self.S.emit(self.nc, self.es)


ALL_PHASES = tuple(f"{p}_{l}" for l in range(2) for p in ("ffn1", "mix", "xa", "ffn2"))


def build_nc(phases=ALL_PHASES, dbg=False):
    nc = bass.Bass("TRN2", target_bir_lowering=False)
    es = ExitStack()
    with es:
        p = Prog(nc, es, phases, dbg)
        p.build()
    nc.used_w = set(p.w.keys())
    return nc


def make_in_maps(inp, x_override=None, used_w=None, ncores=NCORES):
    gains = build_gains(inp)
    cb, cf = build_consts()
    x = np.asarray(inp["x"], np.float32) if x_override is None else x_override
    mem = np.asarray(inp["mem"], np.float32)
    shared = {k: np.ascontiguousarray(np.asarray(inp[k], np.float32)) for k in W_SHAPES
              if used_w is None or k in used_w}
    maps = []
    for core in range(ncores):
        b, h = core // 2, core % 2
        m = dict(shared)
        m["xT"] = np.ascontiguousarray(x[b, h * TOK:(h + 1) * TOK, :].T)
        m["memT"] = np.ascontiguousarray(mem[b].T)
        m["gains"] = gains
        m["cb"] = cb
        m["cf"] = cf
        m["flag"] = np.full((128, 128), float(h), np.float32)
        maps.append(m)
    return maps


def kernel(**inputs):
    nc = build_nc()
    maps = make_in_maps(inputs, used_w=nc.used_w)
    res = run_bass_kernel_spmd(nc, maps, core_ids=list(range(NCORES)))
    out = np.empty((4, 4096, D), np.float32)
    for core in range(NCORES):
        b, h = core // 2, core % 2
        out[b, h * TOK:(h + 1) * TOK, :] = np.asarray(res.results[core]["outT"]).T
    return out
```

```python
from contextlib import ExitStack
import numpy as np
import concourse.bass as bass
import concourse.mybir as mybir
from concourse.bass_utils import run_bass_kernel_spmd

F32 = mybir.dt.float32
BF16 = mybir.dt.bfloat16
AF = mybir.ActivationFunctionType
ALU = mybir.AluOpType

D = 1024
NCH = 8
TOK = 2048
TC = 512
NTC = TOK // TC
DFF = 2816
NFC = DFF // 128
EPS = 1e-6
NCORES = 8
NGROUPS = [4]


class Chan:
    def __init__(self, name):
        self.name = name
        self.sem = None
        self.cnt = 0


class Buf:
    def __init__(self, name, after=()):
        self.name = name
        self.writers = {}
        self.readers = {}
        self.chan = None
        for i, o in enumerate(after):
            self.readers[("init", i)] = o


class Op:
    __slots__ = ("eng", "fn", "deps", "dma", "chan", "val", "flag", "key", "inc")

    def __init__(self, eng, fn, dma=False, chan=None):
        self.eng = eng
        self.fn = fn
        self.deps = []
        self.dma = dma
        self.chan = chan
        self.val = 0
        self.flag = False
        self.key = None
        self.inc = 16


ENGS = ("pe", "act", "dve", "pool", "sp")


class Sched:
    def __init__(self):
        self.q = {e: [] for e in ENGS}
        self.chans = []
        self.nbuf = 0

    def buf(self, name, after=()):
        return Buf(name, after)

    def bufs(self, name, n, after=()):
        return [Buf(f"{name}{i}", after) for i in range(n)]

    def chan(self, name):
        c = Chan(f"{name}_{len(self.chans)}")
        self.chans.append(c)
        return c

    def _add(self, op, reads, writes):
        deps = {}

        def need(o):
            if o is op:
                return
            if (not o.dma) and (not op.dma) and o.eng == "pe" and op.eng == "pe":
                return
            deps[id(o)] = o

        for b in reads:
            for k, o in b.writers.items():
                need(o)
        for b in writes:
            for k, o in b.writers.items():
                if op.dma and o.dma:
                    continue
                need(o)
            for k, o in b.readers.items():
                need(o)
        op.deps = list(deps.values())
        for o in op.deps:
            o.flag = True
        for b in writes:
            if op.dma:
                b.writers = {k: o for k, o in b.writers.items() if o.dma}
                b.writers[op.key] = op
            else:
                b.writers = {op.key: op}
            b.readers = {}
        for b in reads:
            if b not in writes:
                b.readers[op.key] = op
        self.q[op.eng].append(op)
        return op

    def op(self, eng, fn, reads=(), writes=()):
        o = Op(eng, fn)
        o.key = eng
        return self._add(o, reads, writes)

    def dma(self, eng, out, in_, reads=(), writes=(), chan=None):
        if chan is None:
            b = writes[0]
            if b.chan is None:
                b.chan = self.chan("c_" + b.name)
            chan = b.chan
        o = Op(eng, lambda e: e.dma_start(out=out, in_=in_), dma=True, chan=chan)
        chan.cnt += 16
        o.val = chan.cnt
        o.key = ("ch", id(chan))
        return self._add(o, reads, writes)

    def coll(self, fn, reads=(), writes=()):
        ch = self.chan(f"cc{len(self.chans)}")
        o = Op("pool", fn, dma=True, chan=ch)
        o.inc = 1
        ch.cnt += 1
        o.val = ch.cnt
        o.key = ("ch", id(ch))
        return self._add(o, reads, writes)

    def barrier(self):
        res = []
        for e in ENGS:
            for o in reversed(self.q[e]):
                if not o.dma:
                    o.flag = True
                    res.append(o)
                    break
        seen = set()
        for e in ENGS:
            for o in reversed(self.q[e]):
                if o.dma and id(o.chan) not in seen:
                    seen.add(id(o.chan))
                    res.append(o)
        return res

    def emit(self, nc, es):
        for e in ENGS:
            c = 0
            for o in self.q[e]:
                if not o.dma and o.flag:
                    c += 1
                    o.val = c
        esem = {e: es.enter_context(nc.semaphore("sem_" + e)) for e in ENGS}
        for ch in self.chans:
            ch.sem = es.enter_context(nc.semaphore(ch.name))
        block = es.enter_context(nc.Block())

        def body_for(ename):
            def body(e):
                waited = {}
                for o in self.q[ename]:
                    need = {}
                    for d in o.deps:
                        sem = d.chan.sem if d.dma else esem[d.eng]
                        k = id(sem)
                        if waited.get(k, 0) >= d.val:
                            continue
                        if k not in need or need[k][1] < d.val:
                            need[k] = (sem, d.val)
                    for k, (sem, val) in need.items():
                        e.wait_ge(sem, val)
                        waited[k] = val
                    ins = o.fn(e)
                    if ins is None:
                        continue
                    if o.dma:
                        ins.then_inc(o.chan.sem, o.inc)
                    elif o.flag:
                        ins.then_inc(esem[ename], 1)
            return body

        block.tensor(body_for("pe"))
        block.scalar(body_for("act"))
        block.vector(body_for("dve"))
        block.gpsimd(body_for("pool"))
        block.sync(body_for("sp"))


def _col_chunks(v):
    v = np.asarray(v, np.float32)
    return np.ascontiguousarray(v.reshape(-1, 128).T)


def _gain_layout():
    cols = {}
    n = 0
    for l in range(2):
        for nm, w in ((f"ffn1_{l}", 8), (f"mix_{l}", 8), (f"xa_{l}", 8), (f"xam_{l}", 8), (f"ffn2_{l}", 8),
                      (f"xaq_{l}", 2), (f"xak_{l}", 2)):
            cols[nm] = n
            n += w
    for nm, w in (("evq", 1), ("evk", 1), ("odq", 1), ("odk", 1), ("sinks", 8)):
        cols[nm] = n
        n += w
    return cols, n


GCOL, NGCOL = _gain_layout()


def build_gains(inp):
    g = np.zeros((128, NGCOL), np.float32)
    for l in range(2):
        g[:, GCOL[f"ffn1_{l}"]:][:, :8] = _col_chunks(inp["ffn1_norm"][l])
        g[:, GCOL[f"mix_{l}"]:][:, :8] = _col_chunks(inp["mix_norm"][l])
        g[:, GCOL[f"xa_{l}"]:][:, :8] = _col_chunks(inp["xa_norm"][l])
        g[:, GCOL[f"xam_{l}"]:][:, :8] = _col_chunks(inp["xa_mem_norm"][l])
        g[:, GCOL[f"ffn2_{l}"]:][:, :8] = _col_chunks(inp["ffn2_norm"][l])
        g[:, GCOL[f"xaq_{l}"]:][:, :2] = _col_chunks(inp["xa_q_gain"][l])
        g[:, GCOL[f"xak_{l}"]:][:, :2] = _col_chunks(inp["xa_k_gain"][l])
    g[:, GCOL["evq"]] = np.tile(np.asarray(inp["ev_q_gain"][0], np.float32), 2)
    g[:, GCOL["evk"]] = np.tile(np.asarray(inp["ev_k_gain"][0], np.float32), 2)
    g[:, GCOL["odq"]] = np.tile(np.asarray(inp["od_q_gain"][0], np.float32), 2)
    g[:, GCOL["odk"]] = np.tile(np.asarray(inp["od_k_gain"][0], np.float32), 2)
    g[:, GCOL["sinks"]:GCOL["sinks"] + 8] = np.asarray(inp["ev_sinks"][0], np.float32)[None, :]
    return g


NTC_MASK = 23
CB_NEGU, CB_NEGI, CB_TRI7 = 0, 128, 256
CB_MULTC = CB_TRI7 + 7 * 128
CB_MULTA = CB_MULTC + NTC_MASK * 128
CB_N = CB_MULTA + 8 * 128
CF_D0, CF_D0C, CF_NSLC, CF_NSLA, CF_LNS = 0, 128, 256, 256 + 16 * 17, 256 + 16 * 17 + 8 * 2
CF_N = CF_LNS + 4


def slopes(n):
    return [2.0 ** (-8.0 * (i + 1) / n) for i in range(n)]


def build_consts():
    s = np.arange(128)[:, None]
    t = np.arange(128)[None, :]
    cb = np.zeros((128, CB_N), np.float32)
    cb[:, CB_NEGU:CB_NEGU + 128] = -(s >= t).astype(np.float32)
    cb[:, CB_NEGI:CB_NEGI + 128] = -(s == t).astype(np.float32)
    cb[:, CB_TRI7 + 3 * 128:CB_TRI7 + 4 * 128] = (s < t)
    cb[:, CB_TRI7 + 4 * 128:CB_TRI7 + 7 * 128] = 1.0
    for j in range(17):
        d = 128 * j + t - s
        m = ((d >= 0) & (d <= 128)).astype(np.float32) + ((d >= 0) & (d % 4 == 0) & (d <= 512)) \
            + ((d >= 0) & (d % 16 == 0) & (d <= 2048))
        cb[:, CB_MULTC + (3 + j) * 128:CB_MULTC + (4 + j) * 128] = m
    for j in range(2):
        d = 128 * j + t - s
        cb[:, CB_MULTA + (3 + j) * 128:CB_MULTA + (4 + j) * 128] = ((d >= 0) & (d <= 127))
    cf = np.zeros((128, CF_N), np.float32)
    cf[:, CF_D0:CF_D0 + 128] = t - s
    cf[:, CF_D0C:CF_D0C + 128] = np.maximum(t - s, 0)
    sc = slopes(16)
    for h in range(16):
        for j in range(17):
            cf[:, CF_NSLC + h * 17 + j] = -sc[h] * 128.0 * j
    sa = slopes(8)
    for h in range(8):
        for j in range(2):
            cf[:, CF_NSLA + h * 2 + j] = -sa[h] * 128.0 * j
    cf[:, CF_LNS + 0] = 0.0
    cf[:, CF_LNS + 1] = np.log(0.125)
    cf[:, CF_LNS + 2] = np.log(1.0 / 16.0)
    cf[:, CF_LNS + 3] = 1.0
    return cb, cf


W_SHAPES = {
    "ffn1_w_gu": [2, D, 2 * DFF], "ffn1_w_down": [2, DFF, D],
    "ffn2_w_gu": [2, D, 2 * DFF], "ffn2_w_down": [2, DFF, D],
    "ev_w_in": [1, D, 2304], "ev_w_out": [1, D, D], "od_w_in": [1, D, 3072], "od_w_out": [1, D, D],
    "xa_w_q": [2, D, D], "xa_w_kv": [2, D, 2 * D], "xa_w_o": [2, D, D],
}
UNIT = 2048
NUNITS = 14


class Ring:
    def __init__(self, S, tile, n):
        self.tile = tile
        self.n = n
        self.bufs = [S.buf(f"ring{i}") for i in range(n)]
        self.pos = 0

    def alloc(self, k=1):
        if self.pos + k > self.n:
            self.pos = 0
        u = self.pos
        self.pos += k
        return self.tile[:, u * UNIT:(u + k) * UNIT], self.bufs[u:u + k]


class Prog:
    def __init__(self, nc, es, phases, dbg=False):
        self.nc = nc
        self.es = es
        self.S = S = Sched()
        self.phases = phases
        dt = nc.dram_tensor
        self.xT_d = dt("xT", [D, TOK], F32, kind="ExternalInput")
        self.memT_d = dt("memT", [D, 256], F32, kind="ExternalInput")
        self.gains_d = dt("gains", [128, NGCOL], F32, kind="ExternalInput")
        self.cb_d = dt("cb", [128, CB_N], F32, kind="ExternalInput")
        self.cf_d = dt("cf", [128, CF_N], F32, kind="ExternalInput")
        self.flag_d = dt("flag", [128, 128], F32, kind="ExternalInput")
        class _LazyW(dict):
            def __missing__(d, nm):
                d[nm] = dt(nm, W_SHAPES[nm], F32, kind="ExternalInput")
                return d[nm]
        self.w = _LazyW()
        self.dbg = dbg
        self.out_d = dt("outT", [D, TOK], F32, kind="ExternalOutput")
        self.out_buf = S.buf("outT")
        self.scr = {}
        dk = dict(kind="ExternalOutput") if dbg else {}
        if dbg:
            self.xn_d = dt("xn_dump", [D, TOK], BF16, kind="ExternalOutput")
        for l, (nk, vc) in enumerate(((6, 640), (8, 1024))):
            self.scr[l] = dict(
                q=dt(f"q{l}_d", [8 * 128, TOK], BF16, **dk), q_b=S.buf(f"q{l}_d"),
                kloc=[dt(f"kloc{l}_{c}", [128, TOK], BF16) for c in range(nk)],
                kloc_b=[S.buf(f"kloc{l}_{c}") for c in range(nk)],
                kall=[dt(f"kall{l}_{c}", [256, TOK], BF16) for c in range(nk)],
                kall_b=[S.buf(f"kall{l}_{c}") for c in range(nk)],
                vloc=[dt(f"vloc{l}_{j}", [256, vc], BF16) for j in range(8)],
                vloc_b=[S.buf(f"vloc{l}_{j}") for j in range(8)],
                vall=[dt(f"vall{l}_{j}", [512, vc], BF16) for j in range(8)],
                vall_b=[S.buf(f"vall{l}_{j}") for j in range(8)],
                nk=nk, vc=vc)

        sb = lambda name, shape, dtype: es.enter_context(nc.sbuf_tensor(name, shape, dtype))
        ps = lambda name: es.enter_context(nc.psum_tensor(name, [128, 512], F32))
        self.xT = sb("xT_sb", [128, NCH, TOK], F32)
        self.xT_b = [[S.buf(f"xT{c}_{t}") for t in range(NTC)] for c in range(NCH)]
        self.xn = sb("xn_sb", [128, NCH, TOK], BF16)
        self.xn_b = [[S.buf(f"xn{c}_{t}") for t in range(NTC)] for c in range(NCH)]
        self.gains = sb("gains_sb", [128, NGCOL], F32)
        self.cb = sb("cb_sb", [128, CB_N], BF16)
        self.cf = sb("cf_sb", [128, CF_N], F32)
        self.const_b = S.buf("consts")
        self.onesm = sb("onesm", [128, 128], BF16)
        self.ones256 = sb("ones256", [128, 128], BF16)
        self.ones1 = sb("ones1", [128, 128], BF16)
        self.bd64 = sb("bd64", [128, 128], BF16)
        self.flagt = sb("flagt", [128, 128], F32)
        self.flagb = sb("flagb", [128, 128], BF16)
        self.epsc = sb("epsc", [128, 2], F32)
        self.exps = sb("exps", [128, 8], F32)
        self.sq = [sb(f"sq{i}", [128, TC], BF16) for i in range(4)]
        self.sq_b = S.bufs("sq", 4)
        self.rstd = [sb(f"rstd{i}", [128, TC], F32) for i in range(2)]
        self.rstd_b = S.bufs("rstd", 2)
        self.ring_t = sb("ring", [128, NUNITS * UNIT], BF16)
        self.ring = Ring(S, self.ring_t, NUNITS)
        self.sf = sb("scr_f32", [128, 3072], F32)
        self.sbf = sb("scr_bf", [128, 8192], BF16)
        self.ps = [ps(f"ps{i}") for i in range(8)]
        self.ps_b = S.bufs("ps", 8)
        self.norm_i = 0
        self.sq_i = 0

    def gcol(self, name, i=0):
        c = GCOL[name] + i
        return self.gains[:, c:c + 1]

    def setup(self):
        S = self.S
        cb_ = self.const_b
        S.dma("sp", self.gains[:, :], self.gains_d[:, :], writes=[cb_])
        S.dma("sp", self.cf[:, :], self.cf_d[:, :], writes=[cb_])
        S.dma("sp", self.flagt[:, :], self.flag_d[:, :], writes=[cb_])
        for i in range(0, CB_N, 1024):
            j = min(CB_N, i + 1024)
            S.dma("pool", self.cb[:, i:j], self.cb_d[:, i:j], writes=[cb_])
        S.op("pool", lambda e: e.memset(self.onesm[:, :], 1.0 / D), writes=[cb_])
        S.op("pool", lambda e: e.memset(self.ones256[:, :], 1.0 / 256), writes=[cb_])
        S.op("pool", lambda e: e.memset(self.ones1[:, :], 1.0), writes=[cb_])
        S.op("pool", lambda e: e.memset(self.bd64[:, :], 0.0), writes=[cb_])
        S.op("pool", lambda e: e.memset(self.bd64[0:64, 0:64], 1.0 / 64), writes=[cb_])
        S.op("pool", lambda e: e.memset(self.bd64[64:128, 64:128], 1.0 / 64), writes=[cb_])
        S.op("pool", lambda e: e.memset(self.epsc[:, :], EPS), writes=[cb_])
        S.op("pool", lambda e: e.memset(self.sbf[:, :], 0.0), writes=[cb_])
        S.op("dve", lambda e: e.tensor_copy(out=self.flagb[:, :], in_=self.flagt[:, :]), reads=[cb_], writes=[cb_])
        sc = GCOL["sinks"]
        S.op("act", lambda e: e.activation(out=self.exps[:, :], in_=self.gains[:, sc:sc + 8], func=AF.Exp),
             reads=[cb_], writes=[cb_])
        xv = self.xT_d.ap().rearrange("(c p) t -> p c t", p=128)
        for t in range(NTC):
            for c in range(NCH):
                S.dma("sp", self.xT[:, c, t * TC:(t + 1) * TC], xv[:, c, t * TC:(t + 1) * TC],
                      writes=[self.xT_b[c][t]])

    def finish(self):
        S = self.S
        ov = self.out_d.ap().rearrange("(c p) t -> p c t", p=128)
        for t in range(NTC):
            for c in range(NCH):
                S.dma("sp", ov[:, c, t * TC:(t + 1) * TC], self.xT[:, c, t * TC:(t + 1) * TC],
                      reads=[self.xT_b[c][t]], writes=[self.out_buf])
        if self.dbg:
            xv = self.xn_d.ap().rearrange("(c p) t -> p c t", p=128)
            for t in range(NTC):
                for c in range(NCH):
                    S.dma("sp", xv[:, c, t * TC:(t + 1) * TC], self.xn[:, c, t * TC:(t + 1) * TC],
                          reads=[self.xn_b[c][t]], writes=[self.out_buf])
        S.op("sp", lambda e: None, reads=[self.out_buf])

    def rstd_from(self, srcs, ones, n, lnbias, out_rs, out_rsb, reads_extra=()):
        S = self.S
        i = self.norm_i % 2
        self.norm_i += 1
        pss, pssb = self.ps[6 + i], self.ps_b[6 + i]
        k = len(srcs)
        for c, (ap, bl) in enumerate(srcs):
            q = self.sq_i % 4
            self.sq_i += 1
            sq, sqb = self.sq[q], self.sq_b[q]
            S.op("act", lambda e, sq=sq, ap=ap: e.activation(out=sq[:, 0:n], in_=ap, func=AF.Square),
                 reads=list(bl), writes=[sqb])
            S.op("pe", lambda e, sq=sq, pss=pss, c=c: e.matmul(pss[:, 0:n], lhsT=ones[:, :], rhs=sq[:, 0:n],
                                                             start=(c == 0), stop=(c == k - 1)),
                 reads=[self.const_b, sqb], writes=[pssb])
        S.op("act", lambda e: e.activation(out=out_rs, in_=pss[:, 0:n], func=AF.Ln, bias=self.epsc[:, 0:1]),
             reads=[pssb, self.const_b], writes=[out_rsb])
        S.op("act", lambda e: e.activation(out=out_rs, in_=out_rs, func=AF.Exp, scale=-0.5, bias=lnbias),
             reads=[out_rsb, self.const_b], writes=[out_rsb])

    def lnb(self, i):
        return self.cf[:, CF_LNS + i:CF_LNS + i + 1]

    def rmsnorm_x(self, gname):
        S = self.S
        xT, xn = self.xT, self.xn
        for t in range(NTC):
            sl = slice(t * TC, (t + 1) * TC)
            i = self.norm_i % 2
            rs, rsb = self.rstd[i], self.rstd_b[i]
            self.rstd_from([(xT[:, c, sl], [self.xT_b[c][t]]) for c in range(NCH)], self.onesm, TC,
                           self.lnb(0), rs[:, :], rsb)
            for c in range(NCH):
                S.op("dve", lambda e, c=c, rs=rs, sl=sl: e.scalar_tensor_tensor(
                        out=xn[:, c, sl], in0=xT[:, c, sl], scalar=self.gcol(gname, c),
                        in1=rs[:, :], op0=ALU.mult, op1=ALU.mult),
                     reads=[self.xT_b[c][t], rsb, self.const_b], writes=[self.xn_b[c][t]])

    def load_cols(self, w_ap2d, c0, ncols=256):
        ap, bufs = self.ring.alloc(1)
        v = ap.rearrange("p (c f) -> p c f", c=NCH)
        src = w_ap2d.rearrange("(c p) f -> p c f", p=128)
        self.S.dma("pool", v[:, :, 0:ncols], src[:, :, c0:c0 + ncols], writes=bufs)
        return v, bufs

    def load_rows(self, w_ap2d, r0):
        ap, bufs = self.ring.alloc(1)
        v = ap.rearrange("p (j m) -> p j m", j=2)
        src = w_ap2d[r0:r0 + 256, :].rearrange("(j p) m -> p j m", p=128)
        self.S.dma("pool", v[:, :, :], src, writes=bufs)
        return v, bufs

    def ffn(self, wgu_d, wdn_d, layer):
        S = self.S
        after = S.barrier()
        sg = [self.sf[:, i * TC:(i + 1) * TC] for i in range(2)]
        sg_b = S.bufs("sg", 2, after)
        act = [self.sbf[:, i * 2 * TC:(i + 1) * 2 * TC].rearrange("p (j t) -> p j t", j=2) for i in range(3)]
        act_b = [[S.buf(f"act{i}_{j}", after) for j in range(2)] for i in range(3)]
        NG = NFC // 2
        gu2 = wgu_d.ap()[layer]
        dn2 = wdn_d.ap()[layer]
        W = {}

        def load(g):
            f0 = g * 256
            W[g] = (self.load_cols(gu2, f0), self.load_cols(gu2, DFF + f0), self.load_rows(dn2, f0))

        items = [(g, t) for g in range(NG) for t in range(NTC)]
        xn = self.xn

        def emit_gu(i):
            g, t = items[i]
            sl = slice(t * TC, (t + 1) * TC)
            a = i % 3
            for j in range(2):
                k = (2 * i + j) % 2
                pg, pgb = self.ps[k], self.ps_b[k]
                pu, pub = self.ps[2 + k], self.ps_b[2 + k]
                for (pp, ppb, gi) in ((pg, pgb, 0), (pu, pub, 1)):
                    wv, wb = W[g][gi]
                    for c in range(NCH):
                        S.op("pe", lambda e, pp=pp, wv=wv, c=c, j=j, sl=sl: e.matmul(
                                pp[:, :], lhsT=wv[:, c, j * 128:(j + 1) * 128], rhs=xn[:, c, sl],
                                start=(c == 0), stop=(c == NCH - 1)),
                             reads=wb + [self.xn_b[c][t]], writes=[ppb])
                S.op("act", lambda e, k=k, pg=pg: e.activation(out=sg[k], in_=pg[:, :], func=AF.Silu),
                     reads=[pgb], writes=[sg_b[k]])
                S.op("dve", lambda e, k=k, pu=pu, a=a, j=j: e.tensor_tensor(
                        out=act[a][:, j, :], in0=pu[:, :], in1=sg[k], op=ALU.mult),
                     reads=[pub, sg_b[k]], writes=[act_b[a][j]])

        def emit_down(i):
            g, t = items[i]
            sl = slice(t * TC, (t + 1) * TC)
            a = i % 3
            wv, wb = W[g][2]
            for m in range(NCH):
                k = 4 + (m % 2)
                py, pyb = self.ps[k], self.ps_b[k]
                for j in range(2):
                    S.op("pe", lambda e, py=py, j=j, m=m, wv=wv, a=a: e.matmul(
                            py[:, :], lhsT=wv[:, j, m * 128:(m + 1) * 128], rhs=act[a][:, j, :],
                            start=(j == 0), stop=(j == 1)),
                         reads=wb + [act_b[a][j]], writes=[pyb])
                S.op("dve", lambda e, py=py, m=m, sl=sl: e.scalar_tensor_tensor(
                        out=self.xT[:, m, sl], in0=py[:, :], scalar=0.5, in1=self.xT[:, m, sl],
                        op0=ALU.mult, op1=ALU.add),
                     reads=[pyb, self.xT_b[m][t]], writes=[self.xT_b[m][t]])

        PF = 3
        for g in range(min(PF, NG)):
            load(g)
        n = len(items)
        for i in range(n + 1):
            if i < n:
                g, t = items[i]
                if t == 1 and g + PF < NG:
                    load(g + PF)
                emit_gu(i)
            if i >= 1:
                emit_down(i - 1)

    def out_proj(self, w2d):
        S = self.S
        units = [self.load_rows(w2d, r * 256) for r in range(4)]
        for t in range(NTC):
            sl = slice(t * TC, (t + 1) * TC)
            for m in range(NCH):
                k = 4 + (m % 2)
                py, pyb = self.ps[k], self.ps_b[k]
                for oc in range(NCH):
                    wv, wb = units[oc // 2]
                    S.op("pe", lambda e, py=py, wv=wv, oc=oc, m=m, sl=sl: e.matmul(
                            py[:, :], lhsT=wv[:, oc % 2, m * 128:(m + 1) * 128], rhs=self.xn[:, oc, sl],
                            start=(oc == 0), stop=(oc == NCH - 1)),
                         reads=wb + [self.xn_b[oc][t]], writes=[pyb])
                S.op("dve", lambda e, py=py, m=m, sl=sl: e.tensor_tensor(
                        out=self.xT[:, m, sl], in0=py[:, :], in1=self.xT[:, m, sl], op=ALU.add),
                     reads=[pyb, self.xT_b[m][t]], writes=[self.xT_b[m][t]])

    def mixer_proj(self, layer):
        S = self.S
        after = S.barrier()
        sc = self.scr[layer]
        even = layer == 0
        w2d = (self.w["ev_w_in"] if even else self.w["od_w_in"]).ap()[0]
        nun = 9 if even else 12
        stg = [self.sbf[:, i * TC:(i + 1) * TC] for i in range(4)]
        stg_b = S.bufs("stg", 4, after)
        stg_ch = [S.chan(f"stgch{layer}_{i}") for i in range(4)]
        sqh = [self.sbf[:, (4 + i) * TC:(5 + i) * TC] for i in range(2)]
        sqh_b = S.bufs("sqh", 2, after)
        rr = [self.sf[:, i * TC:(i + 1) * TC] for i in range(4)]
        rr_b = S.bufs("rr", 4, after)
        vst = [self.sbf[:, 3072 + i * 1024:3072 + (i + 1) * 1024] for i in range(2)]
        vst_b = S.bufs("vst", 2, after)
        vst_ch = [S.chan(f"vstch{layer}_{i}") for i in range(2)]
        cnt = {"s": 0, "q": 0, "v": 0}
        gq, gk = ("evq", "evk") if even else ("odq", "odk")
        jobs = []
        if even:
            for j in range(4):
                jobs.append((j // 2, (j % 2) * 128, "norm", gq, 1, sc["q"], sc["q_b"], j))
            for g in range(2):
                jobs.append((2, ("dup", g * 64), "norm", gk, 0, sc["kloc"], sc["kloc_b"], g))
            for j in range(4):
                jobs.append((3 + j // 2, (j % 2) * 128, "scale", None, 0.125, sc["q"], sc["q_b"], 4 + j))
            for j in range(4):
                jobs.append((5 + j // 2, (j % 2) * 128, "scale", None, 1.0, sc["kloc"], sc["kloc_b"], 2 + j))
            vjobs = [[(2, 128, 128)], [(7, 0, 256), (8, 0, 256)]]
        else:
            for j in range(8):
                jobs.append((j // 2, (j % 2) * 128, "norm", gq, 1, sc["q"], sc["q_b"], j))
            for j in range(8):
                jobs.append((4 + j // 2, (j % 2) * 128, "norm", gk, 0, sc["kloc"], sc["kloc_b"], j))
            vjobs = [[(8, 0, 256), (9, 0, 256)], [(10, 0, 256), (11, 0, 256)]]
        U = {}

        def need(u):
            if u not in U:
                U[u] = self.load_cols(w2d, u * 256)
            return U[u]

        for u in range(nun):
            need(u)
        xn = self.xn
        for (u, off, kind, gname, par, dten, dbuf, drow) in jobs:
            wv, wb = need(u)
            for t in range(NTC):
                sl = slice(t * TC, (t + 1) * TC)
                k = cnt["q"] % 4
                cnt["q"] += 1
                pq, pqb = self.ps[k], self.ps_b[k]
                if isinstance(off, tuple):
                    c0 = off[1]
                    for half in range(2):
                        for c in range(NCH):
                            S.op("pe", lambda e, pq=pq, wv=wv, c=c, c0=c0, half=half, sl=sl: e.matmul(
                                    pq[half * 64:(half + 1) * 64, :], lhsT=wv[:, c, c0:c0 + 64], rhs=xn[:, c, sl],
                                    start=(c == 0), stop=(c == NCH - 1)),
                                 reads=wb + [self.xn_b[c][t]], writes=[pqb])
                else:
                    for c in range(NCH):
                        S.op("pe", lambda e, pq=pq, wv=wv, c=c, off=off, sl=sl: e.matmul(
                                pq[:, :], lhsT=wv[:, c, off:off + 128], rhs=xn[:, c, sl],
                                start=(c == 0), stop=(c == NCH - 1)),
                             reads=wb + [self.xn_b[c][t]], writes=[pqb])
                si = cnt["s"] % 4
                cnt["s"] += 1
                if kind == "norm":
                    r, rb = rr[k], rr_b[k]
                    self.rstd_from([(pq[:, :], [pqb])], self.bd64, TC, self.lnb(par), r, rb)
                    S.op("dve", lambda e, si=si, pq=pq, r=r, gname=gname: e.scalar_tensor_tensor(
                            out=stg[si], in0=pq[:, :], scalar=self.gcol(gname), in1=r, op0=ALU.mult, op1=ALU.mult),
                         reads=[pqb, rb, self.const_b], writes=[stg_b[si]])
                else:
                    S.op("act", lambda e, si=si, pq=pq, par=par: e.activation(
                            out=stg[si], in_=pq[:, :], func=AF.Copy, scale=float(par)),
                         reads=[pqb], writes=[stg_b[si]])
                if isinstance(dten, list):
                    S.dma("sp", dten[drow][:, sl], stg[si], reads=[stg_b[si]], writes=[dbuf[drow]], chan=stg_ch[si])
                    if t == NTC - 1:
                        self.gather(sc["kloc"][drow], sc["kloc_b"][drow], sc["kall"][drow], sc["kall_b"][drow])
                else:
                    S.dma("sp", dten[drow * 128:(drow + 1) * 128, sl], stg[si], reads=[stg_b[si]], writes=[dbuf],
                          chan=stg_ch[si])
        vc = sc["vc"]
        for nb in range(TOK // 128):
            tb, tsl = nb // 4, slice(nb * 128, (nb + 1) * 128)
            vi = cnt["v"] % 2
            cnt["v"] += 1
            col = 0
            for bi, grp in enumerate(vjobs):
                pv, pvb = self.ps[2 + 2 * vi + bi], self.ps_b[2 + 2 * vi + bi]
                pc = 0
                for (u, c0, n) in grp:
                    wv, wb = need(u)
                    for c in range(NCH):
                        S.op("pe", lambda e, pv=pv, wv=wv, c=c, c0=c0, n=n, pc=pc, tsl=tsl: e.matmul(
                                pv[:, pc:pc + n], lhsT=xn[:, c, tsl], rhs=wv[:, c, c0:c0 + n],
                                start=(c == 0), stop=(c == NCH - 1)),
                             reads=wb + [self.xn_b[c][tb]], writes=[pvb])
                    pc += n
                eng = "act" if bi == 0 else "dve"
                if eng == "act":
                    S.op("act", lambda e, pv=pv, vi=vi, col=col, pc=pc: e.activation(
                            out=vst[vi][:, col:col + pc], in_=pv[:, 0:pc], func=AF.Copy),
                         reads=[pvb], writes=[vst_b[vi]])
                else:
                    S.op("dve", lambda e, pv=pv, vi=vi, col=col, pc=pc: e.tensor_copy(
                            out=vst[vi][:, col:col + pc], in_=pv[:, 0:pc]),
                         reads=[pvb], writes=[vst_b[vi]])
                col += pc
            jv = nb // 2
            S.dma("sp", sc["vloc"][jv][(nb % 2) * 128:(nb % 2 + 1) * 128, :], vst[vi][:, 0:vc], reads=[vst_b[vi]],
                  writes=[sc["vloc_b"][jv]], chan=vst_ch[vi])
            if nb % 2 == 1:
                self.gather(sc["vloc"][jv], sc["vloc_b"][jv], sc["vall"][jv], sc["vall_b"][jv])

    def gather(self, src, src_b, dst, dst_b):
        S = self.S
        import os
        if os.environ.get("KDBG_NOCOLL"):
            n = src.shape[0]
            S.dma("sp", dst[0:n, :], src[:, :], reads=[src_b], writes=[dst_b])
            return
        groups = [[2 * i, 2 * i + 1] for i in range(NGROUPS[0])]
        S.coll(lambda e: e.collective_compute("AllGather", ALU.bypass, replica_groups=groups,
                                              ins=[src.ap().opt()], outs=[dst.ap().opt()]),
               reads=[src_b], writes=[dst_b])

    def load_attn_tiles(self, layer, qrow, krow, vcol, nprev):
        S = self.S
        sc = self.scr[layer]
        nk = sc["nk"]
        qh = []
        for hh in range(2):
            qa, qb_ = self.ring.alloc(1)
            lo, hi = hh * 64, (hh + 1) * 64
            zl, zh = (1 - hh) * 64, (2 - hh) * 64
            S.dma("sp", qa[lo:hi, :], sc["q"][qrow * 128 + lo:qrow * 128 + hi, :], reads=[sc["q_b"]], writes=qb_)
            S.op("pool", lambda e, qa=qa, zl=zl, zh=zh: e.memset(qa[zl:zh, :], 0.0), writes=qb_)
            qh.append((qa, qb_))
        ka, kb_ = self.ring.alloc(2)
        npc = nprev * 128
        S.dma("sp", ka[:, 0:npc], sc["kall"][krow][0:128, TOK - npc:TOK], reads=[sc["kall_b"][krow]], writes=kb_)
        S.dma("sp", ka[:, npc:npc + TOK], sc["kloc"][krow][:, :], reads=[sc["kloc_b"][krow]], writes=kb_)
        va, vb_ = self.ring.alloc(2)
        v3 = va.rearrange("p (n c) -> p n c", c=128)
        if nprev == 16:
            for j in range(8):
                S.dma("sp", v3[:, 2 * j:2 * j + 2, :],
                      sc["vall"][j][0:256, vcol:vcol + 128].rearrange("(n p) c -> p n c", p=128),
                      reads=[sc["vall_b"][j]], writes=vb_)
        else:
            assert nprev == 1
            S.dma("sp", v3[:, 0:1, :], sc["vall"][7][128:256, vcol:vcol + 128].rearrange("(n p) c -> p n c", p=128),
                  reads=[sc["vall_b"][7]], writes=vb_)
        for j in range(8):
            S.dma("sp", v3[:, nprev + 2 * j:nprev + 2 * j + 2, :],
                  sc["vloc"][j][:, vcol:vcol + 128].rearrange("(n p) c -> p n c", p=128),
                  reads=[sc["vloc_b"][j]], writes=vb_)
        S.op("dve", lambda e: e.tensor_scalar(out=va[:, 0:npc], in0=va[:, 0:npc], scalar1=self.flagt[:, 0:1],
                                              scalar2=1.0, op0=ALU.mult, op1=ALU.mult),
             reads=vb_ + [self.const_b], writes=vb_)
        return qh, (ka, kb_), (v3, vb_)

    def banded(self, layer, pairs, ND, nprev, mult_off, nsl_off, sl_list, sink):
        S = self.S
        after = S.barrier()
        NT = ND + 6
        msk = [self.sbf[:, i * 3072:i * 3072 + NT * 128] for i in range(2)]
        msk_b = S.bufs("msk", 2, after)
        P = [self.sbf[:, 6144 + i * TC:6144 + (i + 1) * TC] for i in range(4)]
        P_b = S.bufs("P", 4, after)
        tmp = [self.sf[:, i * TC:(i + 1) * TC] for i in range(2)]
        tmp_b = S.bufs("tmp", 2, after)
        cb_ = self.const_b
        hcount = 0
        ucount = 0
        ccount = 0
        for pi, (qrow, krow, vcol, oc, vh) in enumerate(pairs):
            qh, (ka, kb_), (v3, vb_) = self.load_attn_tiles(layer, qrow, krow, vcol, nprev)
            for hh in range(2):
                qa, qb_ = qh[hh]
                h = 2 * pi + hh
                hp = slice(hh * 64, (hh + 1) * 64)
                vs = hp if vh is None else slice(vh, vh + 64)
                mi = hcount % 2
                hcount += 1
                m, mb = msk[mi], msk_b[mi]
                S.op("pool", lambda e, m=m: e.memset(m[:, 0:384], 0.0), writes=[mb])
                S.op("pool", lambda e, m=m: e.memset(m[:, (ND + 3) * 128:(ND + 6) * 128], 0.0), writes=[mb])
                for j in range(ND):
                    src = CF_D0C if j == 0 else CF_D0
                    S.op("act", lambda e, m=m, j=j, src=src, h=h: e.activation(
                            out=m[:, (3 + j) * 128:(4 + j) * 128], in_=self.cf[:, src:src + 128], func=AF.Exp,
                            scale=float(-sl_list[h]), bias=self.cf[:, nsl_off + h * ND + j:nsl_off + h * ND + j + 1]),
                         reads=[cb_], writes=[mb])
                S.op("dve", lambda e, m=m: e.tensor_tensor(out=m, in0=m, in1=self.cb[:, mult_off:mult_off + NT * 128],
                                                          op=ALU.mult),
                     reads=[cb_, mb], writes=[mb])
                for t in range(NTC):
                    ucount = self._banded_chunk(t, ccount, ucount, nprev, ND, hp, vs, h, oc, sink, m, mb, P, P_b, tmp, tmp_b,
                                                qa, qb_, ka, kb_, v3, vb_)
                    ccount += 1

    def _banded_chunk(self, t, ccount, ucount, nprev, ND, hp, vs, h, oc, sink, m, mb, P, P_b, tmp, tmp_b,
                      qa, qb_, ka, kb_, v3, vb_):
        S = self.S
        cb_ = self.const_b
        full_v = (vs == hp)
        ci = ccount % 2
        po, pob = self.ps[2 + ci], self.ps_b[2 + ci]
        pd, pdb = self.ps[4 + ci], self.ps_b[4 + ci]
        Lq0 = nprev + 4 * t
        units = [(L, Lq0 - L) for L in range(max(0, Lq0 - (ND - 1)), Lq0 + 4)]
        n = len(units)
        qsl = slice(t * TC, (t + 1) * TC)

        ZB = (0, 1, 6, 7)

        def cr(j0):
            lo = max(0, -j0)
            hi = min(3, ND - 1 - j0)
            return lo * 128, (hi + 1) * 128

        def st_z(i):
            L, j0 = units[i]
            c0, c1 = cr(j0)
            k = ZB[(ucount + i) % 4]
            S.op("pe", lambda e, k=k, L=L, c0=c0, c1=c1: e.matmul(
                    self.ps[k][:, c0:c1], lhsT=ka[:, L * 128:(L + 1) * 128],
                    rhs=qa[:, t * TC + c0:t * TC + c1], start=True, stop=True),
                 reads=kb_ + qb_, writes=[self.ps_b[k]])

        def st_p(i):
            L, j0 = units[i]
            c0, c1 = cr(j0)
            k = ZB[(ucount + i) % 4]
            p = (ucount + i) % 4
            S.op("act", lambda e, k=k, p=p, c0=c0, c1=c1: e.activation(out=P[p][:, c0:c1], in_=self.ps[k][:, c0:c1], func=AF.Exp),
                 reads=[self.ps_b[k]], writes=[P_b[p]])
            S.op("dve", lambda e, p=p, j0=j0, c0=c0, c1=c1: e.tensor_tensor(
                    out=P[p][:, c0:c1], in0=P[p][:, c0:c1], in1=m[:, (j0 + 3) * 128 + c0:(j0 + 3) * 128 + c1], op=ALU.mult),
                 reads=[P_b[p], mb], writes=[P_b[p]])

        def st_o(i):
            L, j0 = units[i]
            c0, c1 = cr(j0)
            p = (ucount + i) % 4
            if full_v:
                S.op("pe", lambda e, p=p, L=L, i=i, c0=c0, c1=c1: e.matmul(po[:, c0:c1], lhsT=v3[:, L, :], rhs=P[p][:, c0:c1],
                                                             start=(i == 0), stop=(i == n - 1)),
                     reads=vb_ + [P_b[p]], writes=[pob])
            else:
                S.op("pe", lambda e, p=p, L=L, i=i, c0=c0, c1=c1: e.matmul(po[hp, c0:c1], lhsT=v3[:, L, vs], rhs=P[p][:, c0:c1],
                                                             start=(i == 0), stop=(i == n - 1)),
                     reads=vb_ + [P_b[p]], writes=[pob])
            ones = self.flagb[:, :] if L < nprev else self.ones1[:, :]
            S.op("pe", lambda e, p=p, ones=ones, i=i, c0=c0, c1=c1: e.matmul(pd[:, c0:c1], lhsT=ones, rhs=P[p][:, c0:c1],
                                                               start=(i == 0), stop=(i == n - 1)),
                 reads=[cb_, P_b[p]], writes=[pdb])

        for step in range(n + 3):
            if step < n:
                st_z(step)
            if 0 <= step - 1 < n:
                st_p(step - 1)
            if 0 <= step - 3 < n:
                st_o(step - 3)
        ti = ccount % 2
        tm, tmb = tmp[ti], tmp_b[ti]
        if sink:
            S.op("act", lambda e: e.activation(out=tm[hp, :], in_=pd[hp, :], func=AF.Ln, bias=self.exps[hp, h:h + 1]),
                 reads=[pdb, cb_], writes=[tmb])
        else:
            S.op("act", lambda e: e.activation(out=tm[hp, :], in_=pd[hp, :], func=AF.Ln),
                 reads=[pdb], writes=[tmb])
        S.op("act", lambda e: e.activation(out=tm[hp, :], in_=tm[hp, :], func=AF.Exp, scale=-1.0),
             reads=[tmb], writes=[tmb])
        S.op("dve", lambda e: e.tensor_tensor(out=self.xn[hp, oc, qsl], in0=po[hp, :], in1=tm[hp, :], op=ALU.mult),
             reads=[pob, tmb], writes=[self.xn_b[oc][t]])
        return ucount + n

    def stick(self, layer, pairs):
        S = self.S
        after = S.barrier()
        nprev = 16
        ef = [self.sf[:, i * TC:(i + 1) * TC] for i in range(2)]
        ef_b = S.bufs("ef", 2, after)
        sp = [self.sbf[:, i * TC:(i + 1) * TC] for i in range(2)]
        sp_b = S.bufs("sp", 2, after)
        A = [self.sbf[:, (2 + i) * TC:(3 + i) * TC] for i in range(2)]
        A_b = S.bufs("A", 2, after)
        Rs = [self.sbf[:, (4 + i) * TC:(5 + i) * TC] for i in range(2)]
        Rs_b = S.bufs("Rs", 2, after)
        cb_ = self.const_b
        negU = self.cb[:, CB_NEGU:CB_NEGU + 128]
        negI = self.cb[:, CB_NEGI:CB_NEGI + 128]
        one_c = self.cf[:, CF_LNS + 3:CF_LNS + 4]
        ccount = 0
        ucount = 0
        for pi, (qrow, krow, vcol, oc) in enumerate(pairs):
            qh, (ka, kb_), (v3, vb_) = self.load_attn_tiles(layer, qrow, krow, vcol, nprev)
            for hh in range(2):
                qa, qb_ = qh[hh]
                hp = slice(hh * 64, (hh + 1) * 64)
                for t in range(NTC):
                    ucount = self._stick_chunk(t, ccount, ucount, nprev, hp, oc, ef, ef_b, sp, sp_b, A, A_b, Rs, Rs_b,
                                               qa, qb_, ka, kb_, v3, vb_)
                    ccount += 1

    def _stick_chunk(self, t, ccount, ucount, nprev, hp, oc, ef, ef_b, sp, sp_b, A, A_b, Rs, Rs_b,
                     qa, qb_, ka, kb_, v3, vb_):
        S = self.S
        cb_ = self.const_b
        negU = self.cb[:, CB_NEGU:CB_NEGU + 128]
        negI = self.cb[:, CB_NEGI:CB_NEGI + 128]
        one_c = self.cf[:, CF_LNS + 3:CF_LNS + 4]
        ci = ccount % 2
        pR, pRb = self.ps[4 + ci], self.ps_b[4 + ci]
        pO, pOb = self.ps[6 + ci], self.ps_b[6 + ci]
        qsl = slice(t * TC, (t + 1) * TC)
        Ldiag0 = nprev + 4 * t
        Ls = list(range(Ldiag0 + 3, -1, -1))
        n = len(Ls)

        def c0of(i):
            L = Ls[i]
            return (L - Ldiag0) * 128 if L >= Ldiag0 else 0

        tri = self.cb[:, CB_TRI7 + 3 * 128:CB_TRI7 + 4 * 128]

        def s0(i):
            L = Ls[i]
            c0 = c0of(i)
            k = (ucount + i) % 2
            S.op("pe", lambda e, k=k, L=L, c0=c0: e.matmul(self.ps[k][:, c0:], lhsT=ka[:, L * 128:(L + 1) * 128],
                                                          rhs=qa[:, t * TC + c0:(t + 1) * TC], start=True, stop=True),
                 reads=kb_ + qb_, writes=[self.ps_b[k]])

        def s1(i):
            L = Ls[i]
            c0 = c0of(i)
            k = (ucount + i) % 2
            S.op("act", lambda e, k=k, c0=c0: e.activation(out=ef[k][:, c0:], in_=self.ps[k][:, c0:], func=AF.Exp),
                 reads=[self.ps_b[k]], writes=[ef_b[k]])
            S.op("act", lambda e, k=k, c0=c0: e.activation(out=sp[k][:, c0:], in_=ef[k][:, c0:], func=AF.Ln, bias=one_c),
                 reads=[ef_b[k], cb_], writes=[sp_b[k]])
            if L >= Ldiag0:
                S.op("dve", lambda e, k=k, c0=c0: e.tensor_tensor(
                        out=sp[k][:, c0:c0 + 128], in0=sp[k][:, c0:c0 + 128], in1=tri, op=ALU.mult),
                     reads=[sp_b[k], cb_], writes=[sp_b[k]])

        def s2(i):
            L = Ls[i]
            c0 = c0of(i)
            k = (ucount + i) % 2
            pC, pCb = self.ps[2 + k], self.ps_b[2 + k]
            S.op("pe", lambda e, k=k, pC=pC, c0=c0: e.matmul(pC[:, c0:], lhsT=negU, rhs=sp[k][:, c0:], start=True, stop=False),
                 reads=[cb_, sp_b[k]], writes=[pCb])
            last = (i == 0)
            S.op("pe", lambda e, pC=pC, L=L, last=last, c0=c0: e.matmul(
                    pC[:, c0:], lhsT=ka[:, L * 128:(L + 1) * 128], rhs=qa[:, t * TC + c0:(t + 1) * TC], start=False, stop=last),
                 reads=kb_ + qb_, writes=[pCb])
            if i > 0:
                kp = (ucount + i - 1) % 2
                cp = c0of(i - 1)
                S.op("pe", lambda e, pC=pC, kp=kp, cp=cp: e.matmul(pC[:, cp:], lhsT=negI, rhs=Rs[kp][:, cp:], start=False, stop=True),
                     reads=[cb_, Rs_b[kp]], writes=[pCb])
            S.op("pe", lambda e, k=k, i=i, c0=c0: e.matmul(pR[:, c0:], lhsT=self.ones1[:, :], rhs=sp[k][:, c0:],
                                                          start=(i == 0), stop=(i == n - 1)),
                 reads=[cb_, sp_b[k]], writes=[pRb])
            if i < n - 1:
                S.op("dve", lambda e, k=k, c0=c0: e.tensor_copy(out=Rs[k][:, c0:], in_=pR[:, c0:]),
                     reads=[pRb], writes=[Rs_b[k]])

        def s3(i):
            L = Ls[i]
            c0 = c0of(i)
            k = (ucount + i) % 2
            pC, pCb = self.ps[2 + k], self.ps_b[2 + k]
            S.op("act", lambda e, k=k, pC=pC, c0=c0: e.activation(out=A[k][:, c0:], in_=pC[:, c0:], func=AF.Exp),
                 reads=[pCb], writes=[A_b[k]])
            if L >= Ldiag0:
                S.op("dve", lambda e, k=k, c0=c0: e.tensor_tensor(
                        out=A[k][:, c0:c0 + 128], in0=A[k][:, c0:c0 + 128], in1=tri, op=ALU.mult),
                     reads=[A_b[k], cb_], writes=[A_b[k]])

        def s4(i):
            L = Ls[i]
            c0 = c0of(i)
            k = (ucount + i) % 2
            S.op("pe", lambda e, k=k, L=L, i=i, c0=c0: e.matmul(pO[:, c0:], lhsT=v3[:, L, :], rhs=A[k][:, c0:],
                                                         start=(i == 0), stop=(i == n - 1)),
                 reads=vb_ + [A_b[k]], writes=[pOb])

        for step in range(n + 3):
            if step < n:
                s0(step)
            if 0 <= step - 1 < n:
                s1(step - 1)
            if 0 <= step - 2 < n:
                s2(step - 2)
                s3(step - 2)
            if 0 <= step - 3 < n:
                s4(step - 3)
        S.op("dve", lambda e: e.tensor_copy(out=self.xn[hp, oc, qsl], in_=pO[hp, :]),
             reads=[pOb], writes=[self.xn_b[oc][t]])
        return ucount + n

    def xattn(self, layer):
        S = self.S
        after = S.barrier()
        cb_ = self.const_b
        memT = self.sf[:, 0:2048].rearrange("p (c m) -> p c m", c=NCH)
        memT_b = S.buf("memT", after)
        rr = [self.sf[:, 2048 + i * TC:2048 + (i + 1) * TC] for i in range(2)]
        rr_b = S.bufs("xrr", 2, after)
        qn = self.sbf[:, 0:4096].rearrange("p (c t) -> p c t", c=NCH)
        qn_b = S.bufs("qn", NCH, after)
        Pm = [self.sbf[:, 4096 + i * 1024:4096 + (i + 1) * 1024].rearrange("p (m t) -> p m t", m=2) for i in range(2)]
        Pm_b = S.bufs("Pm", 2, after)
        rd = [self.sbf[:, 6144 + i * TC:6144 + (i + 1) * TC] for i in range(2)]
        S.dma("sp", memT, self.memT_d.ap().rearrange("(c p) m -> p c m", p=128), writes=[memT_b])
        mn_ap, mn_b = self.ring.alloc(1)
        memn = mn_ap.rearrange("p (c m) -> p c m", c=NCH)
        self.rstd_from([(memT[:, c, :], [memT_b]) for c in range(NCH)], self.onesm, 256, self.lnb(0), rr[0][:, 0:256], rr_b[0])
        for c in range(NCH):
            S.op("dve", lambda e, c=c: e.scalar_tensor_tensor(
                    out=memn[:, c, :], in0=memT[:, c, :], scalar=self.gcol(f"xam_{layer}", c), in1=rr[0][:, 0:256],
                    op0=ALU.mult, op1=ALU.mult),
                 reads=[memT_b, rr_b[0], cb_], writes=mn_b)
        wkv = self.w["xa_w_kv"].ap()[layer]
        kt_ap, kt_b = self.ring.alloc(1)
        KT = kt_ap.rearrange("p (c m) -> p c m", c=NCH)
        vm_ap, vm_b = self.ring.alloc(1)
        Vm = vm_ap.rearrange("p (m c) -> p m c", m=2)
        for hd in range(4):
            wv, wb = self.load_cols(wkv, hd * 256)
            pk = [self.ps[0], self.ps[1]]
            pkb = [self.ps_b[0], self.ps_b[1]]
            for cc in range(2):
                for c in range(NCH):
                    S.op("pe", lambda e, cc=cc, c=c, wv=wv: e.matmul(pk[cc][:, 0:256], lhsT=wv[:, c, cc * 128:(cc + 1) * 128],
                                                                     rhs=memn[:, c, :], start=(c == 0), stop=(c == NCH - 1)),
                         reads=wb + mn_b, writes=[pkb[cc]])
            r, rb = rr[1][:, 0:256], rr_b[1]
            self.rstd_from([(pk[cc][:, 0:256], [pkb[cc]]) for cc in range(2)], self.ones256, 256, self.lnb(0), r, rb)
            for cc in range(2):
                S.op("dve", lambda e, cc=cc, hd=hd, r=r: e.scalar_tensor_tensor(
                        out=KT[:, 2 * hd + cc, :], in0=pk[cc][:, 0:256], scalar=self.gcol(f"xak_{layer}", cc), in1=r,
                        op0=ALU.mult, op1=ALU.mult),
                     reads=[pkb[cc], rb, cb_], writes=kt_b)
        for g in range(4):
            wv, wb = self.load_cols(wkv, D + g * 256)
            for mb in range(2):
                pv, pvb = self.ps[2 + mb], self.ps_b[2 + mb]
                for c in range(NCH):
                    S.op("pe", lambda e, pv=pv, c=c, mb=mb, wv=wv: e.matmul(
                            pv[:, 0:256], lhsT=memn[:, c, mb * 128:(mb + 1) * 128], rhs=wv[:, c, :],
                            start=(c == 0), stop=(c == NCH - 1)),
                         reads=wb + mn_b, writes=[pvb])
                S.op("act", lambda e, pv=pv, mb=mb, g=g: e.activation(out=Vm[:, mb, g * 256:(g + 1) * 256], in_=pv[:, 0:256],
                                                                      func=AF.Copy),
                     reads=[pvb], writes=vm_b)
        wq = self.w["xa_w_q"].ap()[layer]
        wo = self.w["xa_w_o"].ap()[layer]
        WQ = [self.load_cols(wq, g * 256) for g in range(4)]
        xn = self.xn
        for t in range(NTC):
            sl = slice(t * TC, (t + 1) * TC)
            for hd in range(4):
                wv, wb = WQ[hd]
                pq = [self.ps[0], self.ps[1]]
                pqb = [self.ps_b[0], self.ps_b[1]]
                for cc in range(2):
                    for c in range(NCH):
                        S.op("pe", lambda e, cc=cc, c=c, wv=wv, sl=sl: e.matmul(
                                pq[cc][:, :], lhsT=wv[:, c, cc * 128:(cc + 1) * 128], rhs=xn[:, c, sl],
                                start=(c == 0), stop=(c == NCH - 1)),
                             reads=wb + [self.xn_b[c][t]], writes=[pqb[cc]])
                ri = (t * 4 + hd) % 2
                r, rb = rr[ri], rr_b[ri]
                self.rstd_from([(pq[cc][:, :], [pqb[cc]]) for cc in range(2)], self.ones256, TC, self.lnb(2), r, rb)
                for cc in range(2):
                    S.op("dve", lambda e, cc=cc, hd=hd, r=r: e.scalar_tensor_tensor(
                            out=qn[:, 2 * hd + cc, :], in0=pq[cc][:, :], scalar=self.gcol(f"xaq_{layer}", cc), in1=r,
                            op0=ALU.mult, op1=ALU.mult),
                         reads=[pqb[cc], rb, cb_], writes=[qn_b[2 * hd + cc]])
            for hd in range(4):
                pi = (t * 4 + hd) % 2
                Pt, Ptb = Pm[pi], Pm_b[pi]
                for mb in range(2):
                    pz, pzb = self.ps[2 + mb], self.ps_b[2 + mb]
                    for cc in range(2):
                        S.op("pe", lambda e, pz=pz, mb=mb, cc=cc, hd=hd: e.matmul(
                                pz[:, :], lhsT=KT[:, 2 * hd + cc, mb * 128:(mb + 1) * 128], rhs=qn[:, 2 * hd + cc, :],
                                start=(cc == 0), stop=(cc == 1)),
                             reads=kt_b + [qn_b[2 * hd + cc]], writes=[pzb])
                    S.op("act", lambda e, pz=pz, mb=mb, Pt=Pt: e.activation(out=Pt[:, mb, :], in_=pz[:, :], func=AF.Exp),
                         reads=[pzb], writes=[Ptb])
                pd, pdb = self.ps[4], self.ps_b[4]
                for mb in range(2):
                    S.op("pe", lambda e, mb=mb, Pt=Pt: e.matmul(pd[:, :], lhsT=self.ones1[:, :], rhs=Pt[:, mb, :],
                                                              start=(mb == 0), stop=(mb == 1)),
                         reads=[cb_, Ptb], writes=[pdb])
                r, rb = rr[pi], rr_b[pi]
                S.op("act", lambda e, r=r: e.activation(out=r, in_=pd[:, :], func=AF.Ln), reads=[pdb], writes=[rb])
                S.op("act", lambda e, r=r: e.activation(out=r, in_=r, func=AF.Exp, scale=-1.0), reads=[rb], writes=[rb])
                for dv in range(2):
                    po, pob = self.ps[5 + dv], self.ps_b[5 + dv]
                    for mb in range(2):
                        S.op("pe", lambda e, po=po, mb=mb, dv=dv, hd=hd, Pt=Pt: e.matmul(
                                po[:, :], lhsT=Vm[:, mb, hd * 256 + dv * 128:hd * 256 + (dv + 1) * 128], rhs=Pt[:, mb, :],
                                start=(mb == 0), stop=(mb == 1)),
                             reads=vm_b + [Ptb], writes=[pob])
                    S.op("dve", lambda e, po=po, dv=dv, hd=hd, sl=sl, r=r: e.tensor_tensor(
                            out=xn[:, 2 * hd + dv, sl], in0=po[:, :], in1=r, op=ALU.mult),
                         reads=[pob, rb], writes=[self.xn_b[2 * hd + dv][t]])
        self.out_proj(wo)

    def build(self):
        self.setup()
        ph = self.phases
        for l in range(2):
            if f"ffn1_{l}" in ph:
                self.rmsnorm_x(f"ffn1_{l}")
                self.ffn(self.w["ffn1_w_gu"], self.w["ffn1_w_down"], l)
            if f"mix_{l}" in ph:
                self.rmsnorm_x(f"mix_{l}")
                self.mixer_proj(l)
                if l == 0:
                    self.banded(0, [(j, j // 2, 0, j, (j // 2) * 64) for j in range(4)], 2, 1,
                                CB_MULTA, CF_NSLA, slopes(8), True)
                    self.stick(0, [(4 + j, 2 + j, 128 + j * 128, 4 + j) for j in range(4)])
                    if "noout" not in ph:
                        self.out_proj(self.w["ev_w_out"].ap()[0])
                else:
                    self.banded(1, [(j, j, j * 128, j, None) for j in range(8)], 17, 16,
                                CB_MULTC, CF_NSLC, slopes(16), False)
                    if "noout" not in ph:
                        self.out_proj(self.w["od_w_out"].ap()[0])
            if f"xa_{l}" in ph:
                self.rmsnorm_x(f"xa_{l}")
                self.xattn(l)
            if f"ffn2_{l}" in ph:
                self.rmsnorm_x(f"ffn2_{l}")
                self.ffn(self.w["ffn2_w_gu"], self.w["ffn2_w_down"], l)
        self.finish()
        self.S.emit(self.nc, self.es)


ALL_PHASES = tuple(f"{p}_{l}" for l in range(2) for p in ("ffn1", "mix", "xa", "ffn2"))


def build_nc(phases=ALL_PHASES, dbg=False):
    nc = bass.Bass("TRN2", target_bir_lowering=False)
    es = ExitStack()
    with es:
        p = Prog(nc, es, phases, dbg)
        p.build()
    nc.used_w = set(p.w.keys())
    return nc


def make_in_maps(inp, x_override=None, used_w=None, ncores=NCORES):
    gains = build_gains(inp)
    cb, cf = build_consts()
    x = np.asarray(inp["x"], np.float32) if x_override is None else x_override
    mem = np.asarray(inp["mem"], np.float32)
    shared = {k: np.ascontiguousarray(np.asarray(inp[k], np.float32)) for k in W_SHAPES
              if used_w is None or k in used_w}
    maps = []
    for core in range(ncores):
        b, h = core // 2, core % 2
        m = dict(shared)
        m["xT"] = np.ascontiguousarray(x[b, h * TOK:(h + 1) * TOK, :].T)
        m["memT"] = np.ascontiguousarray(mem[b].T)
        m["gains"] = gains
        m["cb"] = cb
        m["cf"] = cf
        m["flag"] = np.full((128, 128), float(h), np.float32)
        maps.append(m)
    return maps


def kernel(**inputs):
    nc = build_nc()
    maps = make_in_maps(inputs, used_w=nc.used_w)
    res = run_bass_kernel_spmd(nc, maps, core_ids=list(range(NCORES)))
    out = np.empty((4, 4096, D), np.float32)
    for core in range(NCORES):
        b, h = core // 2, core % 2
        out[b, h * TOK:(h + 1) * TOK, :] = np.asarray(res.results[core]["outT"]).T
    return out
```

```python
from contextlib import ExitStack
import numpy as np
import concourse.bass as bass
import concourse.mybir as mybir
from concourse.bass_utils import run_bass_kernel_spmd

F32 = mybir.dt.float32
BF16 = mybir.dt.bfloat16
AF = mybir.ActivationFunctionType
ALU = mybir.AluOpType

D = 1024
NCH = 8
TOK = 2048
TC = 512
NTC = TOK // TC
DFF = 2816
NFC = DFF // 128
EPS = 1e-6
NCORES = 8
NGROUPS = [4]


class Chan:
    def __init__(self, name):
        self.name = name
        self.sem = None
        self.cnt = 0


class Buf:
    def __init__(self, name, after=()):
        self.name = name
        self.writers = {}
        self.readers = {}
        self.chan = None
        for i, o in enumerate(after):
            self.readers[("init", i)] = o


class Op:
    __slots__ = ("eng", "fn", "deps", "dma", "chan", "val", "flag", "key", "inc")

    def __init__(self, eng, fn, dma=False, chan=None):
        self.eng = eng
        self.fn = fn
        self.deps = []
        self.dma = dma
        self.chan = chan
        self.val = 0
        self.flag = False
        self.key = None
        self.inc = 16


ENGS = ("pe", "act", "dve", "pool", "sp")


class Sched:
    def __init__(self):
        self.q = {e: [] for e in ENGS}
        self.chans = []
        self.nbuf = 0

    def buf(self, name, after=()):
        return Buf(name, after)

    def bufs(self, name, n, after=()):
        return [Buf(f"{name}{i}", after) for i in range(n)]

    def chan(self, name):
        c = Chan(f"{name}_{len(self.chans)}")
        self.chans.append(c)
        return c

    def _add(self, op, reads, writes):
        deps = {}

        def need(o):
            if o is op:
                return
            if (not o.dma) and (not op.dma) and o.eng == "pe" and op.eng == "pe":
                return
            deps[id(o)] = o

        for b in reads:
            for k, o in b.writers.items():
                need(o)
        for b in writes:
            for k, o in b.writers.items():
                if op.dma and o.dma:
                    continue
                need(o)
            for k, o in b.readers.items():
                need(o)
        op.deps = list(deps.values())
        for o in op.deps:
            o.flag = True
        for b in writes:
            if op.dma:
                b.writers = {k: o for k, o in b.writers.items() if o.dma}
                b.writers[op.key] = op
            else:
                b.writers = {op.key: op}
            b.readers = {}
        for b in reads:
            if b not in writes:
                b.readers[op.key] = op
        self.q[op.eng].append(op)
        return op

    def op(self, eng, fn, reads=(), writes=()):
        o = Op(eng, fn)
        o.key = eng
        return self._add(o, reads, writes)

    def dma(self, eng, out, in_, reads=(), writes=(), chan=None):
        if chan is None:
            b = writes[0]
            if b.chan is None:
                b.chan = self.chan("c_" + b.name)
            chan = b.chan
        o = Op(eng, lambda e: e.dma_start(out=out, in_=in_), dma=True, chan=chan)
        chan.cnt += 16
        o.val = chan.cnt
        o.key = ("ch", id(chan))
        return self._add(o, reads, writes)

    def coll(self, fn, reads=(), writes=()):
        ch = self.chan(f"cc{len(self.chans)}")
        o = Op("pool", fn, dma=True, chan=ch)
        o.inc = 1
        ch.cnt += 1
        o.val = ch.cnt
        o.key = ("ch", id(ch))
        return self._add(o, reads, writes)

    def barrier(self):
        res = []
        for e in ENGS:
            for o in reversed(self.q[e]):
                if not o.dma:
                    o.flag = True
                    res.append(o)
                    break
        seen = set()
        for e in ENGS:
            for o in reversed(self.q[e]):
                if o.dma and id(o.chan) not in seen:
                    seen.add(id(o.chan))
                    res.append(o)
        return res

    def emit(self, nc, es):
        for e in ENGS:
            c = 0
            for o in self.q[e]:
                if not o.dma and o.flag:
                    c += 1
                    o.val = c
        esem = {e: es.enter_context(nc.semaphore("sem_" + e)) for e in ENGS}
        for ch in self.chans:
            ch.sem = es.enter_context(nc.semaphore(ch.name))
        block = es.enter_context(nc.Block())

        def body_for(ename):
            def body(e):
                waited = {}
                for o in self.q[ename]:
                    need = {}
                    for d in o.deps:
                        sem = d.chan.sem if d.dma else esem[d.eng]
                        k = id(sem)
                        if waited.get(k, 0) >= d.val:
                            continue
                        if k not in need or need[k][1] < d.val:
                            need[k] = (sem, d.val)
                    for k, (sem, val) in need.items():
                        e.wait_ge(sem, val)
                        waited[k] = val
                    ins = o.fn(e)
                    if ins is None:
                        continue
                    if o.dma:
                        ins.then_inc(o.chan.sem, o.inc)
                    elif o.flag:
                        ins.then_inc(esem[ename], 1)
            return body

        block.tensor(body_for("pe"))
        block.scalar(body_for("act"))
        block.vector(body_for("dve"))
        block.gpsimd(body_for("pool"))
        block.sync(body_for("sp"))


def _col_chunks(v):
    v = np.asarray(v, np.float32)
    return np.ascontiguousarray(v.reshape(-1, 128).T)


def _gain_layout():
    cols = {}
    n = 0
    for l in range(2):
        for nm, w in ((f"ffn1_{l}", 8), (f"mix_{l}", 8), (f"xa_{l}", 8), (f"xam_{l}", 8), (f"ffn2_{l}", 8),
                      (f"xaq_{l}", 2), (f"xak_{l}", 2)):
            cols[nm] = n
            n += w
    for nm, w in (("evq", 1), ("evk", 1), ("odq", 1), ("odk", 1), ("sinks", 8)):
        cols[nm] = n
        n += w
    return cols, n


GCOL, NGCOL = _gain_layout()


def build_gains(inp):
    g = np.zeros((128, NGCOL), np.float32)
    for l in range(2):
        g[:, GCOL[f"ffn1_{l}"]:][:, :8] = _col_chunks(inp["ffn1_norm"][l])
        g[:, GCOL[f"mix_{l}"]:][:, :8] = _col_chunks(inp["mix_norm"][l])
        g[:, GCOL[f"xa_{l}"]:][:, :8] = _col_chunks(inp["xa_norm"][l])
        g[:, GCOL[f"xam_{l}"]:][:, :8] = _col_chunks(inp["xa_mem_norm"][l])
        g[:, GCOL[f"ffn2_{l}"]:][:, :8] = _col_chunks(inp["ffn2_norm"][l])
        g[:, GCOL[f"xaq_{l}"]:][:, :2] = _col_chunks(inp["xa_q_gain"][l])
        g[:, GCOL[f"xak_{l}"]:][:, :2] = _col_chunks(inp["xa_k_gain"][l])
    g[:, GCOL["evq"]] = np.tile(np.asarray(inp["ev_q_gain"][0], np.float32), 2)
    g[:, GCOL["evk"]] = np.tile(np.asarray(inp["ev_k_gain"][0], np.float32), 2)
    g[:, GCOL["odq"]] = np.tile(np.asarray(inp["od_q_gain"][0], np.float32), 2)
    g[:, GCOL["odk"]] = np.tile(np.asarray(inp["od_k_gain"][0], np.float32), 2)
    g[:, GCOL["sinks"]:GCOL["sinks"] + 8] = np.asarray(inp["ev_sinks"][0], np.float32)[None, :]
    return g


NTC_MASK = 23
CB_NEGU, CB_NEGI, CB_TRI7 = 0, 128, 256
CB_MULTC = CB_TRI7 + 7 * 128
CB_MULTA = CB_MULTC + NTC_MASK * 128
CB_N = CB_MULTA + 8 * 128
CF_D0, CF_D0C, CF_NSLC, CF_NSLA, CF_LNS = 0, 128, 256, 256 + 16 * 17, 256 + 16 * 17 + 8 * 2
CF_N = CF_LNS + 4


def slopes(n):
    return [2.0 ** (-8.0 * (i + 1) / n) for i in range(n)]


def build_consts():
    s = np.arange(128)[:, None]
    t = np.arange(128)[None, :]
    cb = np.zeros((128, CB_N), np.float32)
    cb[:, CB_NEGU:CB_NEGU + 128] = -(s >= t).astype(np.float32)
    cb[:, CB_NEGI:CB_NEGI + 128] = -(s == t).astype(np.float32)
    cb[:, CB_TRI7 + 3 * 128:CB_TRI7 + 4 * 128] = (s < t)
    cb[:, CB_TRI7 + 4 * 128:CB_TRI7 + 7 * 128] = 1.0
    for j in range(17):
        d = 128 * j + t - s
        m = ((d >= 0) & (d <= 128)).astype(np.float32) + ((d >= 0) & (d % 4 == 0) & (d <= 512)) \
            + ((d >= 0) & (d % 16 == 0) & (d <= 2048))
        cb[:, CB_MULTC + (3 + j) * 128:CB_MULTC + (4 + j) * 128] = m
    for j in range(2):
        d = 128 * j + t - s
        cb[:, CB_MULTA + (3 + j) * 128:CB_MULTA + (4 + j) * 128] = ((d >= 0) & (d <= 127))
    cf = np.zeros((128, CF_N), np.float32)
    cf[:, CF_D0:CF_D0 + 128] = t - s
    cf[:, CF_D0C:CF_D0C + 128] = np.maximum(t - s, 0)
    sc = slopes(16)
    for h in range(16):
        for j in range(17):
            cf[:, CF_NSLC + h * 17 + j] = -sc[h] * 128.0 * j
    sa = slopes(8)
    for h in range(8):
        for j in range(2):
            cf[:, CF_NSLA + h * 2 + j] = -sa[h] * 128.0 * j
    cf[:, CF_LNS + 0] = 0.0
    cf[:, CF_LNS + 1] = np.log(0.125)
    cf[:, CF_LNS + 2] = np.log(1.0 / 16.0)
    cf[:, CF_LNS + 3] = 1.0
    return cb, cf


W_SHAPES = {
    "ffn1_w_gu": [2, D, 2 * DFF], "ffn1_w_down": [2, DFF, D],
    "ffn2_w_gu": [2, D, 2 * DFF], "ffn2_w_down": [2, DFF, D],
    "ev_w_in": [1, D, 2304], "ev_w_out": [1, D, D], "od_w_in": [1, D, 3072], "od_w_out": [1, D, D],
    "xa_w_q": [2, D, D], "xa_w_kv": [2, D, 2 * D], "xa_w_o": [2, D, D],
}
UNIT = 2048
NUNITS = 14


class Ring:
    def __init__(self, S, tile, n):
        self.tile = tile
        self.n = n
        self.bufs = [S.buf(f"ring{i}") for i in range(n)]
        self.pos = 0

    def alloc(self, k=1):
        if self.pos + k > self.n:
            self.pos = 0
        u = self.pos
        self.pos += k
        return self.tile[:, u * UNIT:(u + k) * UNIT], self.bufs[u:u + k]


class Prog:
    def __init__(self, nc, es, phases, dbg=False):
        self.nc = nc
        self.es = es
        self.S = S = Sched()
        self.phases = phases
        dt = nc.dram_tensor
        self.xT_d = dt("xT", [D, TOK], F32, kind="ExternalInput")
        self.memT_d = dt("memT", [D, 256], F32, kind="ExternalInput")
        self.gains_d = dt("gains", [128, NGCOL], F32, kind="ExternalInput")
        self.cb_d = dt("cb", [128, CB_N], F32, kind="ExternalInput")
        self.cf_d = dt("cf", [128, CF_N], F32, kind="ExternalInput")
        self.flag_d = dt("flag", [128, 128], F32, kind="ExternalInput")
        class _LazyW(dict):
            def __missing__(d, nm):
                d[nm] = dt(nm, W_SHAPES[nm], F32, kind="ExternalInput")
                return d[nm]
        self.w = _LazyW()
        self.dbg = dbg
        self.out_d = dt("outT", [D, TOK], F32, kind="ExternalOutput")
        self.out_buf = S.buf("outT")
        self.scr = {}
        dk = dict(kind="ExternalOutput") if dbg else {}
        if dbg:
            self.xn_d = dt("xn_dump", [D, TOK], BF16, kind="ExternalOutput")
        for l, (nk, vc) in enumerate(((6, 640), (8, 1024))):
            self.scr[l] = dict(
                q=dt(f"q{l}_d", [8 * 128, TOK], BF16, **dk), q_b=S.buf(f"q{l}_d"),
                kloc=[dt(f"kloc{l}_{c}", [128, TOK], BF16) for c in range(nk)],
                kloc_b=[S.buf(f"kloc{l}_{c}") for c in range(nk)],
                kall=[dt(f"kall{l}_{c}", [256, TOK], BF16) for c in range(nk)],
                kall_b=[S.buf(f"kall{l}_{c}") for c in range(nk)],
                vloc=[dt(f"vloc{l}_{j}", [256, vc], BF16) for j in range(8)],
                vloc_b=[S.buf(f"vloc{l}_{j}") for j in range(8)],
                vall=[dt(f"vall{l}_{j}", [512, vc], BF16) for j in range(8)],
                vall_b=[S.buf(f"vall{l}_{j}") for j in range(8)],
                nk=nk, vc=vc)

        sb = lambda name, shape, dtype: es.enter_context(nc.sbuf_tensor(name, shape, dtype))
        ps = lambda name: es.enter_context(nc.psum_tensor(name, [128, 512], F32))
        self.xT = sb("xT_sb", [128, NCH, TOK], F32)
        self.xT_b = [[S.buf(f"xT{c}_{t}") for t in range(NTC)] for c in range(NCH)]
        self.xn = sb("xn_sb", [128, NCH, TOK], BF16)
        self.xn_b = [[S.buf(f"xn{c}_{t}") for t in range(NTC)] for c in range(NCH)]
        self.gains = sb("gains_sb", [128, NGCOL], F32)
        self.cb = sb("cb_sb", [128, CB_N], BF16)
        self.cf = sb("cf_sb", [128, CF_N], F32)
        self.const_b = S.buf("consts")
        self.onesm = sb("onesm", [128, 128], BF16)
        self.ones256 = sb("ones256", [128, 128], BF16)
        self.ones1 = sb("ones1", [128, 128], BF16)
        self.bd64 = sb("bd64", [128, 128], BF16)
        self.flagt = sb("flagt", [128, 128], F32)
        self.flagb = sb("flagb", [128, 128], BF16)
        self.epsc = sb("epsc", [128, 2], F32)
        self.exps = sb("exps", [128, 8], F32)
        self.sq = [sb(f"sq{i}", [128, TC], BF16) for i in range(4)]
        self.sq_b = S.bufs("sq", 4)
        self.rstd = [sb(f"rstd{i}", [128, TC], F32) for i in range(2)]
        self.rstd_b = S.bufs("rstd", 2)
        self.ring_t = sb("ring", [128, NUNITS * UNIT], BF16)
        self.ring = Ring(S, self.ring_t, NUNITS)
        self.sf = sb("scr_f32", [128, 3072], F32)
        self.sbf = sb("scr_bf", [128, 8192], BF16)
        self.ps = [ps(f"ps{i}") for i in range(8)]
        self.ps_b = S.bufs("ps", 8)
        self.norm_i = 0
        self.sq_i = 0

    def gcol(self, name, i=0):
        c = GCOL[name] + i
        return self.gains[:, c:c + 1]

    def setup(self):
        S = self.S
        cb_ = self.const_b
        S.dma("sp", self.gains[:, :], self.gains_d[:, :], writes=[cb_])
        S.dma("sp", self.cf[:, :], self.cf_d[:, :], writes=[cb_])
        S.dma("sp", self.flagt[:, :], self.flag_d[:, :], writes=[cb_])
        for i in range(0, CB_N, 1024):
            j = min(CB_N, i + 1024)
            S.dma("pool", self.cb[:, i:j], self.cb_d[:, i:j], writes=[cb_])
        S.op("pool", lambda e: e.memset(self.onesm[:, :], 1.0 / D), writes=[cb_])
        S.op("pool", lambda e: e.memset(self.ones256[:, :], 1.0 / 256), writes=[cb_])
        S.op("pool", lambda e: e.memset(self.ones1[:, :], 1.0), writes=[cb_])
        S.op("pool", lambda e: e.memset(self.bd64[:, :], 0.0), writes=[cb_])
        S.op("pool", lambda e: e.memset(self.bd64[0:64, 0:64], 1.0 / 64), writes=[cb_])
        S.op("pool", lambda e: e.memset(self.bd64[64:128, 64:128], 1.0 / 64), writes=[cb_])
        S.op("pool", lambda e: e.memset(self.epsc[:, :], EPS), writes=[cb_])
        S.op("pool", lambda e: e.memset(self.sbf[:, :], 0.0), writes=[cb_])
        S.op("dve", lambda e: e.tensor_copy(out=self.flagb[:, :], in_=self.flagt[:, :]), reads=[cb_], writes=[cb_])
        sc = GCOL["sinks"]
        S.op("act", lambda e: e.activation(out=self.exps[:, :], in_=self.gains[:, sc:sc + 8], func=AF.Exp),
             reads=[cb_], writes=[cb_])
        xv = self.xT_d.ap().rearrange("(c p) t -> p c t", p=128)
        for t in range(NTC):
            for c in range(NCH):
                S.dma("sp", self.xT[:, c, t * TC:(t + 1) * TC], xv[:, c, t * TC:(t + 1) * TC],
                      writes=[self.xT_b[c][t]])

    def finish(self):
        S = self.S
        ov = self.out_d.ap().rearrange("(c p) t -> p c t", p=128)
        for t in range(NTC):
            for c in range(NCH):
                S.dma("sp", ov[:, c, t * TC:(t + 1) * TC], self.xT[:, c, t * TC:(t + 1) * TC],
                      reads=[self.xT_b[c][t]], writes=[self.out_buf])
        if self.dbg:
            xv = self.xn_d.ap().rearrange("(c p) t -> p c t", p=128)
            for t in range(NTC):
                for c in range(NCH):
                    S.dma("sp", xv[:, c, t * TC:(t + 1) * TC], self.xn[:, c, t * TC:(t + 1) * TC],
                          reads=[self.xn_b[c][t]], writes=[self.out_buf])
        S.op("sp", lambda e: None, reads=[self.out_buf])

    def rstd_from(self, srcs, ones, n, lnbias, out_rs, out_rsb, reads_extra=()):
        S = self.S
        i = self.norm_i % 2
        self.norm_i += 1
        pss, pssb = self.ps[6 + i], self.ps_b[6 + i]
        k = len(srcs)
        for c, (ap, bl) in enumerate(srcs):
            q = self.sq_i % 4
            self.sq_i += 1
            sq, sqb = self.sq[q], self.sq_b[q]
            S.op("act", lambda e, sq=sq, ap=ap: e.activation(out=sq[:, 0:n], in_=ap, func=AF.Square),
                 reads=list(bl), writes=[sqb])
            S.op("pe", lambda e, sq=sq, pss=pss, c=c: e.matmul(pss[:, 0:n], lhsT=ones[:, :], rhs=sq[:, 0:n],
                                                             start=(c == 0), stop=(c == k - 1)),
                 reads=[self.const_b, sqb], writes=[pssb])
        S.op("act", lambda e: e.activation(out=out_rs, in_=pss[:, 0:n], func=AF.Ln, bias=self.epsc[:, 0:1]),
             reads=[pssb, self.const_b], writes=[out_rsb])
        S.op("act", lambda e: e.activation(out=out_rs, in_=out_rs, func=AF.Exp, scale=-0.5, bias=lnbias),
             reads=[out_rsb, self.const_b], writes=[out_rsb])

    def lnb(self, i):
        return self.cf[:, CF_LNS + i:CF_LNS + i + 1]

    def rmsnorm_x(self, gname):
        S = self.S
        xT, xn = self.xT, self.xn
        for t in range(NTC):
            sl = slice(t * TC, (t + 1) * TC)
            i = self.norm_i % 2
            rs, rsb = self.rstd[i], self.rstd_b[i]
            self.rstd_from([(xT[:, c, sl], [self.xT_b[c][t]]) for c in range(NCH)], self.onesm, TC,
                           self.lnb(0), rs[:, :], rsb)
            for c in range(NCH):
                S.op("dve", lambda e, c=c, rs=rs, sl=sl: e.scalar_tensor_tensor(
                        out=xn[:, c, sl], in0=xT[:, c, sl], scalar=self.gcol(gname, c),
                        in1=rs[:, :], op0=ALU.mult, op1=ALU.mult),
                     reads=[self.xT_b[c][t], rsb, self.const_b], writes=[self.xn_b[c][t]])

    def load_cols(self, w_ap2d, c0, ncols=256):
        ap, bufs = self.ring.alloc(1)
        v = ap.rearrange("p (c f) -> p c f", c=NCH)
        src = w_ap2d.rearrange("(c p) f -> p c f", p=128)
        self.S.dma("pool", v[:, :, 0:ncols], src[:, :, c0:c0 + ncols], writes=bufs)
        return v, bufs

    def load_rows(self, w_ap2d, r0):
        ap, bufs = self.ring.alloc(1)
        v = ap.rearrange("p (j m) -> p j m", j=2)
        src = w_ap2d[r0:r0 + 256, :].rearrange("(j p) m -> p j m", p=128)
        self.S.dma("pool", v[:, :, :], src, writes=bufs)
        return v, bufs

    def ffn(self, wgu_d, wdn_d, layer):
        S = self.S
        after = S.barrier()
        sg = [self.sf[:, i * TC:(i + 1) * TC] for i in range(2)]
        sg_b = S.bufs("sg", 2, after)
        act = [self.sbf[:, i * 2 * TC:(i + 1) * 2 * TC].rearrange("p (j t) -> p j t", j=2) for i in range(3)]
        act_b = [[S.buf(f"act{i}_{j}", after) for j in range(2)] for i in range(3)]
        NG = NFC // 2
        gu2 = wgu_d.ap()[layer]
        dn2 = wdn_d.ap()[layer]
        W = {}

        def load(g):
            f0 = g * 256
            W[g] = (self.load_cols(gu2, f0), self.load_cols(gu2, DFF + f0), self.load_rows(dn2, f0))

        items = [(g, t) for g in range(NG) for t in range(NTC)]
        xn = self.xn

        def emit_gu(i):
            g, t = items[i]
            sl = slice(t * TC, (t + 1) * TC)
            a = i % 3
            for j in range(2):
                k = (2 * i + j) % 2
                pg, pgb = self.ps[k], self.ps_b[k]
                pu, pub = self.ps[2 + k], self.ps_b[2 + k]
                for (pp, ppb, gi) in ((pg, pgb, 0), (pu, pub, 1)):
                    wv, wb = W[g][gi]
                    for c in range(NCH):
                        S.op("pe", lambda e, pp=pp, wv=wv, c=c, j=j, sl=sl: e.matmul(
                                pp[:, :], lhsT=wv[:, c, j * 128:(j + 1) * 128], rhs=xn[:, c, sl],
                                start=(c == 0), stop=(c == NCH - 1)),
                             reads=wb + [self.xn_b[c][t]], writes=[ppb])
                S.op("act", lambda e, k=k, pg=pg: e.activation(out=sg[k], in_=pg[:, :], func=AF.Silu),
                     reads=[pgb], writes=[sg_b[k]])
                S.op("dve", lambda e, k=k, pu=pu, a=a, j=j: e.tensor_tensor(
                        out=act[a][:, j, :], in0=pu[:, :], in1=sg[k], op=ALU.mult),
                     reads=[pub, sg_b[k]], writes=[act_b[a][j]])

        def emit_down(i):
            g, t = items[i]
            sl = slice(t * TC, (t + 1) * TC)
            a = i % 3
            wv, wb = W[g][2]
            for m in range(NCH):
                k = 4 + (m % 2)
                py, pyb = self.ps[k], self.ps_b[k]
                for j in range(2):
                    S.op("pe", lambda e, py=py, j=j, m=m, wv=wv, a=a: e.matmul(
                            py[:, :], lhsT=wv[:, j, m * 128:(m + 1) * 128], rhs=act[a][:, j, :],
                            start=(j == 0), stop=(j == 1)),
                         reads=wb + [act_b[a][j]], writes=[pyb])
                S.op("dve", lambda e, py=py, m=m, sl=sl: e.scalar_tensor_tensor(
                        out=self.xT[:, m, sl], in0=py[:, :], scalar=0.5, in1=self.xT[:, m, sl],
                        op0=ALU.mult, op1=ALU.add),
                     reads=[pyb, self.xT_b[m][t]], writes=[self.xT_b[m][t]])

        PF = 3
        for g in range(min(PF, NG)):
            load(g)
        n = len(items)
        for i in range(n + 1):
            if i < n:
                g, t = items[i]
                if t == 1 and g + PF < NG:
                    load(g + PF)
                emit_gu(i)
            if i >= 1:
                emit_down(i - 1)

    def out_proj(self, w2d):
        S = self.S
        units = [self.load_rows(w2d, r * 256) for r in range(4)]
        for t in range(NTC):
            sl = slice(t * TC, (t + 1) * TC)
            for m in range(NCH):
                k = 4 + (m % 2)
                py, pyb = self.ps[k], self.ps_b[k]
                for oc in range(NCH):
                    wv, wb = units[oc // 2]
                    S.op("pe", lambda e, py=py, wv=wv, oc=oc, m=m, sl=sl: e.matmul(
                            py[:, :], lhsT=wv[:, oc % 2, m * 128:(m + 1) * 128], rhs=self.xn[:, oc, sl],
                            start=(oc == 0), stop=(oc == NCH - 1)),
                         reads=wb + [self.xn_b[oc][t]], writes=[pyb])
                S.op("dve", lambda e, py=py, m=m, sl=sl: e.tensor_tensor(
                        out=self.xT[:, m, sl], in0=py[:, :], in1=self.xT[:, m, sl], op=ALU.add),
                     reads=[pyb, self.xT_b[m][t]], writes=[self.xT_b[m][t]])

    def mixer_proj(self, layer):
        S = self.S
        after = S.barrier()
        sc = self.scr[layer]
        even = layer == 0
        w2d = (self.w["ev_w_in"] if even else self.w["od_w_in"]).ap()[0]
        nun = 9 if even else 12
        stg = [self.sbf[:, i * TC:(i + 1) * TC] for i in range(4)]
        stg_b = S.bufs("stg", 4, after)
        stg_ch = [S.chan(f"stgch{layer}_{i}") for i in range(4)]
        sqh = [self.sbf[:, (4 + i) * TC:(5 + i) * TC] for i in range(2)]
        sqh_b = S.bufs("sqh", 2, after)
        rr = [self.sf[:, i * TC:(i + 1) * TC] for i in range(4)]
        rr_b = S.bufs("rr", 4, after)
        vst = [self.sbf[:, 3072 + i * 1024:3072 + (i + 1) * 1024] for i in range(2)]
        vst_b = S.bufs("vst", 2, after)
        vst_ch = [S.chan(f"vstch{layer}_{i}") for i in range(2)]
        cnt = {"s": 0, "q": 0, "v": 0}
        gq, gk = ("evq", "evk") if even else ("odq", "odk")
        jobs = []
        if even:
            for j in range(4):
                jobs.append((j // 2, (j % 2) * 128, "norm", gq, 1, sc["q"], sc["q_b"], j))
            for g in range(2):
                jobs.append((2, ("dup", g * 64), "norm", gk, 0, sc["kloc"], sc["kloc_b"], g))
            for j in range(4):
                jobs.append((3 + j // 2, (j % 2) * 128, "scale", None, 0.125, sc["q"], sc["q_b"], 4 + j))
            for j in range(4):
                jobs.append((5 + j // 2, (j % 2) * 128, "scale", None, 1.0, sc["kloc"], sc["kloc_b"], 2 + j))
            vjobs = [[(2, 128, 128)], [(7, 0, 256), (8, 0, 256)]]
        else:
            for j in range(8):
                jobs.append((j // 2, (j % 2) * 128, "norm", gq, 1, sc["q"], sc["q_b"], j))
            for j in range(8):
                jobs.append((4 + j // 2, (j % 2) * 128, "norm", gk, 0, sc["kloc"], sc["kloc_b"], j))
            vjobs = [[(8, 0, 256), (9, 0, 256)], [(10, 0, 256), (11, 0, 256)]]
        U = {}

        def need(u):
            if u not in U:
                U[u] = self.load_cols(w2d, u * 256)
            return U[u]

        for u in range(nun):
            need(u)
        xn = self.xn
        for (u, off, kind, gname, par, dten, dbuf, drow) in jobs:
            wv, wb = need(u)
            for t in range(NTC):
                sl = slice(t * TC, (t + 1) * TC)
                k = cnt["q"] % 4
                cnt["q"] += 1
                pq, pqb = self.ps[k], self.ps_b[k]
                if isinstance(off, tuple):
                    c0 = off[1]
                    for half in range(2):
                        for c in range(NCH):
                            S.op("pe", lambda e, pq=pq, wv=wv, c=c, c0=c0, half=half, sl=sl: e.matmul(
                                    pq[half * 64:(half + 1) * 64, :], lhsT=wv[:, c, c0:c0 + 64], rhs=xn[:, c, sl],
                                    start=(c == 0), stop=(c == NCH - 1)),
                                 reads=wb + [self.xn_b[c][t]], writes=[pqb])
                else:
                    for c in range(NCH):
                        S.op("pe", lambda e, pq=pq, wv=wv, c=c, off=off, sl=sl: e.matmul(
                                pq[:, :], lhsT=wv[:, c, off:off + 128], rhs=xn[:, c, sl],
                                start=(c == 0), stop=(c == NCH - 1)),
                             reads=wb + [self.xn_b[c][t]], writes=[pqb])
                si = cnt["s"] % 4
                cnt["s"] += 1
                if kind == "norm":
                    r, rb = rr[k], rr_b[k]
                    self.rstd_from([(pq[:, :], [pqb])], self.bd64, TC, self.lnb(par), r, rb)
                    S.op("dve", lambda e, si=si, pq=pq, r=r, gname=gname: e.scalar_tensor_tensor(
                            out=stg[si], in0=pq[:, :], scalar=self.gcol(gname), in1=r, op0=ALU.mult, op1=ALU.mult),
                         reads=[pqb, rb, self.const_b], writes=[stg_b[si]])
                else:
                    S.op("act", lambda e, si=si, pq=pq, par=par: e.activation(
                            out=stg[si], in_=pq[:, :], func=AF.Copy, scale=float(par)),
                         reads=[pqb], writes=[stg_b[si]])
                if isinstance(dten, list):
                    S.dma("sp", dten[drow][:, sl], stg[si], reads=[stg_b[si]], writes=[dbuf[drow]], chan=stg_ch[si])
                    if t == NTC - 1:
                        self.gather(sc["kloc"][drow], sc["kloc_b"][drow], sc["kall"][drow], sc["kall_b"][drow])
                else:
                    S.dma("sp", dten[drow * 128:(drow + 1) * 128, sl], stg[si], reads=[stg_b[si]], writes=[dbuf],
                          chan=stg_ch[si])
        vc = sc["vc"]
        for nb in range(TOK // 128):
            tb, tsl = nb // 4, slice(nb * 128, (nb + 1) * 128)
            vi = cnt["v"] % 2
            cnt["v"] += 1
            col = 0
            for bi, grp in enumerate(vjobs):
                pv, pvb = self.ps[2 + 2 * vi + bi], self.ps_b[2 + 2 * vi + bi]
                pc = 0
                for (u, c0, n) in grp:
                    wv, wb = need(u)
                    for c in range(NCH):
                        S.op("pe", lambda e, pv=pv, wv=wv, c=c, c0=c0, n=n, pc=pc, tsl=tsl: e.matmul(
                                pv[:, pc:pc + n], lhsT=xn[:, c, tsl], rhs=wv[:, c, c0:c0 + n],
                                start=(c == 0), stop=(c == NCH - 1)),
                             reads=wb + [self.xn_b[c][tb]], writes=[pvb])
                    pc += n
                eng = "act" if bi == 0 else "dve"
                if eng == "act":
                    S.op("act", lambda e, pv=pv, vi=vi, col=col, pc=pc: e.activation(
                            out=vst[vi][:, col:col + pc], in_=pv[:, 0:pc], func=AF.Copy),
                         reads=[pvb], writes=[vst_b[vi]])
                else:
                    S.op("dve", lambda e, pv=pv, vi=vi, col=col, pc=pc: e.tensor_copy(
                            out=vst[vi][:, col:col + pc], in_=pv[:, 0:pc]),
                         reads=[pvb], writes=[vst_b[vi]])
                col += pc
            jv = nb // 2
            S.dma("sp", sc["vloc"][jv][(nb % 2) * 128:(nb % 2 + 1) * 128, :], vst[vi][:, 0:vc], reads=[vst_b[vi]],
                  writes=[sc["vloc_b"][jv]], chan=vst_ch[vi])
            if nb % 2 == 1:
                self.gather(sc["vloc"][jv], sc["vloc_b"][jv], sc["vall"][jv], sc["vall_b"][jv])

    def gather(self, src, src_b, dst, dst_b):
        S = self.S
        import os
        if os.environ.get("KDBG_NOCOLL"):
            n = src.shape[0]
            S.dma("sp", dst[0:n, :], src[:, :], reads=[src_b], writes=[dst_b])
            return
        groups = [[2 * i, 2 * i + 1] for i in range(NGROUPS[0])]
        S.coll(lambda e: e.collective_compute("AllGather", ALU.bypass, replica_groups=groups,
                                              ins=[src.ap().opt()], outs=[dst.ap().opt()]),
               reads=[src_b], writes=[dst_b])

    def load_attn_tiles(self, layer, qrow, krow, vcol, nprev):
        S = self.S
        sc = self.scr[layer]
        nk = sc["nk"]
        qh = []
        for hh in range(2):
            qa, qb_ = self.ring.alloc(1)
            lo, hi = hh * 64, (hh + 1) * 64
            zl, zh = (1 - hh) * 64, (2 - hh) * 64
            S.dma("sp", qa[lo:hi, :], sc["q"][qrow * 128 + lo:qrow * 128 + hi, :], reads=[sc["q_b"]], writes=qb_)
            S.op("pool", lambda e, qa=qa, zl=zl, zh=zh: e.memset(qa[zl:zh, :], 0.0), writes=qb_)
            qh.append((qa, qb_))
        ka, kb_ = self.ring.alloc(2)
        npc = nprev * 128
        S.dma("sp", ka[:, 0:npc], sc["kall"][krow][0:128, TOK - npc:TOK], reads=[sc["kall_b"][krow]], writes=kb_)
        S.dma("sp", ka[:, npc:npc + TOK], sc["kloc"][krow][:, :], reads=[sc["kloc_b"][krow]], writes=kb_)
        va, vb_ = self.ring.alloc(2)
        v3 = va.rearrange("p (n c) -> p n c", c=128)
        if nprev == 16:
            for j in range(8):
                S.dma("sp", v3[:, 2 * j:2 * j + 2, :],
                      sc["vall"][j][0:256, vcol:vcol + 128].rearrange("(n p) c -> p n c", p=128),
                      reads=[sc["vall_b"][j]], writes=vb_)
        else:
            assert nprev == 1
            S.dma("sp", v3[:, 0:1, :], sc["vall"][7][128:256, vcol:vcol + 128].rearrange("(n p) c -> p n c", p=128),
                  reads=[sc["vall_b"][7]], writes=vb_)
        for j in range(8):
            S.dma("sp", v3[:, nprev + 2 * j:nprev + 2 * j + 2, :],
                  sc["vloc"][j][:, vcol:vcol + 128].rearrange("(n p) c -> p n c", p=128),
                  reads=[sc["vloc_b"][j]], writes=vb_)
        S.op("dve", lambda e: e.tensor_scalar(out=va[:, 0:npc], in0=va[:, 0:npc], scalar1=self.flagt[:, 0:1],
                                              scalar2=1.0, op0=ALU.mult, op1=ALU.mult),
             reads=vb_ + [self.const_b], writes=vb_)
        return qh, (ka, kb_), (v3, vb_)

    def banded(self, layer, pairs, ND, nprev, mult_off, nsl_off, sl_list, sink):
        S = self.S
        after = S.barrier()
        NT = ND + 6
        msk = [self.sbf[:, i * 3072:i * 3072 + NT * 128] for i in range(2)]
        msk_b = S.bufs("msk", 2, after)
        P = [self.sbf[:, 6144 + i * TC:6144 + (i + 1) * TC] for i in range(4)]
        P_b = S.bufs("P", 4, after)
        tmp = [self.sf[:, i * TC:(i + 1) * TC] for i in range(2)]
        tmp_b = S.bufs("tmp", 2, after)
        cb_ = self.const_b
        hcount = 0
        ucount = 0
        ccount = 0
        for pi, (qrow, krow, vcol, oc, vh) in enumerate(pairs):
            qh, (ka, kb_), (v3, vb_) = self.load_attn_tiles(layer, qrow, krow, vcol, nprev)
            for hh in range(2):
                qa, qb_ = qh[hh]
                h = 2 * pi + hh
                hp = slice(hh * 64, (hh + 1) * 64)
                vs = hp if vh is None else slice(vh, vh + 64)
                mi = hcount % 2
                hcount += 1
                m, mb = msk[mi], msk_b[mi]
                S.op("pool", lambda e, m=m: e.memset(m[:, 0:384], 0.0), writes=[mb])
                S.op("pool", lambda e, m=m: e.memset(m[:, (ND + 3) * 128:(ND + 6) * 128], 0.0), writes=[mb])
                for j in range(ND):
                    src = CF_D0C if j == 0 else CF_D0
                    S.op("act", lambda e, m=m, j=j, src=src, h=h: e.activation(
                            out=m[:, (3 + j) * 128:(4 + j) * 128], in_=self.cf[:, src:src + 128], func=AF.Exp,
                            scale=float(-sl_list[h]), bias=self.cf[:, nsl_off + h * ND + j:nsl_off + h * ND + j + 1]),
                         reads=[cb_], writes=[mb])
                S.op("dve", lambda e, m=m: e.tensor_tensor(out=m, in0=m, in1=self.cb[:, mult_off:mult_off + NT * 128],
                                                          op=ALU.mult),
                     reads=[cb_, mb], writes=[mb])
                for t in range(NTC):
                    ucount = self._banded_chunk(t, ccount, ucount, nprev, ND, hp, vs, h, oc, sink, m, mb, P, P_b, tmp, tmp_b,
                                                qa, qb_, ka, kb_, v3, vb_)
                    ccount += 1

    def _banded_chunk(self, t, ccount, ucount, nprev, ND, hp, vs, h, oc, sink, m, mb, P, P_b, tmp, tmp_b,
                      qa, qb_, ka, kb_, v3, vb_):
        S = self.S
        cb_ = self.const_b
        full_v = (vs == hp)
        ci = ccount % 2
        po, pob = self.ps[2 + ci], self.ps_b[2 + ci]
        pd, pdb = self.ps[4 + ci], self.ps_b[4 + ci]
        Lq0 = nprev + 4 * t
        units = [(L, Lq0 - L) for L in range(max(0, Lq0 - (ND - 1)), Lq0 + 4)]
        n = len(units)
        qsl = slice(t * TC, (t + 1) * TC)

        ZB = (0, 1, 6, 7)

        def cr(j0):
            lo = max(0, -j0)
            hi = min(3, ND - 1 - j0)
            return lo * 128, (hi + 1) * 128

        def st_z(i):
            L, j0 = units[i]
            c0, c1 = cr(j0)
            k = ZB[(ucount + i) % 4]
            S.op("pe", lambda e, k=k, L=L, c0=c0, c1=c1: e.matmul(
                    self.ps[k][:, c0:c1], lhsT=ka[:, L * 128:(L + 1) * 128],
                    rhs=qa[:, t * TC + c0:t * TC + c1], start=True, stop=True),
                 reads=kb_ + qb_, writes=[self.ps_b[k]])

        def st_p(i):
            L, j0 = units[i]
            c0, c1 = cr(j0)
            k = ZB[(ucount + i) % 4]
            p = (ucount + i) % 4
            S.op("act", lambda e, k=k, p=p, c0=c0, c1=c1: e.activation(out=P[p][:, c0:c1], in_=self.ps[k][:, c0:c1], func=AF.Exp),
                 reads=[self.ps_b[k]], writes=[P_b[p]])
            S.op("dve", lambda e, p=p, j0=j0, c0=c0, c1=c1: e.tensor_tensor(
                    out=P[p][:, c0:c1], in0=P[p][:, c0:c1], in1=m[:, (j0 + 3) * 128 + c0:(j0 + 3) * 128 + c1], op=ALU.mult),
                 reads=[P_b[p], mb], writes=[P_b[p]])

        def st_o(i):
            L, j0 = units[i]
            c0, c1 = cr(j0)
            p = (ucount + i) % 4
            if full_v:
                S.op("pe", lambda e, p=p, L=L, i=i, c0=c0, c1=c1: e.matmul(po[:, c0:c1], lhsT=v3[:, L, :], rhs=P[p][:, c0:c1],
                                                             start=(i == 0), stop=(i == n - 1)),
                     reads=vb_ + [P_b[p]], writes=[pob])
            else:
                S.op("pe", lambda e, p=p, L=L, i=i, c0=c0, c1=c1: e.matmul(po[hp, c0:c1], lhsT=v3[:, L, vs], rhs=P[p][:, c0:c1],
                                                             start=(i == 0), stop=(i == n - 1)),
                     reads=vb_ + [P_b[p]], writes=[pob])
            ones = self.flagb[:, :] if L < nprev else self.ones1[:, :]
            S.op("pe", lambda e, p=p, ones=ones, i=i, c0=c0, c1=c1: e.matmul(pd[:, c0:c1], lhsT=ones, rhs=P[p][:, c0:c1],
                                                               start=(i == 0), stop=(i == n - 1)),
                 reads=[cb_, P_b[p]], writes=[pdb])

        for step in range(n + 3):
            if step < n:
                st_z(step)
            if 0 <= step - 1 < n:
                st_p(step - 1)
            if 0 <= step - 3 < n:
                st_o(step - 3)
        ti = ccount % 2
        tm, tmb = tmp[ti], tmp_b[ti]
        if sink:
            S.op("act", lambda e: e.activation(out=tm[hp, :], in_=pd[hp, :], func=AF.Ln, bias=self.exps[hp, h:h + 1]),
                 reads=[pdb, cb_], writes=[tmb])
        else:
            S.op("act", lambda e: e.activation(out=tm[hp, :], in_=pd[hp, :], func=AF.Ln),
                 reads=[pdb], writes=[tmb])
        S.op("act", lambda e: e.activation(out=tm[hp, :], in_=tm[hp, :], func=AF.Exp, scale=-1.0),
             reads=[tmb], writes=[tmb])
        S.op("dve", lambda e: e.tensor_tensor(out=self.xn[hp, oc, qsl], in0=po[hp, :], in1=tm[hp, :], op=ALU.mult),
             reads=[pob, tmb], writes=[self.xn_b[oc][t]])
        return ucount + n

    def stick(self, layer, pairs):
        S = self.S
        after = S.barrier()
        nprev = 16
        ef = [self.sf[:, i * TC:(i + 1) * TC] for i in range(2)]
        ef_b = S.bufs("ef", 2, after)
        sp = [self.sbf[:, i * TC:(i + 1) * TC] for i in range(2)]
        sp_b = S.bufs("sp", 2, after)
        A = [self.sbf[:, (2 + i) * TC:(3 + i) * TC] for i in range(2)]
        A_b = S.bufs("A", 2, after)
        Rs = [self.sbf[:, (4 + i) * TC:(5 + i) * TC] for i in range(2)]
        Rs_b = S.bufs("Rs", 2, after)
        cb_ = self.const_b
        negU = self.cb[:, CB_NEGU:CB_NEGU + 128]
        negI = self.cb[:, CB_NEGI:CB_NEGI + 128]
        one_c = self.cf[:, CF_LNS + 3:CF_LNS + 4]
        ccount = 0
        ucount = 0
        for pi, (qrow, krow, vcol, oc) in enumerate(pairs):
            qh, (ka, kb_), (v3, vb_) = self.load_attn_tiles(layer, qrow, krow, vcol, nprev)
            for hh in range(2):
                qa, qb_ = qh[hh]
                hp = slice(hh * 64, (hh + 1) * 64)
                for t in range(NTC):
                    ucount = self._stick_chunk(t, ccount, ucount, nprev, hp, oc, ef, ef_b, sp, sp_b, A, A_b, Rs, Rs_b,
                                               qa, qb_, ka, kb_, v3, vb_)
                    ccount += 1

    def _stick_chunk(self, t, ccount, ucount, nprev, hp, oc, ef, ef_b, sp, sp_b, A, A_b, Rs, Rs_b,
                     qa, qb_, ka, kb_, v3, vb_):
        S = self.S
        cb_ = self.const_b
        negU = self.cb[:, CB_NEGU:CB_NEGU + 128]
        negI = self.cb[:, CB_NEGI:CB_NEGI + 128]
        one_c = self.cf[:, CF_LNS + 3:CF_LNS + 4]
        ci = ccount % 2
        pR, pRb = self.ps[4 + ci], self.ps_b[4 + ci]
        pO, pOb = self.ps[6 + ci], self.ps_b[6 + ci]
        qsl = slice(t * TC, (t + 1) * TC)
        Ldiag0 = nprev + 4 * t
        Ls = list(range(Ldiag0 + 3, -1, -1))
        n = len(Ls)

        def c0of(i):
            L = Ls[i]
            return (L - Ldiag0) * 128 if L >= Ldiag0 else 0

        tri = self.cb[:, CB_TRI7 + 3 * 128:CB_TRI7 + 4 * 128]

        def s0(i):
            L = Ls[i]
            c0 = c0of(i)
            k = (ucount + i) % 2
            S.op("pe", lambda e, k=k, L=L, c0=c0: e.matmul(self.ps[k][:, c0:], lhsT=ka[:, L * 128:(L + 1) * 128],
                                                          rhs=qa[:, t * TC + c0:(t + 1) * TC], start=True, stop=True),
                 reads=kb_ + qb_, writes=[self.ps_b[k]])

        def s1(i):
            L = Ls[i]
            c0 = c0of(i)
            k = (ucount + i) % 2
            S.op("act", lambda e, k=k, c0=c0: e.activation(out=self.ps[k][:, c0:], in_=self.ps[k][:, c0:], func=AF.Exp),
                 reads=[self.ps_b[k]], writes=[self.ps_b[k]])
            S.op("act", lambda e, k=k, c0=c0: e.activation(out=sp[k][:, c0:], in_=self.ps[k][:, c0:], func=AF.Ln, bias=one_c),
                 reads=[self.ps_b[k], cb_], writes=[sp_b[k]])
            if L >= Ldiag0:
                S.op("dve", lambda e, k=k, c0=c0: e.tensor_tensor(
                        out=sp[k][:, c0:c0 + 128], in0=sp[k][:, c0:c0 + 128], in1=tri, op=ALU.mult),
                     reads=[sp_b[k], cb_], writes=[sp_b[k]])

        def s2(i):
            L = Ls[i]
            c0 = c0of(i)
            k = (ucount + i) % 2
            pC, pCb = self.ps[2 + k], self.ps_b[2 + k]
            S.op("pe", lambda e, k=k, pC=pC, c0=c0: e.matmul(pC[:, c0:], lhsT=negU, rhs=sp[k][:, c0:], start=True, stop=False),
                 reads=[cb_, sp_b[k]], writes=[pCb])
            last = (i == 0)
            S.op("pe", lambda e, pC=pC, L=L, last=last, c0=c0: e.matmul(
                    pC[:, c0:], lhsT=ka[:, L * 128:(L + 1) * 128], rhs=qa[:, t * TC + c0:(t + 1) * TC], start=False, stop=last),
                 reads=kb_ + qb_, writes=[pCb])
            if i > 0:
                kp = (ucount + i - 1) % 2
                cp = c0of(i - 1)
                S.op("pe", lambda e, pC=pC, kp=kp, cp=cp: e.matmul(pC[:, cp:], lhsT=negI, rhs=Rs[kp][:, cp:], start=False, stop=True),
                     reads=[cb_, Rs_b[kp]], writes=[pCb])
            S.op("pe", lambda e, k=k, i=i, c0=c0: e.matmul(pR[:, c0:], lhsT=self.ones1[:, :], rhs=sp[k][:, c0:],
                                                          start=(i == 0), stop=(i == n - 1)),
                 reads=[cb_, sp_b[k]], writes=[pRb])
            if i < n - 1:
                S.op("dve", lambda e, k=k, c0=c0: e.tensor_copy(out=Rs[k][:, c0:], in_=pR[:, c0:]),
                     reads=[pRb], writes=[Rs_b[k]])

        def s3(i):
            L = Ls[i]
            c0 = c0of(i)
            k = (ucount + i) % 2
            pC, pCb = self.ps[2 + k], self.ps_b[2 + k]
            S.op("act", lambda e, k=k, pC=pC, c0=c0: e.activation(out=A[k][:, c0:], in_=pC[:, c0:], func=AF.Exp),
                 reads=[pCb], writes=[A_b[k]])
            if L >= Ldiag0:
                S.op("dve", lambda e, k=k, c0=c0: e.tensor_tensor(
                        out=A[k][:, c0:c0 + 128], in0=A[k][:, c0:c0 + 128], in1=tri, op=ALU.mult),
                     reads=[A_b[k], cb_], writes=[A_b[k]])

        def s4(i):
            L = Ls[i]
            c0 = c0of(i)
            k = (ucount + i) % 2
            S.op("pe", lambda e, k=k, L=L, i=i, c0=c0: e.matmul(pO[:, c0:], lhsT=v3[:, L, :], rhs=A[k][:, c0:],
                                                         start=(i == 0), stop=(i == n - 1)),
                 reads=vb_ + [A_b[k]], writes=[pOb])

        for step in range(n + 3):
            if step < n:
                s0(step)
            if 0 <= step - 1 < n:
                s1(step - 1)
            if 0 <= step - 2 < n:
                s2(step - 2)
                s3(step - 2)
            if 0 <= step - 3 < n:
                s4(step - 3)
        S.op("dve", lambda e: e.tensor_copy(out=self.xn[hp, oc, qsl], in_=pO[hp, :]),
             reads=[pOb], writes=[self.xn_b[oc][t]])
        return ucount + n

    def xattn(self, layer):
        S = self.S
        after = S.barrier()
        cb_ = self.const_b
        memT = self.sf[:, 0:2048].rearrange("p (c m) -> p c m", c=NCH)
        memT_b = S.buf("memT", after)
        rr = [self.sf[:, 2048 + i * TC:2048 + (i + 1) * TC] for i in range(2)]
        rr_b = S.bufs("xrr", 2, after)
        qn = self.sbf[:, 0:4096].rearrange("p (c t) -> p c t", c=NCH)
        qn_b = S.bufs("qn", NCH, after)
        Pm = [self.sbf[:, 4096 + i * 1024:4096 + (i + 1) * 1024].rearrange("p (m t) -> p m t", m=2) for i in range(2)]
        Pm_b = S.bufs("Pm", 2, after)
        rd = [self.sbf[:, 6144 + i * TC:6144 + (i + 1) * TC] for i in range(2)]
        S.dma("sp", memT, self.memT_d.ap().rearrange("(c p) m -> p c m", p=128), writes=[memT_b])
        mn_ap, mn_b = self.ring.alloc(1)
        memn = mn_ap.rearrange("p (c m) -> p c m", c=NCH)
        self.rstd_from([(memT[:, c, :], [memT_b]) for c in range(NCH)], self.onesm, 256, self.lnb(0), rr[0][:, 0:256], rr_b[0])
        for c in range(NCH):
            S.op("dve", lambda e, c=c: e.scalar_tensor_tensor(
                    out=memn[:, c, :], in0=memT[:, c, :], scalar=self.gcol(f"xam_{layer}", c), in1=rr[0][:, 0:256],
                    op0=ALU.mult, op1=ALU.mult),
                 reads=[memT_b, rr_b[0], cb_], writes=mn_b)
        wkv = self.w["xa_w_kv"].ap()[layer]
        kt_ap, kt_b = self.ring.alloc(1)
        KT = kt_ap.rearrange("p (c m) -> p c m", c=NCH)
        vm_ap, vm_b = self.ring.alloc(1)
        Vm = vm_ap.rearrange("p (m c) -> p m c", m=2)
        for hd in range(4):
            wv, wb = self.load_cols(wkv, hd * 256)
            pk = [self.ps[0], self.ps[1]]
            pkb = [self.ps_b[0], self.ps_b[1]]
            for cc in range(2):
                for c in range(NCH):
                    S.op("pe", lambda e, cc=cc, c=c, wv=wv: e.matmul(pk[cc][:, 0:256], lhsT=wv[:, c, cc * 128:(cc + 1) * 128],
                                                                     rhs=memn[:, c, :], start=(c == 0), stop=(c == NCH - 1)),
                         reads=wb + mn_b, writes=[pkb[cc]])
            r, rb = rr[1][:, 0:256], rr_b[1]
            self.rstd_from([(pk[cc][:, 0:256], [pkb[cc]]) for cc in range(2)], self.ones256, 256, self.lnb(0), r, rb)
            for cc in range(2):
                S.op("dve", lambda e, cc=cc, hd=hd, r=r: e.scalar_tensor_tensor(
                        out=KT[:, 2 * hd + cc, :], in0=pk[cc][:, 0:256], scalar=self.gcol(f"xak_{layer}", cc), in1=r,
                        op0=ALU.mult, op1=ALU.mult),
                     reads=[pkb[cc], rb, cb_], writes=kt_b)
        for g in range(4):
            wv, wb = self.load_cols(wkv, D + g * 256)
            for mb in range(2):
                pv, pvb = self.ps[2 + mb], self.ps_b[2 + mb]
                for c in range(NCH):
                    S.op("pe", lambda e, pv=pv, c=c, mb=mb, wv=wv: e.matmul(
                            pv[:, 0:256], lhsT=memn[:, c, mb * 128:(mb + 1) * 128], rhs=wv[:, c, :],
                            start=(c == 0), stop=(c == NCH - 1)),
                         reads=wb + mn_b, writes=[pvb])
                S.op("act", lambda e, pv=pv, mb=mb, g=g: e.activation(out=Vm[:, mb, g * 256:(g + 1) * 256], in_=pv[:, 0:256],
                                                                      func=AF.Copy),
                     reads=[pvb], writes=vm_b)
        wq = self.w["xa_w_q"].ap()[layer]
        wo = self.w["xa_w_o"].ap()[layer]
        WQ = [self.load_cols(wq, g * 256) for g in range(4)]
        xn = self.xn
        for t in range(NTC):
            sl = slice(t * TC, (t + 1) * TC)
            for hd in range(4):
                wv, wb = WQ[hd]
                pq = [self.ps[0], self.ps[1]]
                pqb = [self.ps_b[0], self.ps_b[1]]
                for cc in range(2):
                    for c in range(NCH):
                        S.op("pe", lambda e, cc=cc, c=c, wv=wv, sl=sl: e.matmul(
                                pq[cc][:, :], lhsT=wv[:, c, cc * 128:(cc + 1) * 128], rhs=xn[:, c, sl],
                                start=(c == 0), stop=(c == NCH - 1)),
                             reads=wb + [self.xn_b[c][t]], writes=[pqb[cc]])
                ri = (t * 4 + hd) % 2
                r, rb = rr[ri], rr_b[ri]
                self.rstd_from([(pq[cc][:, :], [pqb[cc]]) for cc in range(2)], self.ones256, TC, self.lnb(2), r, rb)
                for cc in range(2):
                    S.op("dve", lambda e, cc=cc, hd=hd, r=r: e.scalar_tensor_tensor(
                            out=qn[:, 2 * hd + cc, :], in0=pq[cc][:, :], scalar=self.gcol(f"xaq_{layer}", cc), in1=r,
                            op0=ALU.mult, op1=ALU.mult),
                         reads=[pqb[cc], rb, cb_], writes=[qn_b[2 * hd + cc]])
            for hd in range(4):
                pi = (t * 4 + hd) % 2
                Pt, Ptb = Pm[pi], Pm_b[pi]
                for mb in range(2):
                    pz, pzb = self.ps[2 + mb], self.ps_b[2 + mb]
                    for cc in range(2):
                        S.op("pe", lambda e, pz=pz, mb=mb, cc=cc, hd=hd: e.matmul(
                                pz[:, :], lhsT=KT[:, 2 * hd + cc, mb * 128:(mb + 1) * 128], rhs=qn[:, 2 * hd + cc, :],
                                start=(cc == 0), stop=(cc == 1)),
                             reads=kt_b + [qn_b[2 * hd + cc]], writes=[pzb])
                    S.op("act", lambda e, pz=pz, mb=mb, Pt=Pt: e.activation(out=Pt[:, mb, :], in_=pz[:, :], func=AF.Exp),
                         reads=[pzb], writes=[Ptb])
                pd, pdb = self.ps[4], self.ps_b[4]
                for mb in range(2):
                    S.op("pe", lambda e, mb=mb, Pt=Pt: e.matmul(pd[:, :], lhsT=self.ones1[:, :], rhs=Pt[:, mb, :],
                                                              start=(mb == 0), stop=(mb == 1)),
                         reads=[cb_, Ptb], writes=[pdb])
                r, rb = rr[pi], rr_b[pi]
                S.op("act", lambda e, r=r: e.activation(out=r, in_=pd[:, :], func=AF.Ln), reads=[pdb], writes=[rb])
                S.op("act", lambda e, r=r: e.activation(out=r, in_=r, func=AF.Exp, scale=-1.0), reads=[rb], writes=[rb])
                for dv in range(2):
                    po, pob = self.ps[5 + dv], self.ps_b[5 + dv]
                    for mb in range(2):
                        S.op("pe", lambda e, po=po, mb=mb, dv=dv, hd=hd, Pt=Pt: e.matmul(
                                po[:, :], lhsT=Vm[:, mb, hd * 256 + dv * 128:hd * 256 + (dv + 1) * 128], rhs=Pt[:, mb, :],
                                start=(mb == 0), stop=(mb == 1)),
                             reads=vm_b + [Ptb], writes=[pob])
                    S.op("dve", lambda e, po=po, dv=dv, hd=hd, sl=sl, r=r: e.tensor_tensor(
                            out=xn[:, 2 * hd + dv, sl], in0=po[:, :], in1=r, op=ALU.mult),
                         reads=[pob, rb], writes=[self.xn_b[2 * hd + dv][t]])
        self.out_proj(wo)

    def build(self):
        self.setup()
        ph = self.phases
        for l in range(2):
            if f"ffn1_{l}" in ph:
                self.rmsnorm_x(f"ffn1_{l}")
                self.ffn(self.w["ffn1_w_gu"], self.w["ffn1_w_down"], l)
            if f"mix_{l}" in ph:
                self.rmsnorm_x(f"mix_{l}")
                self.mixer_proj(l)
                if l == 0:
                    self.banded(0, [(j, j // 2, 0, j, (j // 2) * 64) for j in range(4)], 2, 1,
                                CB_MULTA, CF_NSLA, slopes(8), True)
                    self.stick(0, [(4 + j, 2 + j, 128 + j * 128, 4 + j) for j in range(4)])
                    if "noout" not in ph:
                        self.out_proj(self.w["ev_w_out"].ap()[0])
                else:
                    self.banded(1, [(j, j, j * 128, j, None) for j in range(8)], 17, 16,
                                CB_MULTC, CF_NSLC, slopes(16), False)
                    if "noout" not in ph:
                        self.out_proj(self.w["od_w_out"].ap()[0])
            if f"xa_{l}" in ph:
                self.rmsnorm_x(f"xa_{l}")
                self.xattn(l)
            if f"ffn2_{l}" in ph:
                self.rmsnorm_x(f"ffn2_{l}")
                self.ffn(self.w["ffn2_w_gu"], self.w["ffn2_w_down"], l)
        self.finish()
        self.S.emit(self.nc, self.es)


ALL_PHASES = tuple(f"{p}_{l}" for l in range(2) for p in ("ffn1", "mix", "xa", "ffn2"))


def build_nc(phases=ALL_PHASES, dbg=False):
    nc = bass.Bass("TRN2", target_bir_lowering=False)
    es = ExitStack()
    with es:
        p = Prog(nc, es, phases, dbg)
        p.build()
    nc.used_w = set(p.w.keys())
    return nc


def make_in_maps(inp, x_override=None, used_w=None, ncores=NCORES):
    gains = build_gains(inp)
    cb, cf = build_consts()
    x = np.asarray(inp["x"], np.float32) if x_override is None else x_override
    mem = np.asarray(inp["mem"], np.float32)
    shared = {k: np.ascontiguousarray(np.asarray(inp[k], np.float32)) for k in W_SHAPES
              if used_w is None or k in used_w}
    maps = []
    for core in range(ncores):
        b, h = core // 2, core % 2
        m = dict(shared)
        m["xT"] = np.ascontiguousarray(x[b, h * TOK:(h + 1) * TOK, :].T)
        m["memT"] = np.ascontiguousarray(mem[b].T)
        m["gains"] = gains
        m["cb"] = cb
        m["cf"] = cf
        m["flag"] = np.full((128, 128), float(h), np.float32)
        maps.append(m)
    return maps


def kernel(**inputs):
    nc = build_nc()
    maps = make_in_maps(inputs, used_w=nc.used_w)
    res = run_bass_kernel_spmd(nc, maps, core_ids=list(range(NCORES)))
    out = np.empty((4, 4096, D), np.float32)
    for core in range(NCORES):
        b, h = core // 2, core % 2
        out[b, h * TOK:(h + 1) * TOK, :] = np.asarray(res.results[core]["outT"]).T
    return out
```

```python
from contextlib import ExitStack
import numpy as np
import concourse.bass as bass
import concourse.mybir as mybir
from concourse.bass_utils import run_bass_kernel_spmd

F32 = mybir.dt.float32
BF16 = mybir.dt.bfloat16
AF = mybir.ActivationFunctionType
ALU = mybir.AluOpType

D = 1024
NCH = 8
TOK = 2048
TC = 512
NTC = TOK // TC
DFF = 2816
NFC = DFF // 128
EPS = 1e-6
NCORES = 8
NGROUPS = [4]


class Chan:
    def __init__(self, name):
        self.name = name
        self.sem = None
        self.cnt = 0


class Buf:
    def __init__(self, name, after=()):
        self.name = name
        self.writers = {}
        self.readers = {}
        self.chan = None
        for i, o in enumerate(after):
            self.readers[("init", i)] = o


class Op:
    __slots__ = ("eng", "fn", "deps", "dma", "chan", "val", "flag", "key", "inc")

    def __init__(self, eng, fn, dma=False, chan=None):
        self.eng = eng
        self.fn = fn
        self.deps = []
        self.dma = dma
        self.chan = chan
        self.val = 0
        self.flag = False
        self.key = None
        self.inc = 16


ENGS = ("pe", "act", "dve", "pool", "sp")


class Sched:
    def __init__(self):
        self.q = {e: [] for e in ENGS}
        self.chans = []
        self.nbuf = 0

    def buf(self, name, after=()):
        return Buf(name, after)

    def bufs(self, name, n, after=()):
        return [Buf(f"{name}{i}", after) for i in range(n)]

    def chan(self, name):
        c = Chan(f"{name}_{len(self.chans)}")
        self.chans.append(c)
        return c

    def _add(self, op, reads, writes):
        deps = {}

        def need(o):
            if o is op:
                return
            if (not o.dma) and (not op.dma) and o.eng == "pe" and op.eng == "pe":
                return
            deps[id(o)] = o

        for b in reads:
            for k, o in b.writers.items():
                need(o)
        for b in writes:
            for k, o in b.writers.items():
                if op.dma and o.dma:
                    continue
                need(o)
            for k, o in b.readers.items():
                need(o)
        op.deps = list(deps.values())
        for o in op.deps:
            o.flag = True
        for b in writes:
            if op.dma:
                b.writers = {k: o for k, o in b.writers.items() if o.dma}
                b.writers[op.key] = op
            else:
                b.writers = {op.key: op}
            b.readers = {}
        for b in reads:
            if b not in writes:
                b.readers[op.key] = op
        self.q[op.eng].append(op)
        return op

    def op(self, eng, fn, reads=(), writes=()):
        o = Op(eng, fn)
        o.key = eng
        return self._add(o, reads, writes)

    def dma(self, eng, out, in_, reads=(), writes=(), chan=None):
        if chan is None:
            b = writes[0]
            if b.chan is None:
                b.chan = self.chan("c_" + b.name)
            chan = b.chan
        o = Op(eng, lambda e: e.dma_start(out=out, in_=in_), dma=True, chan=chan)
        chan.cnt += 16
        o.val = chan.cnt
        o.key = ("ch", id(chan))
        return self._add(o, reads, writes)

    def coll(self, fn, reads=(), writes=()):
        ch = self.chan(f"cc{len(self.chans)}")
        o = Op("pool", fn, dma=True, chan=ch)
        o.inc = 1
        ch.cnt += 1
        o.val = ch.cnt
        o.key = ("ch", id(ch))
        return self._add(o, reads, writes)

    def barrier(self):
        res = []
        for e in ENGS:
            for o in reversed(self.q[e]):
                if not o.dma:
                    o.flag = True
                    res.append(o)
                    break
        seen = set()
        for e in ENGS:
            for o in reversed(self.q[e]):
                if o.dma and id(o.chan) not in seen:
                    seen.add(id(o.chan))
                    res.append(o)
        return res

    def emit(self, nc, es):
        for e in ENGS:
            c = 0
            for o in self.q[e]:
                if not o.dma and o.flag:
                    c += 1
                    o.val = c
        esem = {e: es.enter_context(nc.semaphore("sem_" + e)) for e in ENGS}
        for ch in self.chans:
            ch.sem = es.enter_context(nc.semaphore(ch.name))
        block = es.enter_context(nc.Block())

        def body_for(ename):
            def body(e):
                waited = {}
                for o in self.q[ename]:
                    need = {}
                    for d in o.deps:
                        sem = d.chan.sem if d.dma else esem[d.eng]
                        k = id(sem)
                        if waited.get(k, 0) >= d.val:
                            continue
                        if k not in need or need[k][1] < d.val:
                            need[k] = (sem, d.val)
                    for k, (sem, val) in need.items():
                        e.wait_ge(sem, val)
                        waited[k] = val
                    ins = o.fn(e)
                    if ins is None:
                        continue
                    if o.dma:
                        ins.then_inc(o.chan.sem, o.inc)
                    elif o.flag:
                        ins.then_inc(esem[ename], 1)
            return body

        block.tensor(body_for("pe"))
        block.scalar(body_for("act"))
        block.vector(body_for("dve"))
        block.gpsimd(body_for("pool"))
        block.sync(body_for("sp"))


def _col_chunks(v):
    v = np.asarray(v, np.float32)
    return np.ascontiguousarray(v.reshape(-1, 128).T)


def _gain_layout():
    cols = {}
    n = 0
    for l in range(2):
        for nm, w in ((f"ffn1_{l}", 8), (f"mix_{l}", 8), (f"xa_{l}", 8), (f"xam_{l}", 8), (f"ffn2_{l}", 8),
                      (f"xaq_{l}", 2), (f"xak_{l}", 2)):
            cols[nm] = n
            n += w
    for nm, w in (("evq", 1), ("evk", 1), ("odq", 1), ("odk", 1), ("sinks", 8)):
        cols[nm] = n
        n += w
    return cols, n


GCOL, NGCOL = _gain_layout()


def build_gains(inp):
    g = np.zeros((128, NGCOL), np.float32)
    for l in range(2):
        g[:, GCOL[f"ffn1_{l}"]:][:, :8] = _col_chunks(inp["ffn1_norm"][l])
        g[:, GCOL[f"mix_{l}"]:][:, :8] = _col_chunks(inp["mix_norm"][l])
        g[:, GCOL[f"xa_{l}"]:][:, :8] = _col_chunks(inp["xa_norm"][l])
        g[:, GCOL[f"xam_{l}"]:][:, :8] = _col_chunks(inp["xa_mem_norm"][l])
        g[:, GCOL[f"ffn2_{l}"]:][:, :8] = _col_chunks(inp["ffn2_norm"][l])
        g[:, GCOL[f"xaq_{l}"]:][:, :2] = _col_chunks(inp["xa_q_gain"][l])
        g[:, GCOL[f"xak_{l}"]:][:, :2] = _col_chunks(inp["xa_k_gain"][l])
    g[:, GCOL["evq"]] = np.tile(np.asarray(inp["ev_q_gain"][0], np.float32), 2)
    g[:, GCOL["evk"]] = np.tile(np.asarray(inp["ev_k_gain"][0], np.float32), 2)
    g[:, GCOL["odq"]] = np.tile(np.asarray(inp["od_q_gain"][0], np.float32), 2)
    g[:, GCOL["odk"]] = np.tile(np.asarray(inp["od_k_gain"][0], np.float32), 2)
    g[:, GCOL["sinks"]:GCOL["sinks"] + 8] = np.asarray(inp["ev_sinks"][0], np.float32)[None, :]
    return g


NTC_MASK = 23
CB_NEGU, CB_NEGI, CB_TRI7 = 0, 128, 256
CB_MULTC = CB_TRI7 + 7 * 128
CB_MULTA = CB_MULTC + NTC_MASK * 128
CB_N = CB_MULTA + 8 * 128
CF_D0, CF_D0C, CF_NSLC, CF_NSLA, CF_LNS = 0, 128, 256, 256 + 16 * 17, 256 + 16 * 17 + 8 * 2
CF_N = CF_LNS + 4


def slopes(n):
    return [2.0 ** (-8.0 * (i + 1) / n) for i in range(n)]


def build_consts():
    s = np.arange(128)[:, None]
    t = np.arange(128)[None, :]
    cb = np.zeros((128, CB_N), np.float32)
    cb[:, CB_NEGU:CB_NEGU + 128] = -(s >= t).astype(np.float32)
    cb[:, CB_NEGI:CB_NEGI + 128] = -(s == t).astype(np.float32)
    cb[:, CB_TRI7 + 3 * 128:CB_TRI7 + 4 * 128] = (s < t)
    cb[:, CB_TRI7 + 4 * 128:CB_TRI7 + 7 * 128] = 1.0
    for j in range(17):
        d = 128 * j + t - s
        m = ((d >= 0) & (d <= 128)).astype(np.float32) + ((d >= 0) & (d % 4 == 0) & (d <= 512)) \
            + ((d >= 0) & (d % 16 == 0) & (d <= 2048))
        cb[:, CB_MULTC + (3 + j) * 128:CB_MULTC + (4 + j) * 128] = m
    for j in range(2):
        d = 128 * j + t - s
        cb[:, CB_MULTA + (3 + j) * 128:CB_MULTA + (4 + j) * 128] = ((d >= 0) & (d <= 127))
    cf = np.zeros((128, CF_N), np.float32)
    cf[:, CF_D0:CF_D0 + 128] = t - s
    cf[:, CF_D0C:CF_D0C + 128] = np.maximum(t - s, 0)
    sc = slopes(16)
    for h in range(16):
        for j in range(17):
            cf[:, CF_NSLC + h * 17 + j] = -sc[h] * 128.0 * j
    sa = slopes(8)
    for h in range(8):
        for j in range(2):
            cf[:, CF_NSLA + h * 2 + j] = -sa[h] * 128.0 * j
    cf[:, CF_LNS + 0] = 0.0
    cf[:, CF_LNS + 1] = np.log(0.125)
    cf[:, CF_LNS + 2] = np.log(1.0 / 16.0)
    cf[:, CF_LNS + 3] = 1.0
    return cb, cf


W_SHAPES = {
    "ffn1_w_gu": [2, D, 2 * DFF], "ffn1_w_down": [2, DFF, D],
    "ffn2_w_gu": [2, D, 2 * DFF], "ffn2_w_down": [2, DFF, D],
    "ev_w_in": [1, D, 2304], "ev_w_out": [1, D, D], "od_w_in": [1, D, 3072], "od_w_out": [1, D, D],
    "xa_w_q": [2, D, D], "xa_w_kv": [2, D, 2 * D], "xa_w_o": [2, D, D],
}
UNIT = 2048
NUNITS = 14


class Ring:
    def __init__(self, S, tile, n):
        self.tile = tile
        self.n = n
        self.bufs = [S.buf(f"ring{i}") for i in range(n)]
        self.pos = 0

    def alloc(self, k=1):
        if self.pos + k > self.n:
            self.pos = 0
        u = self.pos
        self.pos += k
        return self.tile[:, u * UNIT:(u + k) * UNIT], self.bufs[u:u + k]


class Prog:
    def __init__(self, nc, es, phases, dbg=False):
        self.nc = nc
        self.es = es
        self.S = S = Sched()
        self.phases = phases
        dt = nc.dram_tensor
        self.xT_d = dt("xT", [D, TOK], F32, kind="ExternalInput")
        self.memT_d = dt("memT", [D, 256], F32, kind="ExternalInput")
        self.gains_d = dt("gains", [128, NGCOL], F32, kind="ExternalInput")
        self.cb_d = dt("cb", [128, CB_N], F32, kind="ExternalInput")
        self.cf_d = dt("cf", [128, CF_N], F32, kind="ExternalInput")
        self.flag_d = dt("flag", [128, 128], F32, kind="ExternalInput")
        class _LazyW(dict):
            def __missing__(d, nm):
                d[nm] = dt(nm, W_SHAPES[nm], F32, kind="ExternalInput")
                return d[nm]
        self.w = _LazyW()
        self.dbg = dbg
        self.out_d = dt("outT", [D, TOK], F32, kind="ExternalOutput")
        self.out_buf = S.buf("outT")
        self.scr = {}
        dk = dict(kind="ExternalOutput") if dbg else {}
        if dbg:
            self.xn_d = dt("xn_dump", [D, TOK], BF16, kind="ExternalOutput")
        for l, (nk, vc) in enumerate(((6, 640), (8, 1024))):
            self.scr[l] = dict(
                q=dt(f"q{l}_d", [8 * 128, TOK], BF16, **dk), q_b=S.buf(f"q{l}_d"),
                kloc=[dt(f"kloc{l}_{c}", [128, TOK], BF16) for c in range(nk)],
                kloc_b=[S.buf(f"kloc{l}_{c}") for c in range(nk)],
                kall=[dt(f"kall{l}_{c}", [256, TOK], BF16) for c in range(nk)],
                kall_b=[S.buf(f"kall{l}_{c}") for c in range(nk)],
                vloc=[dt(f"vloc{l}_{j}", [256, vc], BF16) for j in range(8)],
                vloc_b=[S.buf(f"vloc{l}_{j}") for j in range(8)],
                vall=[dt(f"vall{l}_{j}", [512, vc], BF16) for j in range(8)],
                vall_b=[S.buf(f"vall{l}_{j}") for j in range(8)],
                nk=nk, vc=vc)

        sb = lambda name, shape, dtype: es.enter_context(nc.sbuf_tensor(name, shape, dtype))
        ps = lambda name: es.enter_context(nc.psum_tensor(name, [128, 512], F32))
        self.xT = sb("xT_sb", [128, NCH, TOK], F32)
        self.xT_b = [[S.buf(f"xT{c}_{t}") for t in range(NTC)] for c in range(NCH)]
        self.xn = sb("xn_sb", [128, NCH, TOK], BF16)
        self.xn_b = [[S.buf(f"xn{c}_{t}") for t in range(NTC)] for c in range(NCH)]
        self.gains = sb("gains_sb", [128, NGCOL], F32)
        self.cb = sb("cb_sb", [128, CB_N], BF16)
        self.cf = sb("cf_sb", [128, CF_N], F32)
        self.const_b = S.buf("consts")
        self.onesm = sb("onesm", [128, 128], BF16)
        self.ones256 = sb("ones256", [128, 128], BF16)
        self.ones1 = sb("ones1", [128, 128], BF16)
        self.bd64 = sb("bd64", [128, 128], BF16)
        self.flagt = sb("flagt", [128, 128], F32)
        self.flagb = sb("flagb", [128, 128], BF16)
        self.epsc = sb("epsc", [128, 2], F32)
        self.exps = sb("exps", [128, 8], F32)
        self.sq = [sb(f"sq{i}", [128, TC], BF16) for i in range(4)]
        self.sq_b = S.bufs("sq", 4)
        self.rstd = [sb(f"rstd{i}", [128, TC], F32) for i in range(2)]
        self.rstd_b = S.bufs("rstd", 2)
        self.ring_t = sb("ring", [128, NUNITS * UNIT], BF16)
        self.ring = Ring(S, self.ring_t, NUNITS)
        self.sf = sb("scr_f32", [128, 3072], F32)
        self.sbf = sb("scr_bf", [128, 8192], BF16)
        self.ps = [ps(f"ps{i}") for i in range(8)]
        self.ps_b = S.bufs("ps", 8)
        self.norm_i = 0
        self.sq_i = 0

    def gcol(self, name, i=0):
        c = GCOL[name] + i
        return self.gains[:, c:c + 1]

    def setup(self):
        S = self.S
        cb_ = self.const_b
        S.dma("sp", self.gains[:, :], self.gains_d[:, :], writes=[cb_])
        S.dma("sp", self.cf[:, :], self.cf_d[:, :], writes=[cb_])
        S.dma("sp", self.flagt[:, :], self.flag_d[:, :], writes=[cb_])
        for i in range(0, CB_N, 1024):
            j = min(CB_N, i + 1024)
            S.dma("pool", self.cb[:, i:j], self.cb_d[:, i:j], writes=[cb_])
        S.op("pool", lambda e: e.memset(self.onesm[:, :], 1.0 / D), writes=[cb_])
        S.op("pool", lambda e: e.memset(self.ones256[:, :], 1.0 / 256), writes=[cb_])
        S.op("pool", lambda e: e.memset(self.ones1[:, :], 1.0), writes=[cb_])
        S.op("pool", lambda e: e.memset(self.bd64[:, :], 0.0), writes=[cb_])
        S.op("pool", lambda e: e.memset(self.bd64[0:64, 0:64], 1.0 / 64), writes=[cb_])
        S.op("pool", lambda e: e.memset(self.bd64[64:128, 64:128], 1.0 / 64), writes=[cb_])
        S.op("pool", lambda e: e.memset(self.epsc[:, :], EPS), writes=[cb_])
        S.op("pool", lambda e: e.memset(self.sbf[:, :], 0.0), writes=[cb_])
        S.op("dve", lambda e: e.tensor_copy(out=self.flagb[:, :], in_=self.flagt[:, :]), reads=[cb_], writes=[cb_])
        sc = GCOL["sinks"]
        S.op("act", lambda e: e.activation(out=self.exps[:, :], in_=self.gains[:, sc:sc + 8], func=AF.Exp),
             reads=[cb_], writes=[cb_])
        xv = self.xT_d.ap().rearrange("(c p) t -> p c t", p=128)
        for t in range(NTC):
            for c in range(NCH):
                S.dma("sp", self.xT[:, c, t * TC:(t + 1) * TC], xv[:, c, t * TC:(t + 1) * TC],
                      writes=[self.xT_b[c][t]])

    def finish(self):
        S = self.S
        ov = self.out_d.ap().rearrange("(c p) t -> p c t", p=128)
        for t in range(NTC):
            for c in range(NCH):
                S.dma("sp", ov[:, c, t * TC:(t + 1) * TC], self.xT[:, c, t * TC:(t + 1) * TC],
                      reads=[self.xT_b[c][t]], writes=[self.out_buf])
        if self.dbg:
            xv = self.xn_d.ap().rearrange("(c p) t -> p c t", p=128)
            for t in range(NTC):
                for c in range(NCH):
                    S.dma("sp", xv[:, c, t * TC:(t + 1) * TC], self.xn[:, c, t * TC:(t + 1) * TC],
                          reads=[self.xn_b[c][t]], writes=[self.out_buf])
        S.op("sp", lambda e: None, reads=[self.out_buf])

    def rstd_from(self, srcs, ones, n, lnbias, out_rs, out_rsb, reads_extra=()):
        S = self.S
        i = self.norm_i % 2
        self.norm_i += 1
        pss, pssb = self.ps[6 + i], self.ps_b[6 + i]
        k = len(srcs)
        for c, (ap, bl) in enumerate(srcs):
            q = self.sq_i % 4
            self.sq_i += 1
            sq, sqb = self.sq[q], self.sq_b[q]
            S.op("act", lambda e, sq=sq, ap=ap: e.activation(out=sq[:, 0:n], in_=ap, func=AF.Square),
                 reads=list(bl), writes=[sqb])
            S.op("pe", lambda e, sq=sq, pss=pss, c=c: e.matmul(pss[:, 0:n], lhsT=ones[:, :], rhs=sq[:, 0:n],
                                                             start=(c == 0), stop=(c == k - 1)),
                 reads=[self.const_b, sqb], writes=[pssb])
        S.op("act", lambda e: e.activation(out=out_rs, in_=pss[:, 0:n], func=AF.Ln, bias=self.epsc[:, 0:1]),
             reads=[pssb, self.const_b], writes=[out_rsb])
        S.op("act", lambda e: e.activation(out=out_rs, in_=out_rs, func=AF.Exp, scale=-0.5, bias=lnbias),
             reads=[out_rsb, self.const_b], writes=[out_rsb])

    def lnb(self, i):
        return self.cf[:, CF_LNS + i:CF_LNS + i + 1]

    def rmsnorm_x(self, gname):
        S = self.S
        xT, xn = self.xT, self.xn
        for t in range(NTC):
            sl = slice(t * TC, (t + 1) * TC)
            i = self.norm_i % 2
            rs, rsb = self.rstd[i], self.rstd_b[i]
            self.rstd_from([(xT[:, c, sl], [self.xT_b[c][t]]) for c in range(NCH)], self.onesm, TC,
                           self.lnb(0), rs[:, :], rsb)
            for c in range(NCH):
                S.op("dve", lambda e, c=c, rs=rs, sl=sl: e.scalar_tensor_tensor(
                        out=xn[:, c, sl], in0=xT[:, c, sl], scalar=self.gcol(gname, c),
                        in1=rs[:, :], op0=ALU.mult, op1=ALU.mult),
                     reads=[self.xT_b[c][t], rsb, self.const_b], writes=[self.xn_b[c][t]])

    def load_cols(self, w_ap2d, c0, ncols=256):
        ap, bufs = self.ring.alloc(1)
        v = ap.rearrange("p (c f) -> p c f", c=NCH)
        src = w_ap2d.rearrange("(c p) f -> p c f", p=128)
        self.S.dma("pool", v[:, :, 0:ncols], src[:, :, c0:c0 + ncols], writes=bufs)
        return v, bufs

    def load_rows(self, w_ap2d, r0):
        ap, bufs = self.ring.alloc(1)
        v = ap.rearrange("p (j m) -> p j m", j=2)
        src = w_ap2d[r0:r0 + 256, :].rearrange("(j p) m -> p j m", p=128)
        self.S.dma("pool", v[:, :, :], src, writes=bufs)
        return v, bufs

    def ffn(self, wgu_d, wdn_d, layer):
        S = self.S
        after = S.barrier()
        sg = [self.sf[:, i * TC:(i + 1) * TC] for i in range(2)]
        sg_b = S.bufs("sg", 2, after)
        act = [self.sbf[:, i * 2 * TC:(i + 1) * 2 * TC].rearrange("p (j t) -> p j t", j=2) for i in range(3)]
        act_b = [[S.buf(f"act{i}_{j}", after) for j in range(2)] for i in range(3)]
        NG = NFC // 2
        gu2 = wgu_d.ap()[layer]
        dn2 = wdn_d.ap()[layer]
        W = {}

        def load(g):
            f0 = g * 256
            W[g] = (self.load_cols(gu2, f0), self.load_cols(gu2, DFF + f0), self.load_rows(dn2, f0))

        items = [(g, t) for g in range(NG) for t in range(NTC)]
        xn = self.xn

        def emit_gu(i):
            g, t = items[i]
            sl = slice(t * TC, (t + 1) * TC)
            a = i % 3
            for j in range(2):
                k = (2 * i + j) % 2
                pg, pgb = self.ps[k], self.ps_b[k]
                pu, pub = self.ps[2 + k], self.ps_b[2 + k]
                for (pp, ppb, gi) in ((pg, pgb, 0), (pu, pub, 1)):
                    wv, wb = W[g][gi]
                    for c in range(NCH):
                        S.op("pe", lambda e, pp=pp, wv=wv, c=c, j=j, sl=sl: e.matmul(
                                pp[:, :], lhsT=wv[:, c, j * 128:(j + 1) * 128], rhs=xn[:, c, sl],
                                start=(c == 0), stop=(c == NCH - 1)),
                             reads=wb + [self.xn_b[c][t]], writes=[ppb])
                S.op("act", lambda e, k=k, pg=pg: e.activation(out=sg[k], in_=pg[:, :], func=AF.Silu),
                     reads=[pgb], writes=[sg_b[k]])
                S.op("dve", lambda e, k=k, pu=pu, a=a, j=j: e.tensor_tensor(
                        out=act[a][:, j, :], in0=pu[:, :], in1=sg[k], op=ALU.mult),
                     reads=[pub, sg_b[k]], writes=[act_b[a][j]])

        def emit_down(i):
            g, t = items[i]
            sl = slice(t * TC, (t + 1) * TC)
            a = i % 3
            wv, wb = W[g][2]
            for m in range(NCH):
                k = 4 + (m % 2)
                py, pyb = self.ps[k], self.ps_b[k]
                for j in range(2):
                    S.op("pe", lambda e, py=py, j=j, m=m, wv=wv, a=a: e.matmul(
                            py[:, :], lhsT=wv[:, j, m * 128:(m + 1) * 128], rhs=act[a][:, j, :],
                            start=(j == 0), stop=(j == 1)),
                         reads=wb + [act_b[a][j]], writes=[pyb])
                S.op("dve", lambda e, py=py, m=m, sl=sl: e.scalar_tensor_tensor(
                        out=self.xT[:, m, sl], in0=py[:, :], scalar=0.5, in1=self.xT[:, m, sl],
                        op0=ALU.mult, op1=ALU.add),
                     reads=[pyb, self.xT_b[m][t]], writes=[self.xT_b[m][t]])

        PF = 3
        for g in range(min(PF, NG)):
            load(g)
        n = len(items)
        for i in range(n + 1):
            if i < n:
                g, t = items[i]
                if t == 1 and g + PF < NG:
                    load(g + PF)
                emit_gu(i)
            if i >= 1:
                emit_down(i - 1)

    def out_proj(self, w2d):
        S = self.S
        units = [self.load_rows(w2d, r * 256) for r in range(4)]
        for t in range(NTC):
            sl = slice(t * TC, (t + 1) * TC)
            for m in range(NCH):
                k = 4 + (m % 2)
                py, pyb = self.ps[k], self.ps_b[k]
                for oc in range(NCH):
                    wv, wb = units[oc // 2]
                    S.op("pe", lambda e, py=py, wv=wv, oc=oc, m=m, sl=sl: e.matmul(
                            py[:, :], lhsT=wv[:, oc % 2, m * 128:(m + 1) * 128], rhs=self.xn[:, oc, sl],
                            start=(oc == 0), stop=(oc == NCH - 1)),
                         reads=wb + [self.xn_b[oc][t]], writes=[pyb])
                S.op("dve", lambda e, py=py, m=m, sl=sl: e.tensor_tensor(
                        out=self.xT[:, m, sl], in0=py[:, :], in1=self.xT[:, m, sl], op=ALU.add),
                     reads=[pyb, self.xT_b[m][t]], writes=[self.xT_b[m][t]])

    def mixer_proj(self, layer):
        S = self.S
        after = S.barrier()
        sc = self.scr[layer]
        even = layer == 0
        w2d = (self.w["ev_w_in"] if even else self.w["od_w_in"]).ap()[0]
        nun = 9 if even else 12
        stg = [self.sbf[:, i * TC:(i + 1) * TC] for i in range(4)]
        stg_b = S.bufs("stg", 4, after)
        stg_ch = [S.chan(f"stgch{layer}_{i}") for i in range(4)]
        sqh = [self.sbf[:, (4 + i) * TC:(5 + i) * TC] for i in range(2)]
        sqh_b = S.bufs("sqh", 2, after)
        rr = [self.sf[:, i * TC:(i + 1) * TC] for i in range(4)]
        rr_b = S.bufs("rr", 4, after)
        vst = [self.sbf[:, 3072 + i * 1024:3072 + (i + 1) * 1024] for i in range(2)]
        vst_b = S.bufs("vst", 2, after)
        vst_ch = [S.chan(f"vstch{layer}_{i}") for i in range(2)]
        cnt = {"s": 0, "q": 0, "v": 0}
        gq, gk = ("evq", "evk") if even else ("odq", "odk")
        jobs = []
        if even:
            for j in range(4):
                jobs.append((j // 2, (j % 2) * 128, "norm", gq, 1, sc["q"], sc["q_b"], j))
            for g in range(2):
                jobs.append((2, ("dup", g * 64), "norm", gk, 0, sc["kloc"], sc["kloc_b"], g))
            for j in range(4):
                jobs.append((3 + j // 2, (j % 2) * 128, "scale", None, 0.125, sc["q"], sc["q_b"], 4 + j))
            for j in range(4):
                jobs.append((5 + j // 2, (j % 2) * 128, "scale", None, 1.0, sc["kloc"], sc["kloc_b"], 2 + j))
            vjobs = [[(2, 128, 128)], [(7, 0, 256), (8, 0, 256)]]
        else:
            for j in range(8):
                jobs.append((j // 2, (j % 2) * 128, "norm", gq, 1, sc["q"], sc["q_b"], j))
            for j in range(8):
                jobs.append((4 + j // 2, (j % 2) * 128, "norm", gk, 0, sc["kloc"], sc["kloc_b"], j))
            vjobs = [[(8, 0, 256), (9, 0, 256)], [(10, 0, 256), (11, 0, 256)]]
        U = {}

        def need(u):
            if u not in U:
                U[u] = self.load_cols(w2d, u * 256)
            return U[u]

        order = [u for grp in vjobs for (u, _, _) in grp] + [jb[0] for jb in sorted(jobs, key=lambda jb: 0 if isinstance(jb[5], list) else 1)]
        for u in order:
            need(u)
        xn = self.xn
        vc = sc["vc"]
        for nb in range(TOK // 128):
            tb, tsl = nb // 4, slice(nb * 128, (nb + 1) * 128)
            vi = cnt["v"] % 2
            cnt["v"] += 1
            col = 0
            for bi, grp in enumerate(vjobs):
                pv, pvb = self.ps[2 + 2 * vi + bi], self.ps_b[2 + 2 * vi + bi]
                pc = 0
                for (u, c0, n) in grp:
                    wv, wb = need(u)
                    for c in range(NCH):
                        S.op("pe", lambda e, pv=pv, wv=wv, c=c, c0=c0, n=n, pc=pc, tsl=tsl: e.matmul(
                                pv[:, pc:pc + n], lhsT=xn[:, c, tsl], rhs=wv[:, c, c0:c0 + n],
                                start=(c == 0), stop=(c == NCH - 1)),
                             reads=wb + [self.xn_b[c][tb]], writes=[pvb])
                    pc += n
                eng = "act" if bi == 0 else "dve"
                if eng == "act":
                    S.op("act", lambda e, pv=pv, vi=vi, col=col, pc=pc: e.activation(
                            out=vst[vi][:, col:col + pc], in_=pv[:, 0:pc], func=AF.Copy),
                         reads=[pvb], writes=[vst_b[vi]])
                else:
                    S.op("dve", lambda e, pv=pv, vi=vi, col=col, pc=pc: e.tensor_copy(
                            out=vst[vi][:, col:col + pc], in_=pv[:, 0:pc]),
                         reads=[pvb], writes=[vst_b[vi]])
                col += pc
            jv = nb // 2
            S.dma("sp", sc["vloc"][jv][(nb % 2) * 128:(nb % 2 + 1) * 128, :], vst[vi][:, 0:vc], reads=[vst_b[vi]],
                  writes=[sc["vloc_b"][jv]], chan=vst_ch[vi])
            if nb % 2 == 1:
                self.gather(sc["vloc"][jv], sc["vloc_b"][jv], sc["vall"][jv], sc["vall_b"][jv])
        jobs.sort(key=lambda jb: 0 if isinstance(jb[5], list) else 1)
        for (u, off, kind, gname, par, dten, dbuf, drow) in jobs:
            wv, wb = need(u)
            for t in range(NTC):
                sl = slice(t * TC, (t + 1) * TC)
                k = cnt["q"] % 4
                cnt["q"] += 1
                pq, pqb = self.ps[k], self.ps_b[k]
                if isinstance(off, tuple):
                    c0 = off[1]
                    for half in range(2):
                        for c in range(NCH):
                            S.op("pe", lambda e, pq=pq, wv=wv, c=c, c0=c0, half=half, sl=sl: e.matmul(
                                    pq[half * 64:(half + 1) * 64, :], lhsT=wv[:, c, c0:c0 + 64], rhs=xn[:, c, sl],
                                    start=(c == 0), stop=(c == NCH - 1)),
                                 reads=wb + [self.xn_b[c][t]], writes=[pqb])
                else:
                    for c in range(NCH):
                        S.op("pe", lambda e, pq=pq, wv=wv, c=c, off=off, sl=sl: e.matmul(
                                pq[:, :], lhsT=wv[:, c, off:off + 128], rhs=xn[:, c, sl],
                                start=(c == 0), stop=(c == NCH - 1)),
                             reads=wb + [self.xn_b[c][t]], writes=[pqb])
                si = cnt["s"] % 4
                cnt["s"] += 1
                if kind == "norm":
                    r, rb = rr[k], rr_b[k]
                    self.rstd_from([(pq[:, :], [pqb])], self.bd64, TC, self.lnb(par), r, rb)
                    S.op("dve", lambda e, si=si, pq=pq, r=r, gname=gname: e.scalar_tensor_tensor(
                            out=stg[si], in0=pq[:, :], scalar=self.gcol(gname), in1=r, op0=ALU.mult, op1=ALU.mult),
                         reads=[pqb, rb, self.const_b], writes=[stg_b[si]])
                else:
                    S.op("act", lambda e, si=si, pq=pq, par=par: e.activation(
                            out=stg[si], in_=pq[:, :], func=AF.Copy, scale=float(par)),
                         reads=[pqb], writes=[stg_b[si]])
                if isinstance(dten, list):
                    S.dma("sp", dten[drow][:, sl], stg[si], reads=[stg_b[si]], writes=[dbuf[drow]], chan=stg_ch[si])
                    if t == NTC - 1:
                        self.gather(sc["kloc"][drow], sc["kloc_b"][drow], sc["kall"][drow], sc["kall_b"][drow])
                else:
                    S.dma("sp", dten[drow * 128:(drow + 1) * 128, sl], stg[si], reads=[stg_b[si]], writes=[dbuf],
                          chan=stg_ch[si])

    def gather(self, src, src_b, dst, dst_b):
        S = self.S
        import os
        if os.environ.get("KDBG_NOCOLL"):
            n = src.shape[0]
            S.dma("sp", dst[0:n, :], src[:, :], reads=[src_b], writes=[dst_b])
            return
        groups = [[2 * i, 2 * i + 1] for i in range(NGROUPS[0])]
        S.coll(lambda e: e.collective_compute("AllGather", ALU.bypass, replica_groups=groups,
                                              ins=[src.ap().opt()], outs=[dst.ap().opt()]),
               reads=[src_b], writes=[dst_b])

    def load_attn_tiles(self, layer, qrow, krow, vcol, nprev):
        S = self.S
        sc = self.scr[layer]
        nk = sc["nk"]
        qh = []
        for hh in range(2):
            qa, qb_ = self.ring.alloc(1)
            lo, hi = hh * 64, (hh + 1) * 64
            zl, zh = (1 - hh) * 64, (2 - hh) * 64
            S.dma("sp", qa[lo:hi, :], sc["q"][qrow * 128 + lo:qrow * 128 + hi, :], reads=[sc["q_b"]], writes=qb_)
            S.op("pool", lambda e, qa=qa, zl=zl, zh=zh: e.memset(qa[zl:zh, :], 0.0), writes=qb_)
            qh.append((qa, qb_))
        ka, kb_ = self.ring.alloc(2)
        npc = nprev * 128
        S.dma("sp", ka[:, 0:npc], sc["kall"][krow][0:128, TOK - npc:TOK], reads=[sc["kall_b"][krow]], writes=kb_)
        S.dma("sp", ka[:, npc:npc + TOK], sc["kloc"][krow][:, :], reads=[sc["kloc_b"][krow]], writes=kb_)
        va, vb_ = self.ring.alloc(2)
        v3 = va.rearrange("p (n c) -> p n c", c=128)
        if nprev == 16:
            for j in range(8):
                S.dma("sp", v3[:, 2 * j:2 * j + 2, :],
                      sc["vall"][j][0:256, vcol:vcol + 128].rearrange("(n p) c -> p n c", p=128),
                      reads=[sc["vall_b"][j]], writes=vb_)
        else:
            assert nprev == 1
            S.dma("sp", v3[:, 0:1, :], sc["vall"][7][128:256, vcol:vcol + 128].rearrange("(n p) c -> p n c", p=128),
                  reads=[sc["vall_b"][7]], writes=vb_)
        for j in range(8):
            S.dma("sp", v3[:, nprev + 2 * j:nprev + 2 * j + 2, :],
                  sc["vloc"][j][:, vcol:vcol + 128].rearrange("(n p) c -> p n c", p=128),
                  reads=[sc["vloc_b"][j]], writes=vb_)
        S.op("dve", lambda e: e.tensor_scalar(out=va[:, 0:npc], in0=va[:, 0:npc], scalar1=self.flagt[:, 0:1],
                                              scalar2=1.0, op0=ALU.mult, op1=ALU.mult),
             reads=vb_ + [self.const_b], writes=vb_)
        return qh, (ka, kb_), (v3, vb_)

    def banded(self, layer, pairs, ND, nprev, mult_off, nsl_off, sl_list, sink):
        S = self.S
        after = S.barrier()
        NT = ND + 6
        msk = [self.sbf[:, i * 3072:i * 3072 + NT * 128] for i in range(2)]
        msk_b = S.bufs("msk", 2, after)
        P = [self.sbf[:, 6144 + i * TC:6144 + (i + 1) * TC] for i in range(4)]
        P_b = S.bufs("P", 4, after)
        tmp = [self.sf[:, i * TC:(i + 1) * TC] for i in range(2)]
        tmp_b = S.bufs("tmp", 2, after)
        cb_ = self.const_b
        hcount = 0
        ucount = 0
        ccount = 0
        for pi, (qrow, krow, vcol, oc, vh) in enumerate(pairs):
            qh, (ka, kb_), (v3, vb_) = self.load_attn_tiles(layer, qrow, krow, vcol, nprev)
            for hh in range(2):
                qa, qb_ = qh[hh]
                h = 2 * pi + hh
                hp = slice(hh * 64, (hh + 1) * 64)
                vs = hp if vh is None else slice(vh, vh + 64)
                mi = hcount % 2
                hcount += 1
                m, mb = msk[mi], msk_b[mi]
                S.op("pool", lambda e, m=m: e.memset(m[:, 0:384], 0.0), writes=[mb])
                S.op("pool", lambda e, m=m: e.memset(m[:, (ND + 3) * 128:(ND + 6) * 128], 0.0), writes=[mb])
                for j in range(ND):
                    src = CF_D0C if j == 0 else CF_D0
                    S.op("act", lambda e, m=m, j=j, src=src, h=h: e.activation(
                            out=m[:, (3 + j) * 128:(4 + j) * 128], in_=self.cf[:, src:src + 128], func=AF.Exp,
                            scale=float(-sl_list[h]), bias=self.cf[:, nsl_off + h * ND + j:nsl_off + h * ND + j + 1]),
                         reads=[cb_], writes=[mb])
                S.op("dve", lambda e, m=m: e.tensor_tensor(out=m, in0=m, in1=self.cb[:, mult_off:mult_off + NT * 128],
                                                          op=ALU.mult),
                     reads=[cb_, mb], writes=[mb])
                for t in range(NTC):
                    ucount = self._banded_chunk(t, ccount, ucount, nprev, ND, hp, vs, h, oc, sink, m, mb, P, P_b, tmp, tmp_b,
                                                qa, qb_, ka, kb_, v3, vb_)
                    ccount += 1

    def _banded_chunk(self, t, ccount, ucount, nprev, ND, hp, vs, h, oc, sink, m, mb, P, P_b, tmp, tmp_b,
                      qa, qb_, ka, kb_, v3, vb_):
        S = self.S
        cb_ = self.const_b
        full_v = (vs == hp)
        ci = ccount % 2
        po, pob = self.ps[2 + ci], self.ps_b[2 + ci]
        pd, pdb = self.ps[4 + ci], self.ps_b[4 + ci]
        Lq0 = nprev + 4 * t
        units = [(L, Lq0 - L) for L in range(max(0, Lq0 - (ND - 1)), Lq0 + 4)]
        n = len(units)
        qsl = slice(t * TC, (t + 1) * TC)

        ZB = (0, 1, 6, 7)

        def cr(j0):
            lo = max(0, -j0)
            hi = min(3, ND - 1 - j0)
            return lo * 128, (hi + 1) * 128

        def st_z(i):
            L, j0 = units[i]
            c0, c1 = cr(j0)
            k = ZB[(ucount + i) % 4]
            S.op("pe", lambda e, k=k, L=L, c0=c0, c1=c1: e.matmul(
                    self.ps[k][:, c0:c1], lhsT=ka[:, L * 128:(L + 1) * 128],
                    rhs=qa[:, t * TC + c0:t * TC + c1], start=True, stop=True),
                 reads=kb_ + qb_, writes=[self.ps_b[k]])

        def st_p(i):
            L, j0 = units[i]
            c0, c1 = cr(j0)
            k = ZB[(ucount + i) % 4]
            p = (ucount + i) % 4
            S.op("act", lambda e, k=k, p=p, c0=c0, c1=c1: e.activation(out=P[p][:, c0:c1], in_=self.ps[k][:, c0:c1], func=AF.Exp),
                 reads=[self.ps_b[k]], writes=[P_b[p]])
            S.op("dve", lambda e, p=p, j0=j0, c0=c0, c1=c1: e.tensor_tensor(
                    out=P[p][:, c0:c1], in0=P[p][:, c0:c1], in1=m[:, (j0 + 3) * 128 + c0:(j0 + 3) * 128 + c1], op=ALU.mult),
                 reads=[P_b[p], mb], writes=[P_b[p]])

        def st_o(i):
            L, j0 = units[i]
            c0, c1 = cr(j0)
            p = (ucount + i) % 4
            if full_v:
                S.op("pe", lambda e, p=p, L=L, i=i, c0=c0, c1=c1: e.matmul(po[:, c0:c1], lhsT=v3[:, L, :], rhs=P[p][:, c0:c1],
                                                             start=(i == 0), stop=(i == n - 1)),
                     reads=vb_ + [P_b[p]], writes=[pob])
            else:
                S.op("pe", lambda e, p=p, L=L, i=i, c0=c0, c1=c1: e.matmul(po[hp, c0:c1], lhsT=v3[:, L, vs], rhs=P[p][:, c0:c1],
                                                             start=(i == 0), stop=(i == n - 1)),
                     reads=vb_ + [P_b[p]], writes=[pob])
            ones = self.flagb[:, :] if L < nprev else self.ones1[:, :]
            S.op("pe", lambda e, p=p, ones=ones, i=i, c0=c0, c1=c1: e.matmul(pd[:, c0:c1], lhsT=ones, rhs=P[p][:, c0:c1],
                                                               start=(i == 0), stop=(i == n - 1)),
                 reads=[cb_, P_b[p]], writes=[pdb])

        for step in range(n + 3):
            if step < n:
                st_z(step)
            if 0 <= step - 1 < n:
                st_p(step - 1)
            if 0 <= step - 3 < n:
                st_o(step - 3)
        ti = ccount % 2
        tm, tmb = tmp[ti], tmp_b[ti]
        if sink:
            S.op("act", lambda e: e.activation(out=tm[hp, :], in_=pd[hp, :], func=AF.Ln, bias=self.exps[hp, h:h + 1]),
                 reads=[pdb, cb_], writes=[tmb])
        else:
            S.op("act", lambda e: e.activation(out=tm[hp, :], in_=pd[hp, :], func=AF.Ln),
                 reads=[pdb], writes=[tmb])
        S.op("act", lambda e: e.activation(out=tm[hp, :], in_=tm[hp, :], func=AF.Exp, scale=-1.0),
             reads=[tmb], writes=[tmb])
        S.op("dve", lambda e: e.tensor_tensor(out=self.xn[hp, oc, qsl], in0=po[hp, :], in1=tm[hp, :], op=ALU.mult),
             reads=[pob, tmb], writes=[self.xn_b[oc][t]])
        return ucount + n

    def stick(self, layer, pairs):
        S = self.S
        after = S.barrier()
        nprev = 16
        ef = [self.sf[:, i * TC:(i + 1) * TC] for i in range(2)]
        ef_b = S.bufs("ef", 2, after)
        sp = [self.sbf[:, i * TC:(i + 1) * TC] for i in range(2)]
        sp_b = S.bufs("sp", 2, after)
        A = [self.sbf[:, (2 + i) * TC:(3 + i) * TC] for i in range(2)]
        A_b = S.bufs("A", 2, after)
        Rs = [self.sbf[:, (4 + i) * TC:(5 + i) * TC] for i in range(2)]
        Rs_b = S.bufs("Rs", 2, after)
        cb_ = self.const_b
        negU = self.cb[:, CB_NEGU:CB_NEGU + 128]
        negI = self.cb[:, CB_NEGI:CB_NEGI + 128]
        one_c = self.cf[:, CF_LNS + 3:CF_LNS + 4]
        ccount = 0
        ucount = 0
        for pi, (qrow, krow, vcol, oc) in enumerate(pairs):
            qh, (ka, kb_), (v3, vb_) = self.load_attn_tiles(layer, qrow, krow, vcol, nprev)
            for hh in range(2):
                qa, qb_ = qh[hh]
                hp = slice(hh * 64, (hh + 1) * 64)
                for t in range(NTC):
                    ucount = self._stick_chunk(t, ccount, ucount, nprev, hp, oc, ef, ef_b, sp, sp_b, A, A_b, Rs, Rs_b,
                                               qa, qb_, ka, kb_, v3, vb_)
                    ccount += 1

    def _stick_chunk(self, t, ccount, ucount, nprev, hp, oc, ef, ef_b, sp, sp_b, A, A_b, Rs, Rs_b,
                     qa, qb_, ka, kb_, v3, vb_):
        S = self.S
        cb_ = self.const_b
        negU = self.cb[:, CB_NEGU:CB_NEGU + 128]
        negI = self.cb[:, CB_NEGI:CB_NEGI + 128]
        one_c = self.cf[:, CF_LNS + 3:CF_LNS + 4]
        ci = ccount % 2
        pR, pRb = self.ps[4 + ci], self.ps_b[4 + ci]
        pO, pOb = self.ps[6 + ci], self.ps_b[6 + ci]
        qsl = slice(t * TC, (t + 1) * TC)
        Ldiag0 = nprev + 4 * t
        Ls = list(range(Ldiag0 + 3, -1, -1))
        n = len(Ls)

        def c0of(i):
            L = Ls[i]
            return (L - Ldiag0) * 128 if L >= Ldiag0 else 0

        tri = self.cb[:, CB_TRI7 + 3 * 128:CB_TRI7 + 4 * 128]

        def s0(i):
            L = Ls[i]
            c0 = c0of(i)
            k = (ucount + i) % 2
            S.op("pe", lambda e, k=k, L=L, c0=c0: e.matmul(self.ps[k][:, c0:], lhsT=ka[:, L * 128:(L + 1) * 128],
                                                          rhs=qa[:, t * TC + c0:(t + 1) * TC], start=True, stop=True),
                 reads=kb_ + qb_, writes=[self.ps_b[k]])

        def s1(i):
            L = Ls[i]
            c0 = c0of(i)
            k = (ucount + i) % 2
            S.op("act", lambda e, k=k, c0=c0: e.activation(out=self.ps[k][:, c0:], in_=self.ps[k][:, c0:], func=AF.Exp),
                 reads=[self.ps_b[k]], writes=[self.ps_b[k]])
            S.op("act", lambda e, k=k, c0=c0: e.activation(out=sp[k][:, c0:], in_=self.ps[k][:, c0:], func=AF.Ln, bias=one_c),
                 reads=[self.ps_b[k], cb_], writes=[sp_b[k]])
            if L >= Ldiag0:
                S.op("dve", lambda e, k=k, c0=c0: e.tensor_tensor(
                        out=sp[k][:, c0:c0 + 128], in0=sp[k][:, c0:c0 + 128], in1=tri, op=ALU.mult),
                     reads=[sp_b[k], cb_], writes=[sp_b[k]])

        def s2(i):
            L = Ls[i]
            c0 = c0of(i)
            k = (ucount + i) % 2
            pC, pCb = self.ps[2 + k], self.ps_b[2 + k]
            S.op("pe", lambda e, k=k, pC=pC, c0=c0: e.matmul(pC[:, c0:], lhsT=negU, rhs=sp[k][:, c0:], start=True, stop=False),
                 reads=[cb_, sp_b[k]], writes=[pCb])
            last = (i == 0)
            S.op("pe", lambda e, pC=pC, L=L, last=last, c0=c0: e.matmul(
                    pC[:, c0:], lhsT=ka[:, L * 128:(L + 1) * 128], rhs=qa[:, t * TC + c0:(t + 1) * TC], start=False, stop=last),
                 reads=kb_ + qb_, writes=[pCb])
            if i > 0:
                kp = (ucount + i - 1) % 2
                cp = c0of(i - 1)
                S.op("pe", lambda e, pC=pC, kp=kp, cp=cp: e.matmul(pC[:, cp:], lhsT=negI, rhs=Rs[kp][:, cp:], start=False, stop=True),
                     reads=[cb_, Rs_b[kp]], writes=[pCb])
            S.op("pe", lambda e, k=k, i=i, c0=c0: e.matmul(pR[:, c0:], lhsT=self.ones1[:, :], rhs=sp[k][:, c0:],
                                                          start=(i == 0), stop=(i == n - 1)),
                 reads=[cb_, sp_b[k]], writes=[pRb])
            if i < n - 1:
                S.op("dve", lambda e, k=k, c0=c0: e.tensor_copy(out=Rs[k][:, c0:], in_=pR[:, c0:]),
                     reads=[pRb], writes=[Rs_b[k]])

        def s3(i):
            L = Ls[i]
            c0 = c0of(i)
            k = (ucount + i) % 2
            pC, pCb = self.ps[2 + k], self.ps_b[2 + k]
            S.op("act", lambda e, k=k, pC=pC, c0=c0: e.activation(out=A[k][:, c0:], in_=pC[:, c0:], func=AF.Exp),
                 reads=[pCb], writes=[A_b[k]])
            if L >= Ldiag0:
                S.op("dve", lambda e, k=k, c0=c0: e.tensor_tensor(
                        out=A[k][:, c0:c0 + 128], in0=A[k][:, c0:c0 + 128], in1=tri, op=ALU.mult),
                     reads=[A_b[k], cb_], writes=[A_b[k]])

        def s4(i):
            L = Ls[i]
            c0 = c0of(i)
            k = (ucount + i) % 2
            S.op("pe", lambda e, k=k, L=L, i=i, c0=c0: e.matmul(pO[:, c0:], lhsT=v3[:, L, :], rhs=A[k][:, c0:],
                                                         start=(i == 0), stop=(i == n - 1)),
                 reads=vb_ + [A_b[k]], writes=[pOb])

        for step in range(n + 3):
            if step < n:
                s0(step)
            if 0 <= step - 1 < n:
                s1(step - 1)
            if 0 <= step - 2 < n:
                s2(step - 2)
                s3(step - 2)
            if 0 <= step - 3 < n:
                s4(step - 3)
        S.op("dve", lambda e: e.tensor_copy(out=self.xn[hp, oc, qsl], in_=pO[hp, :]),
             reads=[pOb], writes=[self.xn_b[oc][t]])
        return ucount + n

    def xattn(self, layer):
        S = self.S
        after = S.barrier()
        cb_ = self.const_b
        memT = self.sf[:, 0:2048].rearrange("p (c m) -> p c m", c=NCH)
        memT_b = S.buf("memT", after)
        rr = [self.sf[:, 2048 + i * TC:2048 + (i + 1) * TC] for i in range(2)]
        rr_b = S.bufs("xrr", 2, after)
        qn = self.sbf[:, 0:4096].rearrange("p (c t) -> p c t", c=NCH)
        qn_b = S.bufs("qn", NCH, after)
        Pm = [self.sbf[:, 4096 + i * 1024:4096 + (i + 1) * 1024].rearrange("p (m t) -> p m t", m=2) for i in range(2)]
        Pm_b = S.bufs("Pm", 2, after)
        rd = [self.sbf[:, 6144 + i * TC:6144 + (i + 1) * TC] for i in range(2)]
        S.dma("sp", memT, self.memT_d.ap().rearrange("(c p) m -> p c m", p=128), writes=[memT_b])
        mn_ap, mn_b = self.ring.alloc(1)
        memn = mn_ap.rearrange("p (c m) -> p c m", c=NCH)
        self.rstd_from([(memT[:, c, :], [memT_b]) for c in range(NCH)], self.onesm, 256, self.lnb(0), rr[0][:, 0:256], rr_b[0])
        for c in range(NCH):
            S.op("dve", lambda e, c=c: e.scalar_tensor_tensor(
                    out=memn[:, c, :], in0=memT[:, c, :], scalar=self.gcol(f"xam_{layer}", c), in1=rr[0][:, 0:256],
                    op0=ALU.mult, op1=ALU.mult),
                 reads=[memT_b, rr_b[0], cb_], writes=mn_b)
        wkv = self.w["xa_w_kv"].ap()[layer]
        kt_ap, kt_b = self.ring.alloc(1)
        KT = kt_ap.rearrange("p (c m) -> p c m", c=NCH)
        vm_ap, vm_b = self.ring.alloc(1)
        Vm = vm_ap.rearrange("p (m c) -> p m c", m=2)
        for hd in range(4):
            wv, wb = self.load_cols(wkv, hd * 256)
            pk = [self.ps[0], self.ps[1]]
            pkb = [self.ps_b[0], self.ps_b[1]]
            for cc in range(2):
                for c in range(NCH):
                    S.op("pe", lambda e, cc=cc, c=c, wv=wv: e.matmul(pk[cc][:, 0:256], lhsT=wv[:, c, cc * 128:(cc + 1) * 128],
                                                                     rhs=memn[:, c, :], start=(c == 0), stop=(c == NCH - 1)),
                         reads=wb + mn_b, writes=[pkb[cc]])
            r, rb = rr[1][:, 0:256], rr_b[1]
            self.rstd_from([(pk[cc][:, 0:256], [pkb[cc]]) for cc in range(2)], self.ones256, 256, self.lnb(0), r, rb)
            for cc in range(2):
                S.op("dve", lambda e, cc=cc, hd=hd, r=r: e.scalar_tensor_tensor(
                        out=KT[:, 2 * hd + cc, :], in0=pk[cc][:, 0:256], scalar=self.gcol(f"xak_{layer}", cc), in1=r,
                        op0=ALU.mult, op1=ALU.mult),
                     reads=[pkb[cc], rb, cb_], writes=kt_b)
        for g in range(4):
            wv, wb = self.load_cols(wkv, D + g * 256)
            for mb in range(2):
                pv, pvb = self.ps[2 + mb], self.ps_b[2 + mb]
                for c in range(NCH):
                    S.op("pe", lambda e, pv=pv, c=c, mb=mb, wv=wv: e.matmul(
                            pv[:, 0:256], lhsT=memn[:, c, mb * 128:(mb + 1) * 128], rhs=wv[:, c, :],
                            start=(c == 0), stop=(c == NCH - 1)),
                         reads=wb + mn_b, writes=[pvb])
                S.op("act", lambda e, pv=pv, mb=mb, g=g: e.activation(out=Vm[:, mb, g * 256:(g + 1) * 256], in_=pv[:, 0:256],
                                                                      func=AF.Copy),
                     reads=[pvb], writes=vm_b)
        wq = self.w["xa_w_q"].ap()[layer]
        wo = self.w["xa_w_o"].ap()[layer]
        WQ = [self.load_cols(wq, g * 256) for g in range(4)]
        xn = self.xn
        for t in range(NTC):
            sl = slice(t * TC, (t + 1) * TC)
            for hd in range(4):
                wv, wb = WQ[hd]
                pq = [self.ps[0], self.ps[1]]
                pqb = [self.ps_b[0], self.ps_b[1]]
                for cc in range(2):
                    for c in range(NCH):
                        S.op("pe", lambda e, cc=cc, c=c, wv=wv, sl=sl: e.matmul(
                                pq[cc][:, :], lhsT=wv[:, c, cc * 128:(cc + 1) * 128], rhs=xn[:, c, sl],
                                start=(c == 0), stop=(c == NCH - 1)),
                             reads=wb + [self.xn_b[c][t]], writes=[pqb[cc]])
                ri = (t * 4 + hd) % 2
                r, rb = rr[ri], rr_b[ri]
                self.rstd_from([(pq[cc][:, :], [pqb[cc]]) for cc in range(2)], self.ones256, TC, self.lnb(2), r, rb)
                for cc in range(2):
                    S.op("dve", lambda e, cc=cc, hd=hd, r=r: e.scalar_tensor_tensor(
                            out=qn[:, 2 * hd + cc, :], in0=pq[cc][:, :], scalar=self.gcol(f"xaq_{layer}", cc), in1=r,
                            op0=ALU.mult, op1=ALU.mult),
                         reads=[pqb[cc], rb, cb_], writes=[qn_b[2 * hd + cc]])
            for hd in range(4):
                pi = (t * 4 + hd) % 2
                Pt, Ptb = Pm[pi], Pm_b[pi]
                for mb in range(2):
                    pz, pzb = self.ps[2 + mb], self.ps_b[2 + mb]
                    for cc in range(2):
                        S.op("pe", lambda e, pz=pz, mb=mb, cc=cc, hd=hd: e.matmul(
                                pz[:, :], lhsT=KT[:, 2 * hd + cc, mb * 128:(mb + 1) * 128], rhs=qn[:, 2 * hd + cc, :],
                                start=(cc == 0), stop=(cc == 1)),
                             reads=kt_b + [qn_b[2 * hd + cc]], writes=[pzb])
                    S.op("act", lambda e, pz=pz, mb=mb, Pt=Pt: e.activation(out=Pt[:, mb, :], in_=pz[:, :], func=AF.Exp),
                         reads=[pzb], writes=[Ptb])
                pd, pdb = self.ps[4], self.ps_b[4]
                for mb in range(2):
                    S.op("pe", lambda e, mb=mb, Pt=Pt: e.matmul(pd[:, :], lhsT=self.ones1[:, :], rhs=Pt[:, mb, :],
                                                              start=(mb == 0), stop=(mb == 1)),
                         reads=[cb_, Ptb], writes=[pdb])
                r, rb = rr[pi], rr_b[pi]
                S.op("act", lambda e, r=r: e.activation(out=r, in_=pd[:, :], func=AF.Ln), reads=[pdb], writes=[rb])
                S.op("act", lambda e, r=r: e.activation(out=r, in_=r, func=AF.Exp, scale=-1.0), reads=[rb], writes=[rb])
                for dv in range(2):
                    po, pob = self.ps[5 + dv], self.ps_b[5 + dv]
                    for mb in range(2):
                        S.op("pe", lambda e, po=po, mb=mb, dv=dv, hd=hd, Pt=Pt: e.matmul(
                                po[:, :], lhsT=Vm[:, mb, hd * 256 + dv * 128:hd * 256 + (dv + 1) * 128], rhs=Pt[:, mb, :],
                                start=(mb == 0), stop=(mb == 1)),
                             reads=vm_b + [Ptb], writes=[pob])
                    S.op("dve", lambda e, po=po, dv=dv, hd=hd, sl=sl, r=r: e.tensor_tensor(
                            out=xn[:, 2 * hd + dv, sl], in0=po[:, :], in1=r, op=ALU.mult),
                         reads=[pob, rb], writes=[self.xn_b[2 * hd + dv][t]])
        self.out_proj(wo)

    def build(self):
        self.setup()
        ph = self.phases
        for l in range(2):
            if f"ffn1_{l}" in ph:
                self.rmsnorm_x(f"ffn1_{l}")
                self.ffn(self.w["ffn1_w_gu"], self.w["ffn1_w_down"], l)
            if f"mix_{l}" in ph:
                self.rmsnorm_x(f"mix_{l}")
                self.mixer_proj(l)
                if l == 0:
                    self.banded(0, [(j, j // 2, 0, j, (j // 2) * 64) for j in range(4)], 2, 1,
                                CB_MULTA, CF_NSLA, slopes(8), True)
                    self.stick(0, [(4 + j, 2 + j, 128 + j * 128, 4 + j) for j in range(4)])
                    if "noout" not in ph:
                        self.out_proj(self.w["ev_w_out"].ap()[0])
                else:
                    self.banded(1, [(j, j, j * 128, j, None) for j in range(8)], 17, 16,
                                CB_MULTC, CF_NSLC, slopes(16), False)
                    if "noout" not in ph:
                        self.out_proj(self.w["od_w_out"].ap()[0])
            if f"xa_{l}" in ph:
                self.rmsnorm_x(f"xa_{l}")
                self.xattn(l)
            if f"ffn2_{l}" in ph:
                self.rmsnorm_x(f"ffn2_{l}")
                self.ffn(self.w["ffn2_w_gu"], self.w["ffn2_w_down"], l)
        self.finish()
        self.S.emit(self.nc, self.es)


ALL_PHASES = tuple(f"{p}_{l}" for l in range(2) for p in ("ffn1", "mix", "xa", "ffn2"))


def build_nc(phases=ALL_PHASES, dbg=False):
    nc = bass.Bass("TRN2", target_bir_lowering=False)
    es = ExitStack()
    with es:
        p = Prog(nc, es, phases, dbg)
        p.build()
    nc.used_w = set(p.w.keys())
    return nc


def make_in_maps(inp, x_override=None, used_w=None, ncores=NCORES):
    gains = build_gains(inp)
    cb, cf = build_consts()
    x = np.asarray(inp["x"], np.float32) if x_override is None else x_override
    mem = np.asarray(inp["mem"], np.float32)
    shared = {k: np.ascontiguousarray(np.asarray(inp[k], np.float32)) for k in W_SHAPES
              if used_w is None or k in used_w}
    maps = []
    for core in range(ncores):
        b, h = core // 2, core % 2
        m = dict(shared)
        m["xT"] = np.ascontiguousarray(x[b, h * TOK:(h + 1) * TOK, :].T)
        m["memT"] = np.ascontiguousarray(mem[b].T)
        m["gains"] = gains
        m["cb"] = cb
        m["cf"] = cf
        m["flag"] = np.full((128, 128), float(h), np.float32)
        maps.append(m)
    return maps


def kernel(**inputs):
    nc = build_nc()
    maps = make_in_maps(inputs, used_w=nc.used_w)
    res = run_bass_kernel_spmd(nc, maps, core_ids=list(range(NCORES)))
    out = np.empty((4, 4096, D), np.float32)
    for core in range(NCORES):
        b, h = core // 2, core % 2
        out[b, h * TOK:(h + 1) * TOK, :] = np.asarray(res.results[core]["outT"]).T
    return out
```

```python
from contextlib import ExitStack
import numpy as np
import concourse.bass as bass
import concourse.mybir as mybir
from concourse.bass_utils import run_bass_kernel_spmd

F32 = mybir.dt.float32
BF16 = mybir.dt.bfloat16
AF = mybir.ActivationFunctionType
ALU = mybir.AluOpType

D = 1024
NCH = 8
TOK = 2048
TC = 512
NTC = TOK // TC
DFF = 2816
NFC = DFF // 128
EPS = 1e-6
NCORES = 8
NGROUPS = [4]


class Chan:
    def __init__(self, name):
        self.name = name
        self.sem = None
        self.cnt = 0


class Buf:
    def __init__(self, name, after=()):
        self.name = name
        self.writers = {}
        self.readers = {}
        self.chan = None
        for i, o in enumerate(after):
            self.readers[("init", i)] = o


class Op:
    __slots__ = ("eng", "fn", "deps", "dma", "chan", "val", "flag", "key", "inc")

    def __init__(self, eng, fn, dma=False, chan=None):
        self.eng = eng
        self.fn = fn
        self.deps = []
        self.dma = dma
        self.chan = chan
        self.val = 0
        self.flag = False
        self.key = None
        self.inc = 16


ENGS = ("pe", "act", "dve", "pool", "sp")


class Sched:
    def __init__(self):
        self.q = {e: [] for e in ENGS}
        self.chans = []
        self.nbuf = 0

    def buf(self, name, after=()):
        return Buf(name, after)

    def bufs(self, name, n, after=()):
        return [Buf(f"{name}{i}", after) for i in range(n)]

    def chan(self, name):
        c = Chan(f"{name}_{len(self.chans)}")
        self.chans.append(c)
        return c

    def _add(self, op, reads, writes):
        deps = {}

        def need(o):
            if o is op:
                return
            if (not o.dma) and (not op.dma) and o.eng == "pe" and op.eng == "pe":
                return
            deps[id(o)] = o

        for b in reads:
            for k, o in b.writers.items():
                need(o)
        for b in writes:
            for k, o in b.writers.items():
                if op.dma and o.dma:
                    continue
                need(o)
            for k, o in b.readers.items():
                need(o)
        op.deps = list(deps.values())
        for o in op.deps:
            o.flag = True
        for b in writes:
            if op.dma:
                b.writers = {k: o for k, o in b.writers.items() if o.dma}
                b.writers[op.key] = op
            else:
                b.writers = {op.key: op}
            b.readers = {}
        for b in reads:
            if b not in writes:
                b.readers[op.key] = op
        self.q[op.eng].append(op)
        return op

    def op(self, eng, fn, reads=(), writes=()):
        o = Op(eng, fn)
        o.key = eng
        return self._add(o, reads, writes)

    def dma(self, eng, out, in_, reads=(), writes=(), chan=None):
        if chan is None:
            b = writes[0]
            if b.chan is None:
                b.chan = self.chan("c_" + b.name)
            chan = b.chan
        o = Op(eng, lambda e: e.dma_start(out=out, in_=in_), dma=True, chan=chan)
        chan.cnt += 16
        o.val = chan.cnt
        o.key = ("ch", id(chan))
        return self._add(o, reads, writes)

    def coll(self, fn, reads=(), writes=()):
        ch = self.chan(f"cc{len(self.chans)}")
        o = Op("pool", fn, dma=True, chan=ch)
        o.inc = 1
        ch.cnt += 1
        o.val = ch.cnt
        o.key = ("ch", id(ch))
        return self._add(o, reads, writes)

    def barrier(self):
        res = []
        for e in ENGS:
            for o in reversed(self.q[e]):
                if not o.dma:
                    o.flag = True
                    res.append(o)
                    break
        seen = set()
        for e in ENGS:
            for o in reversed(self.q[e]):
                if o.dma and id(o.chan) not in seen:
                    seen.add(id(o.chan))
                    res.append(o)
        return res

    def emit(self, nc, es):
        for e in ENGS:
            c = 0
            for o in self.q[e]:
                if not o.dma and o.flag:
                    c += 1
                    o.val = c
        esem = {e: es.enter_context(nc.semaphore("sem_" + e)) for e in ENGS}
        for ch in self.chans:
            ch.sem = es.enter_context(nc.semaphore(ch.name))
        block = es.enter_context(nc.Block())

        def body_for(ename):
            def body(e):
                waited = {}
                for o in self.q[ename]:
                    need = {}
                    for d in o.deps:
                        sem = d.chan.sem if d.dma else esem[d.eng]
                        k = id(sem)
                        if waited.get(k, 0) >= d.val:
                            continue
                        if k not in need or need[k][1] < d.val:
                            need[k] = (sem, d.val)
                    for k, (sem, val) in need.items():
                        e.wait_ge(sem, val)
                        waited[k] = val
                    ins = o.fn(e)
                    if ins is None:
                        continue
                    if o.dma:
                        ins.then_inc(o.chan.sem, o.inc)
                    elif o.flag:
                        ins.then_inc(esem[ename], 1)
            return body

        block.tensor(body_for("pe"))
        block.scalar(body_for("act"))
        block.vector(body_for("dve"))
        block.gpsimd(body_for("pool"))
        block.sync(body_for("sp"))


def _col_chunks(v):
    v = np.asarray(v, np.float32)
    return np.ascontiguousarray(v.reshape(-1, 128).T)


def _gain_layout():
    cols = {}
    n = 0
    for l in range(2):
        for nm, w in ((f"ffn1_{l}", 8), (f"mix_{l}", 8), (f"xa_{l}", 8), (f"xam_{l}", 8), (f"ffn2_{l}", 8),
                      (f"xaq_{l}", 2), (f"xak_{l}", 2)):
            cols[nm] = n
            n += w
    for nm, w in (("evq", 1), ("evk", 1), ("odq", 1), ("odk", 1), ("sinks", 8)):
        cols[nm] = n
        n += w
    return cols, n


GCOL, NGCOL = _gain_layout()


def build_gains(inp):
    g = np.zeros((128, NGCOL), np.float32)
    for l in range(2):
        g[:, GCOL[f"ffn1_{l}"]:][:, :8] = _col_chunks(inp["ffn1_norm"][l])
        g[:, GCOL[f"mix_{l}"]:][:, :8] = _col_chunks(inp["mix_norm"][l])
        g[:, GCOL[f"xa_{l}"]:][:, :8] = _col_chunks(inp["xa_norm"][l])
        g[:, GCOL[f"xam_{l}"]:][:, :8] = _col_chunks(inp["xa_mem_norm"][l])
        g[:, GCOL[f"ffn2_{l}"]:][:, :8] = _col_chunks(inp["ffn2_norm"][l])
        g[:, GCOL[f"xaq_{l}"]:][:, :2] = _col_chunks(inp["xa_q_gain"][l])
        g[:, GCOL[f"xak_{l}"]:][:, :2] = _col_chunks(inp["xa_k_gain"][l])
    g[:, GCOL["evq"]] = np.tile(np.asarray(inp["ev_q_gain"][0], np.float32), 2)
    g[:, GCOL["evk"]] = np.tile(np.asarray(inp["ev_k_gain"][0], np.float32), 2)
    g[:, GCOL["odq"]] = np.tile(np.asarray(inp["od_q_gain"][0], np.float32), 2)
    g[:, GCOL["odk"]] = np.tile(np.asarray(inp["od_k_gain"][0], np.float32), 2)
    g[:, GCOL["sinks"]:GCOL["sinks"] + 8] = np.asarray(inp["ev_sinks"][0], np.float32)[None, :]
    return g


NTC_MASK = 23
CB_NEGU, CB_NEGI, CB_TRI7 = 0, 128, 256
CB_MULTC = CB_TRI7 + 7 * 128
CB_MULTA = CB_MULTC + NTC_MASK * 128
CB_N = CB_MULTA + 8 * 128
CF_D0, CF_D0C, CF_NSLC, CF_NSLA, CF_LNS = 0, 128, 256, 256 + 16 * 17, 256 + 16 * 17 + 8 * 2
CF_N = CF_LNS + 4


def slopes(n):
    return [2.0 ** (-8.0 * (i + 1) / n) for i in range(n)]


def build_consts():
    s = np.arange(128)[:, None]
    t = np.arange(128)[None, :]
    cb = np.zeros((128, CB_N), np.float32)
    cb[:, CB_NEGU:CB_NEGU + 128] = -(s >= t).astype(np.float32)
    cb[:, CB_NEGI:CB_NEGI + 128] = -(s == t).astype(np.float32)
    cb[:, CB_TRI7 + 3 * 128:CB_TRI7 + 4 * 128] = (s < t)
    cb[:, CB_TRI7 + 4 * 128:CB_TRI7 + 7 * 128] = 1.0
    for j in range(17):
        d = 128 * j + t - s
        m = ((d >= 0) & (d <= 128)).astype(np.float32) + ((d >= 0) & (d % 4 == 0) & (d <= 512)) \
            + ((d >= 0) & (d % 16 == 0) & (d <= 2048))
        cb[:, CB_MULTC + (3 + j) * 128:CB_MULTC + (4 + j) * 128] = m
    for j in range(2):
        d = 128 * j + t - s
        cb[:, CB_MULTA + (3 + j) * 128:CB_MULTA + (4 + j) * 128] = ((d >= 0) & (d <= 127))
    cf = np.zeros((128, CF_N), np.float32)
    cf[:, CF_D0:CF_D0 + 128] = t - s
    cf[:, CF_D0C:CF_D0C + 128] = np.maximum(t - s, 0)
    sc = slopes(16)
    for h in range(16):
        for j in range(17):
            cf[:, CF_NSLC + h * 17 + j] = -sc[h] * 128.0 * j
    sa = slopes(8)
    for h in range(8):
        for j in range(2):
            cf[:, CF_NSLA + h * 2 + j] = -sa[h] * 128.0 * j
    cf[:, CF_LNS + 0] = 0.0
    cf[:, CF_LNS + 1] = np.log(0.125)
    cf[:, CF_LNS + 2] = np.log(1.0 / 16.0)
    cf[:, CF_LNS + 3] = 1.0
    return cb, cf


W_SHAPES = {
    "ffn1_w_gu": [2, D, 2 * DFF], "ffn1_w_down": [2, DFF, D],
    "ffn2_w_gu": [2, D, 2 * DFF], "ffn2_w_down": [2, DFF, D],
    "ev_w_in": [1, D, 2304], "ev_w_out": [1, D, D], "od_w_in": [1, D, 3072], "od_w_out": [1, D, D],
    "xa_w_q": [2, D, D], "xa_w_kv": [2, D, 2 * D], "xa_w_o": [2, D, D],
}
UNIT = 2048
NUNITS = 14


class Ring:
    def __init__(self, S, tile, n):
        self.tile = tile
        self.n = n
        self.bufs = [S.buf(f"ring{i}") for i in range(n)]
        self.pos = 0

    def alloc(self, k=1):
        if self.pos + k > self.n:
            self.pos = 0
        u = self.pos
        self.pos += k
        return self.tile[:, u * UNIT:(u + k) * UNIT], self.bufs[u:u + k]


class Prog:
    def __init__(self, nc, es, phases, dbg=False):
        self.nc = nc
        self.es = es
        self.S = S = Sched()
        self.phases = phases
        dt = nc.dram_tensor
        self.xT_d = dt("xT", [D, TOK], F32, kind="ExternalInput")
        self.memT_d = dt("memT", [D, 256], F32, kind="ExternalInput")
        self.gains_d = dt("gains", [128, NGCOL], F32, kind="ExternalInput")
        self.cb_d = dt("cb", [128, CB_N], F32, kind="ExternalInput")
        self.cf_d = dt("cf", [128, CF_N], F32, kind="ExternalInput")
        self.flag_d = dt("flag", [128, 128], F32, kind="ExternalInput")
        class _LazyW(dict):
            def __missing__(d, nm):
                d[nm] = dt(nm, W_SHAPES[nm], F32, kind="ExternalInput")
                return d[nm]
        self.w = _LazyW()
        self.dbg = dbg
        self.out_d = dt("outT", [D, TOK], F32, kind="ExternalOutput")
        self.out_buf = S.buf("outT")
        self.scr = {}
        dk = dict(kind="ExternalOutput") if dbg else {}
        if dbg:
            self.xn_d = dt("xn_dump", [D, TOK], BF16, kind="ExternalOutput")
        for l, (nk, vc) in enumerate(((6, 640), (8, 1024))):
            self.scr[l] = dict(
                q=dt(f"q{l}_d", [8 * 128, TOK], BF16, **dk), q_b=S.buf(f"q{l}_d"),
                kloc=[dt(f"kloc{l}_{c}", [128, TOK], BF16) for c in range(nk)],
                kloc_b=[S.buf(f"kloc{l}_{c}") for c in range(nk)],
                kall=[dt(f"kall{l}_{c}", [256, TOK], BF16) for c in range(nk)],
                kall_b=[S.buf(f"kall{l}_{c}") for c in range(nk)],
                vloc=[dt(f"vloc{l}_{j}", [256, vc], BF16) for j in range(8)],
                vloc_b=[S.buf(f"vloc{l}_{j}") for j in range(8)],
                vall=[dt(f"vall{l}_{j}", [512, vc], BF16) for j in range(8)],
                vall_b=[S.buf(f"vall{l}_{j}") for j in range(8)],
                nk=nk, vc=vc)

        sb = lambda name, shape, dtype: es.enter_context(nc.sbuf_tensor(name, shape, dtype))
        ps = lambda name: es.enter_context(nc.psum_tensor(name, [128, 512], F32))
        self.xT = sb("xT_sb", [128, NCH, TOK], F32)
        self.xT_b = [[S.buf(f"xT{c}_{t}") for t in range(NTC)] for c in range(NCH)]
        self.xn = sb("xn_sb", [128, NCH, TOK], BF16)
        self.xn_b = [[S.buf(f"xn{c}_{t}") for t in range(NTC)] for c in range(NCH)]
        self.gains = sb("gains_sb", [128, NGCOL], F32)
        self.cb = sb("cb_sb", [128, CB_N], BF16)
        self.cf = sb("cf_sb", [128, CF_N], F32)
        self.const_b = S.buf("consts")
        self.onesm = sb("onesm", [128, 128], BF16)
        self.ones256 = sb("ones256", [128, 128], BF16)
        self.ones1 = sb("ones1", [128, 128], BF16)
        self.bd64 = sb("bd64", [128, 128], BF16)
        self.flagt = sb("flagt", [128, 128], F32)
        self.flagb = sb("flagb", [128, 128], BF16)
        self.epsc = sb("epsc", [128, 2], F32)
        self.exps = sb("exps", [128, 8], F32)
        self.sq = [sb(f"sq{i}", [128, TC], BF16) for i in range(4)]
        self.sq_b = S.bufs("sq", 4)
        self.rstd = [sb(f"rstd{i}", [128, TC], F32) for i in range(2)]
        self.rstd_b = S.bufs("rstd", 2)
        self.ring_t = sb("ring", [128, NUNITS * UNIT], BF16)
        self.ring = Ring(S, self.ring_t, NUNITS)
        self.sf = sb("scr_f32", [128, 3072], F32)
        self.sbf = sb("scr_bf", [128, 8192], BF16)
        self.ps = [ps(f"ps{i}") for i in range(8)]
        self.ps_b = S.bufs("ps", 8)
        self.norm_i = 0
        self.sq_i = 0

    def gcol(self, name, i=0):
        c = GCOL[name] + i
        return self.gains[:, c:c + 1]

    def setup(self):
        S = self.S
        cb_ = self.const_b
        S.dma("sp", self.gains[:, :], self.gains_d[:, :], writes=[cb_])
        S.dma("sp", self.cf[:, :], self.cf_d[:, :], writes=[cb_])
        S.dma("sp", self.flagt[:, :], self.flag_d[:, :], writes=[cb_])
        for i in range(0, CB_N, 1024):
            j = min(CB_N, i + 1024)
            S.dma("pool", self.cb[:, i:j], self.cb_d[:, i:j], writes=[cb_])
        S.op("pool", lambda e: e.memset(self.onesm[:, :], 1.0 / D), writes=[cb_])
        S.op("pool", lambda e: e.memset(self.ones256[:, :], 1.0 / 256), writes=[cb_])
        S.op("pool", lambda e: e.memset(self.ones1[:, :], 1.0), writes=[cb_])
        S.op("pool", lambda e: e.memset(self.bd64[:, :], 0.0), writes=[cb_])
        S.op("pool", lambda e: e.memset(self.bd64[0:64, 0:64], 1.0 / 64), writes=[cb_])
        S.op("pool", lambda e: e.memset(self.bd64[64:128, 64:128], 1.0 / 64), writes=[cb_])
        S.op("pool", lambda e: e.memset(self.epsc[:, :], EPS), writes=[cb_])
        S.op("pool", lambda e: e.memset(self.sbf[:, :], 0.0), writes=[cb_])
        S.op("dve", lambda e: e.tensor_copy(out=self.flagb[:, :], in_=self.flagt[:, :]), reads=[cb_], writes=[cb_])
        sc = GCOL["sinks"]
        S.op("act", lambda e: e.activation(out=self.exps[:, :], in_=self.gains[:, sc:sc + 8], func=AF.Exp),
             reads=[cb_], writes=[cb_])
        xv = self.xT_d.ap().rearrange("(c p) t -> p c t", p=128)
        for t in range(NTC):
            for c in range(NCH):
                S.dma("sp", self.xT[:, c, t * TC:(t + 1) * TC], xv[:, c, t * TC:(t + 1) * TC],
                      writes=[self.xT_b[c][t]])

    def finish(self):
        S = self.S
        ov = self.out_d.ap().rearrange("(c p) t -> p c t", p=128)
        for t in range(NTC):
            for c in range(NCH):
                S.dma("sp", ov[:, c, t * TC:(t + 1) * TC], self.xT[:, c, t * TC:(t + 1) * TC],
                      reads=[self.xT_b[c][t]], writes=[self.out_buf])
        if self.dbg:
            xv = self.xn_d.ap().rearrange("(c p) t -> p c t", p=128)
            for t in range(NTC):
                for c in range(NCH):
                    S.dma("sp", xv[:, c, t * TC:(t + 1) * TC], self.xn[:, c, t * TC:(t + 1) * TC],
                          reads=[self.xn_b[c][t]], writes=[self.out_buf])
        S.op("sp", lambda e: None, reads=[self.out_buf])

    def rstd_from(self, srcs, ones, n, lnbias, out_rs, out_rsb, reads_extra=()):
        S = self.S
        i = self.norm_i % 2
        self.norm_i += 1
        pss, pssb = self.ps[6 + i], self.ps_b[6 + i]
        k = len(srcs)
        for c, (ap, bl) in enumerate(srcs):
            q = self.sq_i % 4
            self.sq_i += 1
            sq, sqb = self.sq[q], self.sq_b[q]
            S.op("act", lambda e, sq=sq, ap=ap: e.activation(out=sq[:, 0:n], in_=ap, func=AF.Square),
                 reads=list(bl), writes=[sqb])
            S.op("pe", lambda e, sq=sq, pss=pss, c=c: e.matmul(pss[:, 0:n], lhsT=ones[:, :], rhs=sq[:, 0:n],
                                                             start=(c == 0), stop=(c == k - 1)),
                 reads=[self.const_b, sqb], writes=[pssb])
        S.op("act", lambda e: e.activation(out=out_rs, in_=pss[:, 0:n], func=AF.Ln, bias=self.epsc[:, 0:1]),
             reads=[pssb, self.const_b], writes=[out_rsb])
        S.op("act", lambda e: e.activation(out=out_rs, in_=out_rs, func=AF.Exp, scale=-0.5, bias=lnbias),
             reads=[out_rsb, self.const_b], writes=[out_rsb])

    def lnb(self, i):
        return self.cf[:, CF_LNS + i:CF_LNS + i + 1]

    def rmsnorm_x(self, gname):
        S = self.S
        xT, xn = self.xT, self.xn
        for t in range(NTC):
            sl = slice(t * TC, (t + 1) * TC)
            i = self.norm_i % 2
            rs, rsb = self.rstd[i], self.rstd_b[i]
            self.rstd_from([(xT[:, c, sl], [self.xT_b[c][t]]) for c in range(NCH)], self.onesm, TC,
                           self.lnb(0), rs[:, :], rsb)
            for c in range(NCH):
                S.op("dve", lambda e, c=c, rs=rs, sl=sl: e.scalar_tensor_tensor(
                        out=xn[:, c, sl], in0=xT[:, c, sl], scalar=self.gcol(gname, c),
                        in1=rs[:, :], op0=ALU.mult, op1=ALU.mult),
                     reads=[self.xT_b[c][t], rsb, self.const_b], writes=[self.xn_b[c][t]])

    def load_cols(self, w_ap2d, c0, ncols=256):
        ap, bufs = self.ring.alloc(1)
        v = ap.rearrange("p (c f) -> p c f", c=NCH)
        src = w_ap2d.rearrange("(c p) f -> p c f", p=128)
        self.S.dma("pool", v[:, :, 0:ncols], src[:, :, c0:c0 + ncols], writes=bufs)
        return v, bufs

    def load_rows(self, w_ap2d, r0):
        ap, bufs = self.ring.alloc(1)
        v = ap.rearrange("p (j m) -> p j m", j=2)
        src = w_ap2d[r0:r0 + 256, :].rearrange("(j p) m -> p j m", p=128)
        self.S.dma("pool", v[:, :, :], src, writes=bufs)
        return v, bufs

    def ffn(self, wgu_d, wdn_d, layer, after=None):
        S = self.S
        after = after if after is not None else S.barrier()
        sg = [self.sf[:, i * TC:(i + 1) * TC] for i in range(2)]
        sg_b = S.bufs("sg", 2, after)
        act = [self.sbf[:, i * 2 * TC:(i + 1) * 2 * TC].rearrange("p (j t) -> p j t", j=2) for i in range(3)]
        act_b = [[S.buf(f"act{i}_{j}", after) for j in range(2)] for i in range(3)]
        NG = NFC // 2
        gu2 = wgu_d.ap()[layer]
        dn2 = wdn_d.ap()[layer]
        W = {}

        def load(g):
            f0 = g * 256
            W[g] = (self.load_cols(gu2, f0), self.load_cols(gu2, DFF + f0), self.load_rows(dn2, f0))

        items = [(g, t) for g in range(NG) for t in range(NTC)]
        xn = self.xn

        def emit_gu(i):
            g, t = items[i]
            sl = slice(t * TC, (t + 1) * TC)
            a = i % 3
            for j in range(2):
                k = (2 * i + j) % 2
                pg, pgb = self.ps[k], self.ps_b[k]
                pu, pub = self.ps[2 + k], self.ps_b[2 + k]
                for (pp, ppb, gi) in ((pg, pgb, 0), (pu, pub, 1)):
                    wv, wb = W[g][gi]
                    for c in range(NCH):
                        S.op("pe", lambda e, pp=pp, wv=wv, c=c, j=j, sl=sl: e.matmul(
                                pp[:, :], lhsT=wv[:, c, j * 128:(j + 1) * 128], rhs=xn[:, c, sl],
                                start=(c == 0), stop=(c == NCH - 1)),
                             reads=wb + [self.xn_b[c][t]], writes=[ppb])
                S.op("act", lambda e, k=k, pg=pg: e.activation(out=sg[k], in_=pg[:, :], func=AF.Silu),
                     reads=[pgb], writes=[sg_b[k]])
                S.op("dve", lambda e, k=k, pu=pu, a=a, j=j: e.tensor_tensor(
                        out=act[a][:, j, :], in0=pu[:, :], in1=sg[k], op=ALU.mult),
                     reads=[pub, sg_b[k]], writes=[act_b[a][j]])

        def emit_down(i):
            g, t = items[i]
            sl = slice(t * TC, (t + 1) * TC)
            a = i % 3
            wv, wb = W[g][2]
            for m in range(NCH):
                k = 4 + (m % 2)
                py, pyb = self.ps[k], self.ps_b[k]
                for j in range(2):
                    S.op("pe", lambda e, py=py, j=j, m=m, wv=wv, a=a: e.matmul(
                            py[:, :], lhsT=wv[:, j, m * 128:(m + 1) * 128], rhs=act[a][:, j, :],
                            start=(j == 0), stop=(j == 1)),
                         reads=wb + [act_b[a][j]], writes=[pyb])
                S.op("dve", lambda e, py=py, m=m, sl=sl: e.scalar_tensor_tensor(
                        out=self.xT[:, m, sl], in0=py[:, :], scalar=0.5, in1=self.xT[:, m, sl],
                        op0=ALU.mult, op1=ALU.add),
                     reads=[pyb, self.xT_b[m][t]], writes=[self.xT_b[m][t]])

        PF = 3
        for g in range(min(PF, NG)):
            load(g)
        n = len(items)
        for i in range(n + 1):
            if i < n:
                g, t = items[i]
                if t == 1 and g + PF < NG:
                    load(g + PF)
                emit_gu(i)
            if i >= 1:
                emit_down(i - 1)

    def out_proj(self, w2d):
        S = self.S
        units = [self.load_rows(w2d, r * 256) for r in range(4)]
        for t in range(NTC):
            sl = slice(t * TC, (t + 1) * TC)
            for m in range(NCH):
                k = 4 + (m % 2)
                py, pyb = self.ps[k], self.ps_b[k]
                for oc in range(NCH):
                    wv, wb = units[oc // 2]
                    S.op("pe", lambda e, py=py, wv=wv, oc=oc, m=m, sl=sl: e.matmul(
                            py[:, :], lhsT=wv[:, oc % 2, m * 128:(m + 1) * 128], rhs=self.xn[:, oc, sl],
                            start=(oc == 0), stop=(oc == NCH - 1)),
                         reads=wb + [self.xn_b[oc][t]], writes=[pyb])
                S.op("dve", lambda e, py=py, m=m, sl=sl: e.tensor_tensor(
                        out=self.xT[:, m, sl], in0=py[:, :], in1=self.xT[:, m, sl], op=ALU.add),
                     reads=[pyb, self.xT_b[m][t]], writes=[self.xT_b[m][t]])

    def mixer_proj(self, layer, after=None):
        S = self.S
        after = after if after is not None else S.barrier()
        sc = self.scr[layer]
        even = layer == 0
        w2d = (self.w["ev_w_in"] if even else self.w["od_w_in"]).ap()[0]
        nun = 9 if even else 12
        stg = [self.sbf[:, i * TC:(i + 1) * TC] for i in range(4)]
        stg_b = S.bufs("stg", 4, after)
        stg_ch = [S.chan(f"stgch{layer}_{i}") for i in range(4)]
        sqh = [self.sbf[:, (4 + i) * TC:(5 + i) * TC] for i in range(2)]
        sqh_b = S.bufs("sqh", 2, after)
        rr = [self.sf[:, i * TC:(i + 1) * TC] for i in range(4)]
        rr_b = S.bufs("rr", 4, after)
        vst = [self.sbf[:, 3072 + i * 1024:3072 + (i + 1) * 1024] for i in range(2)]
        vst_b = S.bufs("vst", 2, after)
        vst_ch = [S.chan(f"vstch{layer}_{i}") for i in range(2)]
        cnt = {"s": 0, "q": 0, "v": 0}
        gq, gk = ("evq", "evk") if even else ("odq", "odk")
        jobs = []
        if even:
            for j in range(4):
                jobs.append((j // 2, (j % 2) * 128, "norm", gq, 1, sc["q"], sc["q_b"], j))
            for g in range(2):
                jobs.append((2, ("dup", g * 64), "norm", gk, 0, sc["kloc"], sc["kloc_b"], g))
            for j in range(4):
                jobs.append((3 + j // 2, (j % 2) * 128, "scale", None, 0.125, sc["q"], sc["q_b"], 4 + j))
            for j in range(4):
                jobs.append((5 + j // 2, (j % 2) * 128, "scale", None, 1.0, sc["kloc"], sc["kloc_b"], 2 + j))
            vjobs = [[(2, 128, 128)], [(7, 0, 256), (8, 0, 256)]]
        else:
            for j in range(8):
                jobs.append((j // 2, (j % 2) * 128, "norm", gq, 1, sc["q"], sc["q_b"], j))
            for j in range(8):
                jobs.append((4 + j // 2, (j % 2) * 128, "norm", gk, 0, sc["kloc"], sc["kloc_b"], j))
            vjobs = [[(8, 0, 256), (9, 0, 256)], [(10, 0, 256), (11, 0, 256)]]
        U = {}

        def need(u):
            if u not in U:
                U[u] = self.load_cols(w2d, u * 256)
            return U[u]

        order = [u for grp in vjobs for (u, _, _) in grp] + [jb[0] for jb in sorted(jobs, key=lambda jb: 0 if isinstance(jb[5], list) else 1)]
        for u in order:
            need(u)
        xn = self.xn
        vc = sc["vc"]
        for nb in range(TOK // 128):
            tb, tsl = nb // 4, slice(nb * 128, (nb + 1) * 128)
            vi = cnt["v"] % 2
            cnt["v"] += 1
            col = 0
            for bi, grp in enumerate(vjobs):
                pv, pvb = self.ps[2 + 2 * vi + bi], self.ps_b[2 + 2 * vi + bi]
                pc = 0
                for (u, c0, n) in grp:
                    wv, wb = need(u)
                    for c in range(NCH):
                        S.op("pe", lambda e, pv=pv, wv=wv, c=c, c0=c0, n=n, pc=pc, tsl=tsl: e.matmul(
                                pv[:, pc:pc + n], lhsT=xn[:, c, tsl], rhs=wv[:, c, c0:c0 + n],
                                start=(c == 0), stop=(c == NCH - 1)),
                             reads=wb + [self.xn_b[c][tb]], writes=[pvb])
                    pc += n
                eng = "act" if bi == 0 else "dve"
                if eng == "act":
                    S.op("act", lambda e, pv=pv, vi=vi, col=col, pc=pc: e.activation(
                            out=vst[vi][:, col:col + pc], in_=pv[:, 0:pc], func=AF.Copy),
                         reads=[pvb], writes=[vst_b[vi]])
                else:
                    S.op("dve", lambda e, pv=pv, vi=vi, col=col, pc=pc: e.tensor_copy(
                            out=vst[vi][:, col:col + pc], in_=pv[:, 0:pc]),
                         reads=[pvb], writes=[vst_b[vi]])
                col += pc
            jv = nb // 2
            S.dma("sp", sc["vloc"][jv][(nb % 2) * 128:(nb % 2 + 1) * 128, :], vst[vi][:, 0:vc], reads=[vst_b[vi]],
                  writes=[sc["vloc_b"][jv]], chan=vst_ch[vi])
            if nb % 2 == 1:
                self.gather(sc["vloc"][jv], sc["vloc_b"][jv], sc["vall"][jv], sc["vall_b"][jv])
        jobs.sort(key=lambda jb: 0 if isinstance(jb[5], list) else 1)
        for (u, off, kind, gname, par, dten, dbuf, drow) in jobs:
            wv, wb = need(u)
            for t in range(NTC):
                sl = slice(t * TC, (t + 1) * TC)
                k = cnt["q"] % 4
                cnt["q"] += 1
                pq, pqb = self.ps[k], self.ps_b[k]
                if isinstance(off, tuple):
                    c0 = off[1]
                    for half in range(2):
                        for c in range(NCH):
                            S.op("pe", lambda e, pq=pq, wv=wv, c=c, c0=c0, half=half, sl=sl: e.matmul(
                                    pq[half * 64:(half + 1) * 64, :], lhsT=wv[:, c, c0:c0 + 64], rhs=xn[:, c, sl],
                                    start=(c == 0), stop=(c == NCH - 1)),
                                 reads=wb + [self.xn_b[c][t]], writes=[pqb])
                else:
                    for c in range(NCH):
                        S.op("pe", lambda e, pq=pq, wv=wv, c=c, off=off, sl=sl: e.matmul(
                                pq[:, :], lhsT=wv[:, c, off:off + 128], rhs=xn[:, c, sl],
                                start=(c == 0), stop=(c == NCH - 1)),
                             reads=wb + [self.xn_b[c][t]], writes=[pqb])
                si = cnt["s"] % 4
                cnt["s"] += 1
                if kind == "norm":
                    r, rb = rr[k], rr_b[k]
                    self.rstd_from([(pq[:, :], [pqb])], self.bd64, TC, self.lnb(par), r, rb)
                    S.op("dve", lambda e, si=si, pq=pq, r=r, gname=gname: e.scalar_tensor_tensor(
                            out=stg[si], in0=pq[:, :], scalar=self.gcol(gname), in1=r, op0=ALU.mult, op1=ALU.mult),
                         reads=[pqb, rb, self.const_b], writes=[stg_b[si]])
                else:
                    S.op("act", lambda e, si=si, pq=pq, par=par: e.activation(
                            out=stg[si], in_=pq[:, :], func=AF.Copy, scale=float(par)),
                         reads=[pqb], writes=[stg_b[si]])
                if isinstance(dten, list):
                    S.dma("sp", dten[drow][:, sl], stg[si], reads=[stg_b[si]], writes=[dbuf[drow]], chan=stg_ch[si])
                    if t == NTC - 1:
                        self.gather(sc["kloc"][drow], sc["kloc_b"][drow], sc["kall"][drow], sc["kall_b"][drow])
                else:
                    S.dma("sp", dten[drow * 128:(drow + 1) * 128, sl], stg[si], reads=[stg_b[si]], writes=[dbuf],
                          chan=stg_ch[si])

    def gather(self, src, src_b, dst, dst_b):
        S = self.S
        import os
        if os.environ.get("KDBG_NOCOLL"):
            n = src.shape[0]
            S.dma("sp", dst[0:n, :], src[:, :], reads=[src_b], writes=[dst_b])
            return
        groups = [[2 * i, 2 * i + 1] for i in range(NGROUPS[0])]
        S.coll(lambda e: e.collective_compute("AllGather", ALU.bypass, replica_groups=groups,
                                              ins=[src.ap().opt()], outs=[dst.ap().opt()]),
               reads=[src_b], writes=[dst_b])

    def load_attn_tiles(self, layer, qrow, krow, vcol, nprev):
        S = self.S
        sc = self.scr[layer]
        nk = sc["nk"]
        qh = []
        for hh in range(2):
            qa, qb_ = self.ring.alloc(1)
            lo, hi = hh * 64, (hh + 1) * 64
            zl, zh = (1 - hh) * 64, (2 - hh) * 64
            S.dma("sp", qa[lo:hi, :], sc["q"][qrow * 128 + lo:qrow * 128 + hi, :], reads=[sc["q_b"]], writes=qb_)
            S.op("pool", lambda e, qa=qa, zl=zl, zh=zh: e.memset(qa[zl:zh, :], 0.0), writes=qb_)
            qh.append((qa, qb_))
        ka, kb_ = self.ring.alloc(2)
        npc = nprev * 128
        S.dma("sp", ka[:, 0:npc], sc["kall"][krow][0:128, TOK - npc:TOK], reads=[sc["kall_b"][krow]], writes=kb_)
        S.dma("sp", ka[:, npc:npc + TOK], sc["kloc"][krow][:, :], reads=[sc["kloc_b"][krow]], writes=kb_)
        va, vb_ = self.ring.alloc(2)
        v3 = va.rearrange("p (n c) -> p n c", c=128)
        if nprev == 16:
            for j in range(8):
                S.dma("sp", v3[:, 2 * j:2 * j + 2, :],
                      sc["vall"][j][0:256, vcol:vcol + 128].rearrange("(n p) c -> p n c", p=128),
                      reads=[sc["vall_b"][j]], writes=vb_)
        else:
            assert nprev == 1
            S.dma("sp", v3[:, 0:1, :], sc["vall"][7][128:256, vcol:vcol + 128].rearrange("(n p) c -> p n c", p=128),
                  reads=[sc["vall_b"][7]], writes=vb_)
        for j in range(8):
            S.dma("sp", v3[:, nprev + 2 * j:nprev + 2 * j + 2, :],
                  sc["vloc"][j][:, vcol:vcol + 128].rearrange("(n p) c -> p n c", p=128),
                  reads=[sc["vloc_b"][j]], writes=vb_)
        S.op("dve", lambda e: e.tensor_scalar(out=va[:, 0:npc], in0=va[:, 0:npc], scalar1=self.flagt[:, 0:1],
                                              scalar2=1.0, op0=ALU.mult, op1=ALU.mult),
             reads=vb_ + [self.const_b], writes=vb_)
        return qh, (ka, kb_), (v3, vb_)

    def banded(self, layer, pairs, ND, nprev, mult_off, nsl_off, sl_list, sink):
        S = self.S
        after = S.barrier()
        NT = ND + 6
        msk = [self.sbf[:, i * 3072:i * 3072 + NT * 128] for i in range(2)]
        msk_b = S.bufs("msk", 2, after)
        P = [self.sbf[:, 6144 + i * TC:6144 + (i + 1) * TC] for i in range(4)]
        P_b = S.bufs("P", 4, after)
        tmp = [self.sf[:, i * TC:(i + 1) * TC] for i in range(2)]
        tmp_b = S.bufs("tmp", 2, after)
        cb_ = self.const_b
        hcount = 0
        ucount = 0
        ccount = 0
        for pi, (qrow, krow, vcol, oc, vh) in enumerate(pairs):
            qh, (ka, kb_), (v3, vb_) = self.load_attn_tiles(layer, qrow, krow, vcol, nprev)
            for hh in range(2):
                qa, qb_ = qh[hh]
                h = 2 * pi + hh
                hp = slice(hh * 64, (hh + 1) * 64)
                vs = hp if vh is None else slice(vh, vh + 64)
                mi = hcount % 2
                hcount += 1
                m, mb = msk[mi], msk_b[mi]
                S.op("pool", lambda e, m=m: e.memset(m[:, 0:384], 0.0), writes=[mb])
                S.op("pool", lambda e, m=m: e.memset(m[:, (ND + 3) * 128:(ND + 6) * 128], 0.0), writes=[mb])
                for j in range(ND):
                    src = CF_D0C if j == 0 else CF_D0
                    S.op("act", lambda e, m=m, j=j, src=src, h=h: e.activation(
                            out=m[:, (3 + j) * 128:(4 + j) * 128], in_=self.cf[:, src:src + 128], func=AF.Exp,
                            scale=float(-sl_list[h]), bias=self.cf[:, nsl_off + h * ND + j:nsl_off + h * ND + j + 1]),
                         reads=[cb_], writes=[mb])
                S.op("dve", lambda e, m=m: e.tensor_tensor(out=m, in0=m, in1=self.cb[:, mult_off:mult_off + NT * 128],
                                                          op=ALU.mult),
                     reads=[cb_, mb], writes=[mb])
                for t in range(NTC):
                    ucount = self._banded_chunk(t, ccount, ucount, nprev, ND, hp, vs, h, oc, sink, m, mb, P, P_b, tmp, tmp_b,
                                                qa, qb_, ka, kb_, v3, vb_)
                    ccount += 1

    def _banded_chunk(self, t, ccount, ucount, nprev, ND, hp, vs, h, oc, sink, m, mb, P, P_b, tmp, tmp_b,
                      qa, qb_, ka, kb_, v3, vb_):
        S = self.S
        cb_ = self.const_b
        full_v = (vs == hp)
        ci = ccount % 2
        po, pob = self.ps[2 + ci], self.ps_b[2 + ci]
        pd, pdb = self.ps[4 + ci], self.ps_b[4 + ci]
        Lq0 = nprev + 4 * t
        units = [(L, Lq0 - L) for L in range(max(0, Lq0 - (ND - 1)), Lq0 + 4)]
        n = len(units)
        qsl = slice(t * TC, (t + 1) * TC)

        ZB = (0, 1, 6, 7)

        def cr(j0):
            lo = max(0, -j0)
            hi = min(3, ND - 1 - j0)
            return lo * 128, (hi + 1) * 128

        def st_z(i):
            L, j0 = units[i]
            c0, c1 = cr(j0)
            k = ZB[(ucount + i) % 4]
            S.op("pe", lambda e, k=k, L=L, c0=c0, c1=c1: e.matmul(
                    self.ps[k][:, c0:c1], lhsT=ka[:, L * 128:(L + 1) * 128],
                    rhs=qa[:, t * TC + c0:t * TC + c1], start=True, stop=True),
                 reads=kb_ + qb_, writes=[self.ps_b[k]])

        def st_p(i):
            L, j0 = units[i]
            c0, c1 = cr(j0)
            k = ZB[(ucount + i) % 4]
            p = (ucount + i) % 4
            S.op("act", lambda e, k=k, p=p, c0=c0, c1=c1: e.activation(out=P[p][:, c0:c1], in_=self.ps[k][:, c0:c1], func=AF.Exp),
                 reads=[self.ps_b[k]], writes=[P_b[p]])
            S.op("dve", lambda e, p=p, j0=j0, c0=c0, c1=c1: e.tensor_tensor(
                    out=P[p][:, c0:c1], in0=P[p][:, c0:c1], in1=m[:, (j0 + 3) * 128 + c0:(j0 + 3) * 128 + c1], op=ALU.mult),
                 reads=[P_b[p], mb], writes=[P_b[p]])

        def st_o(i):
            L, j0 = units[i]
            c0, c1 = cr(j0)
            p = (ucount + i) % 4
            if full_v:
                S.op("pe", lambda e, p=p, L=L, i=i, c0=c0, c1=c1: e.matmul(po[:, c0:c1], lhsT=v3[:, L, :], rhs=P[p][:, c0:c1],
                                                             start=(i == 0), stop=(i == n - 1)),
                     reads=vb_ + [P_b[p]], writes=[pob])
            else:
                S.op("pe", lambda e, p=p, L=L, i=i, c0=c0, c1=c1: e.matmul(po[hp, c0:c1], lhsT=v3[:, L, vs], rhs=P[p][:, c0:c1],
                                                             start=(i == 0), stop=(i == n - 1)),
                     reads=vb_ + [P_b[p]], writes=[pob])
            ones = self.flagb[:, :] if L < nprev else self.ones1[:, :]
            S.op("pe", lambda e, p=p, ones=ones, i=i, c0=c0, c1=c1: e.matmul(pd[:, c0:c1], lhsT=ones, rhs=P[p][:, c0:c1],
                                                               start=(i == 0), stop=(i == n - 1)),
                 reads=[cb_, P_b[p]], writes=[pdb])

        for step in range(n + 3):
            if step < n:
                st_z(step)
            if 0 <= step - 1 < n:
                st_p(step - 1)
            if 0 <= step - 3 < n:
                st_o(step - 3)
        ti = ccount % 2
        tm, tmb = tmp[ti], tmp_b[ti]
        if sink:
            S.op("act", lambda e: e.activation(out=tm[hp, :], in_=pd[hp, :], func=AF.Ln, bias=self.exps[hp, h:h + 1]),
                 reads=[pdb, cb_], writes=[tmb])
        else:
            S.op("act", lambda e: e.activation(out=tm[hp, :], in_=pd[hp, :], func=AF.Ln),
                 reads=[pdb], writes=[tmb])
        S.op("act", lambda e: e.activation(out=tm[hp, :], in_=tm[hp, :], func=AF.Exp, scale=-1.0),
             reads=[tmb], writes=[tmb])
        S.op("dve", lambda e: e.tensor_tensor(out=self.xn[hp, oc, qsl], in0=po[hp, :], in1=tm[hp, :], op=ALU.mult),
             reads=[pob, tmb], writes=[self.xn_b[oc][t]])
        return ucount + n

    def stick(self, layer, pairs):
        S = self.S
        after = S.barrier()
        nprev = 16
        ef = [self.sf[:, i * TC:(i + 1) * TC] for i in range(2)]
        ef_b = S.bufs("ef", 2, after)
        sp = [self.sbf[:, i * TC:(i + 1) * TC] for i in range(2)]
        sp_b = S.bufs("sp", 2, after)
        A = [self.sbf[:, (2 + i) * TC:(3 + i) * TC] for i in range(2)]
        A_b = S.bufs("A", 2, after)
        Rs = [self.sbf[:, (4 + i) * TC:(5 + i) * TC] for i in range(2)]
        Rs_b = S.bufs("Rs", 2, after)
        cb_ = self.const_b
        negU = self.cb[:, CB_NEGU:CB_NEGU + 128]
        negI = self.cb[:, CB_NEGI:CB_NEGI + 128]
        one_c = self.cf[:, CF_LNS + 3:CF_LNS + 4]
        ccount = 0
        ucount = 0
        for pi, (qrow, krow, vcol, oc) in enumerate(pairs):
            qh, (ka, kb_), (v3, vb_) = self.load_attn_tiles(layer, qrow, krow, vcol, nprev)
            for hh in range(2):
                qa, qb_ = qh[hh]
                hp = slice(hh * 64, (hh + 1) * 64)
                for t in range(NTC):
                    ucount = self._stick_chunk(t, ccount, ucount, nprev, hp, oc, ef, ef_b, sp, sp_b, A, A_b, Rs, Rs_b,
                                               qa, qb_, ka, kb_, v3, vb_)
                    ccount += 1

    def _stick_chunk(self, t, ccount, ucount, nprev, hp, oc, ef, ef_b, sp, sp_b, A, A_b, Rs, Rs_b,
                     qa, qb_, ka, kb_, v3, vb_):
        S = self.S
        cb_ = self.const_b
        negU = self.cb[:, CB_NEGU:CB_NEGU + 128]
        negI = self.cb[:, CB_NEGI:CB_NEGI + 128]
        one_c = self.cf[:, CF_LNS + 3:CF_LNS + 4]
        ci = ccount % 2
        pR, pRb = self.ps[4 + ci], self.ps_b[4 + ci]
        pO, pOb = self.ps[6 + ci], self.ps_b[6 + ci]
        qsl = slice(t * TC, (t + 1) * TC)
        Ldiag0 = nprev + 4 * t
        Ls = list(range(Ldiag0 + 3, -1, -1))
        n = len(Ls)

        def c0of(i):
            L = Ls[i]
            return (L - Ldiag0) * 128 if L >= Ldiag0 else 0

        tri = self.cb[:, CB_TRI7 + 3 * 128:CB_TRI7 + 4 * 128]

        def s0(i):
            L = Ls[i]
            c0 = c0of(i)
            k = (ucount + i) % 2
            S.op("pe", lambda e, k=k, L=L, c0=c0: e.matmul(self.ps[k][:, c0:], lhsT=ka[:, L * 128:(L + 1) * 128],
                                                          rhs=qa[:, t * TC + c0:(t + 1) * TC], start=True, stop=True),
                 reads=kb_ + qb_, writes=[self.ps_b[k]])

        def s1(i):
            L = Ls[i]
            c0 = c0of(i)
            k = (ucount + i) % 2
            S.op("act", lambda e, k=k, c0=c0: e.activation(out=self.ps[k][:, c0:], in_=self.ps[k][:, c0:], func=AF.Exp),
                 reads=[self.ps_b[k]], writes=[self.ps_b[k]])
            S.op("act", lambda e, k=k, c0=c0: e.activation(out=sp[k][:, c0:], in_=self.ps[k][:, c0:], func=AF.Ln, bias=one_c),
                 reads=[self.ps_b[k], cb_], writes=[sp_b[k]])
            if L >= Ldiag0:
                S.op("dve", lambda e, k=k, c0=c0: e.tensor_tensor(
                        out=sp[k][:, c0:c0 + 128], in0=sp[k][:, c0:c0 + 128], in1=tri, op=ALU.mult),
                     reads=[sp_b[k], cb_], writes=[sp_b[k]])

        def s2(i):
            L = Ls[i]
            c0 = c0of(i)
            k = (ucount + i) % 2
            pC, pCb = self.ps[2 + k], self.ps_b[2 + k]
            S.op("pe", lambda e, k=k, pC=pC, c0=c0: e.matmul(pC[:, c0:], lhsT=negU, rhs=sp[k][:, c0:], start=True, stop=False),
                 reads=[cb_, sp_b[k]], writes=[pCb])
            last = (i == 0)
            S.op("pe", lambda e, pC=pC, L=L, last=last, c0=c0: e.matmul(
                    pC[:, c0:], lhsT=ka[:, L * 128:(L + 1) * 128], rhs=qa[:, t * TC + c0:(t + 1) * TC], start=False, stop=last),
                 reads=kb_ + qb_, writes=[pCb])
            if i > 0:
                kp = (ucount + i - 1) % 2
                cp = c0of(i - 1)
                S.op("pe", lambda e, pC=pC, kp=kp, cp=cp: e.matmul(pC[:, cp:], lhsT=negI, rhs=Rs[kp][:, cp:], start=False, stop=True),
                     reads=[cb_, Rs_b[kp]], writes=[pCb])
            S.op("pe", lambda e, k=k, i=i, c0=c0: e.matmul(pR[:, c0:], lhsT=self.ones1[:, :], rhs=sp[k][:, c0:],
                                                          start=(i == 0), stop=(i == n - 1)),
                 reads=[cb_, sp_b[k]], writes=[pRb])
            if i < n - 1:
                S.op("dve", lambda e, k=k, c0=c0: e.tensor_copy(out=Rs[k][:, c0:], in_=pR[:, c0:]),
                     reads=[pRb], writes=[Rs_b[k]])

        def s3(i):
            L = Ls[i]
            c0 = c0of(i)
            k = (ucount + i) % 2
            pC, pCb = self.ps[2 + k], self.ps_b[2 + k]
            S.op("act", lambda e, k=k, pC=pC, c0=c0: e.activation(out=A[k][:, c0:], in_=pC[:, c0:], func=AF.Exp),
                 reads=[pCb], writes=[A_b[k]])
            if L >= Ldiag0:
                S.op("dve", lambda e, k=k, c0=c0: e.tensor_tensor(
                        out=A[k][:, c0:c0 + 128], in0=A[k][:, c0:c0 + 128], in1=tri, op=ALU.mult),
                     reads=[A_b[k], cb_], writes=[A_b[k]])

        def s4(i):
            L = Ls[i]
            c0 = c0of(i)
            k = (ucount + i) % 2
            S.op("pe", lambda e, k=k, L=L, i=i, c0=c0: e.matmul(pO[:, c0:], lhsT=v3[:, L, :], rhs=A[k][:, c0:],
                                                         start=(i == 0), stop=(i == n - 1)),
                 reads=vb_ + [A_b[k]], writes=[pOb])

        for step in range(n + 3):
            if step < n:
                s0(step)
            if 0 <= step - 1 < n:
                s1(step - 1)
            if 0 <= step - 2 < n:
                s2(step - 2)
                s3(step - 2)
            if 0 <= step - 3 < n:
                s4(step - 3)
        S.op("dve", lambda e: e.tensor_copy(out=self.xn[hp, oc, qsl], in_=pO[hp, :]),
             reads=[pOb], writes=[self.xn_b[oc][t]])
        return ucount + n

    def xattn(self, layer, after=None):
        S = self.S
        after = after if after is not None else S.barrier()
        cb_ = self.const_b
        memT = self.sf[:, 0:2048].rearrange("p (c m) -> p c m", c=NCH)
        memT_b = S.buf("memT", after)
        rr = [self.sf[:, 2048 + i * TC:2048 + (i + 1) * TC] for i in range(2)]
        rr_b = S.bufs("xrr", 2, after)
        qn = self.sbf[:, 0:4096].rearrange("p (c t) -> p c t", c=NCH)
        qn_b = S.bufs("qn", NCH, after)
        Pm = [self.sbf[:, 4096 + i * 1024:4096 + (i + 1) * 1024].rearrange("p (m t) -> p m t", m=2) for i in range(2)]
        Pm_b = S.bufs("Pm", 2, after)
        rd = [self.sbf[:, 6144 + i * TC:6144 + (i + 1) * TC] for i in range(2)]
        S.dma("sp", memT, self.memT_d.ap().rearrange("(c p) m -> p c m", p=128), writes=[memT_b])
        mn_ap, mn_b = self.ring.alloc(1)
        memn = mn_ap.rearrange("p (c m) -> p c m", c=NCH)
        self.rstd_from([(memT[:, c, :], [memT_b]) for c in range(NCH)], self.onesm, 256, self.lnb(0), rr[0][:, 0:256], rr_b[0])
        for c in range(NCH):
            S.op("dve", lambda e, c=c: e.scalar_tensor_tensor(
                    out=memn[:, c, :], in0=memT[:, c, :], scalar=self.gcol(f"xam_{layer}", c), in1=rr[0][:, 0:256],
                    op0=ALU.mult, op1=ALU.mult),
                 reads=[memT_b, rr_b[0], cb_], writes=mn_b)
        wkv = self.w["xa_w_kv"].ap()[layer]
        kt_ap, kt_b = self.ring.alloc(1)
        KT = kt_ap.rearrange("p (c m) -> p c m", c=NCH)
        vm_ap, vm_b = self.ring.alloc(1)
        Vm = vm_ap.rearrange("p (m c) -> p m c", m=2)
        for hd in range(4):
            wv, wb = self.load_cols(wkv, hd * 256)
            pk = [self.ps[0], self.ps[1]]
            pkb = [self.ps_b[0], self.ps_b[1]]
            for cc in range(2):
                for c in range(NCH):
                    S.op("pe", lambda e, cc=cc, c=c, wv=wv: e.matmul(pk[cc][:, 0:256], lhsT=wv[:, c, cc * 128:(cc + 1) * 128],
                                                                     rhs=memn[:, c, :], start=(c == 0), stop=(c == NCH - 1)),
                         reads=wb + mn_b, writes=[pkb[cc]])
            r, rb = rr[1][:, 0:256], rr_b[1]
            self.rstd_from([(pk[cc][:, 0:256], [pkb[cc]]) for cc in range(2)], self.ones256, 256, self.lnb(0), r, rb)
            for cc in range(2):
                S.op("dve", lambda e, cc=cc, hd=hd, r=r: e.scalar_tensor_tensor(
                        out=KT[:, 2 * hd + cc, :], in0=pk[cc][:, 0:256], scalar=self.gcol(f"xak_{layer}", cc), in1=r,
                        op0=ALU.mult, op1=ALU.mult),
                     reads=[pkb[cc], rb, cb_], writes=kt_b)
        for g in range(4):
            wv, wb = self.load_cols(wkv, D + g * 256)
            for mb in range(2):
                pv, pvb = self.ps[2 + mb], self.ps_b[2 + mb]
                for c in range(NCH):
                    S.op("pe", lambda e, pv=pv, c=c, mb=mb, wv=wv: e.matmul(
                            pv[:, 0:256], lhsT=memn[:, c, mb * 128:(mb + 1) * 128], rhs=wv[:, c, :],
                            start=(c == 0), stop=(c == NCH - 1)),
                         reads=wb + mn_b, writes=[pvb])
                S.op("act", lambda e, pv=pv, mb=mb, g=g: e.activation(out=Vm[:, mb, g * 256:(g + 1) * 256], in_=pv[:, 0:256],
                                                                      func=AF.Copy),
                     reads=[pvb], writes=vm_b)
        wq = self.w["xa_w_q"].ap()[layer]
        wo = self.w["xa_w_o"].ap()[layer]
        WQ = [self.load_cols(wq, g * 256) for g in range(4)]
        xn = self.xn
        for t in range(NTC):
            sl = slice(t * TC, (t + 1) * TC)
            for hd in range(4):
                wv, wb = WQ[hd]
                pq = [self.ps[0], self.ps[1]]
                pqb = [self.ps_b[0], self.ps_b[1]]
                for cc in range(2):
                    for c in range(NCH):
                        S.op("pe", lambda e, cc=cc, c=c, wv=wv, sl=sl: e.matmul(
                                pq[cc][:, :], lhsT=wv[:, c, cc * 128:(cc + 1) * 128], rhs=xn[:, c, sl],
                                start=(c == 0), stop=(c == NCH - 1)),
                             reads=wb + [self.xn_b[c][t]], writes=[pqb[cc]])
                ri = (t * 4 + hd) % 2
                r, rb = rr[ri], rr_b[ri]
                self.rstd_from([(pq[cc][:, :], [pqb[cc]]) for cc in range(2)], self.ones256, TC, self.lnb(2), r, rb)
                for cc in range(2):
                    S.op("dve", lambda e, cc=cc, hd=hd, r=r: e.scalar_tensor_tensor(
                            out=qn[:, 2 * hd + cc, :], in0=pq[cc][:, :], scalar=self.gcol(f"xaq_{layer}", cc), in1=r,
                            op0=ALU.mult, op1=ALU.mult),
                         reads=[pqb[cc], rb, cb_], writes=[qn_b[2 * hd + cc]])
            for hd in range(4):
                pi = (t * 4 + hd) % 2
                Pt, Ptb = Pm[pi], Pm_b[pi]
                for mb in range(2):
                    pz, pzb = self.ps[2 + mb], self.ps_b[2 + mb]
                    for cc in range(2):
                        S.op("pe", lambda e, pz=pz, mb=mb, cc=cc, hd=hd: e.matmul(
                                pz[:, :], lhsT=KT[:, 2 * hd + cc, mb * 128:(mb + 1) * 128], rhs=qn[:, 2 * hd + cc, :],
                                start=(cc == 0), stop=(cc == 1)),
                             reads=kt_b + [qn_b[2 * hd + cc]], writes=[pzb])
                    S.op("act", lambda e, pz=pz, mb=mb, Pt=Pt: e.activation(out=Pt[:, mb, :], in_=pz[:, :], func=AF.Exp),
                         reads=[pzb], writes=[Ptb])
                pd, pdb = self.ps[4], self.ps_b[4]
                for mb in range(2):
                    S.op("pe", lambda e, mb=mb, Pt=Pt: e.matmul(pd[:, :], lhsT=self.ones1[:, :], rhs=Pt[:, mb, :],
                                                              start=(mb == 0), stop=(mb == 1)),
                         reads=[cb_, Ptb], writes=[pdb])
                r, rb = rr[pi], rr_b[pi]
                S.op("act", lambda e, r=r: e.activation(out=r, in_=pd[:, :], func=AF.Ln), reads=[pdb], writes=[rb])
                S.op("act", lambda e, r=r: e.activation(out=r, in_=r, func=AF.Exp, scale=-1.0), reads=[rb], writes=[rb])
                for dv in range(2):
                    po, pob = self.ps[5 + dv], self.ps_b[5 + dv]
                    for mb in range(2):
                        S.op("pe", lambda e, po=po, mb=mb, dv=dv, hd=hd, Pt=Pt: e.matmul(
                                po[:, :], lhsT=Vm[:, mb, hd * 256 + dv * 128:hd * 256 + (dv + 1) * 128], rhs=Pt[:, mb, :],
                                start=(mb == 0), stop=(mb == 1)),
                             reads=vm_b + [Ptb], writes=[pob])
                    S.op("dve", lambda e, po=po, dv=dv, hd=hd, sl=sl, r=r: e.tensor_tensor(
                            out=xn[:, 2 * hd + dv, sl], in0=po[:, :], in1=r, op=ALU.mult),
                         reads=[pob, rb], writes=[self.xn_b[2 * hd + dv][t]])
        self.out_proj(wo)

    def build(self):
        self.setup()
        ph = self.phases
        for l in range(2):
            if f"ffn1_{l}" in ph:
                bar = self.S.barrier()
                self.rmsnorm_x(f"ffn1_{l}")
                self.ffn(self.w["ffn1_w_gu"], self.w["ffn1_w_down"], l, bar)
            if f"mix_{l}" in ph:
                bar = self.S.barrier()
                self.rmsnorm_x(f"mix_{l}")
                self.mixer_proj(l, bar)
                if l == 0:
                    self.banded(0, [(j, j // 2, 0, j, (j // 2) * 64) for j in range(4)], 2, 1,
                                CB_MULTA, CF_NSLA, slopes(8), True)
                    self.stick(0, [(4 + j, 2 + j, 128 + j * 128, 4 + j) for j in range(4)])
                    if "noout" not in ph:
                        self.out_proj(self.w["ev_w_out"].ap()[0])
                else:
                    self.banded(1, [(j, j, j * 128, j, None) for j in range(8)], 17, 16,
                                CB_MULTC, CF_NSLC, slopes(16), False)
                    if "noout" not in ph:
                        self.out_proj(self.w["od_w_out"].ap()[0])
            if f"xa_{l}" in ph:
                bar = self.S.barrier()
                self.rmsnorm_x(f"xa_{l}")
                self.xattn(l, bar)
            if f"ffn2_{l}" in ph:
                bar = self.S.barrier()
                self.rmsnorm_x(f"ffn2_{l}")
                self.ffn(self.w["ffn2_w_gu"], self.w["ffn2_w_down"], l, bar)
        self.finish()
        self.S.emit(self.nc, self.es)


ALL_PHASES = tuple(f"{p}_{l}" for l in range(2) for p in ("ffn1", "mix", "xa", "ffn2"))


def build_nc(phases=ALL_PHASES, dbg=False):
    nc = bass.Bass("TRN2", target_bir_lowering=False)
    es = ExitStack()
    with es:
        p = Prog(nc, es, phases, dbg)
        p.build()
    nc.used_w = set(p.w.keys())
    return nc


def make_in_maps(inp, x_override=None, used_w=None, ncores=NCORES):
    gains = build_gains(inp)
    cb, cf = build_consts()
    x = np.asarray(inp["x"], np.float32) if x_override is None else x_override
    mem = np.asarray(inp["mem"], np.float32)
    shared = {k: np.ascontiguousarray(np.asarray(inp[k], np.float32)) for k in W_SHAPES
              if used_w is None or k in used_w}
    maps = []
    for core in range(ncores):
        b, h = core // 2, core % 2
        m = dict(shared)
        m["xT"] = np.ascontiguousarray(x[b, h * TOK:(h + 1) * TOK, :].T)
        m["memT"] = np.ascontiguousarray(mem[b].T)
        m["gains"] = gains
        m["cb"] = cb
        m["cf"] = cf
        m["flag"] = np.full((128, 128), float(h), np.float32)
        maps.append(m)
    return maps


def kernel(**inputs):
    nc = build_nc()
    maps = make_in_maps(inputs, used_w=nc.used_w)
    res = run_bass_kernel_spmd(nc, maps, core_ids=list(range(NCORES)))
    out = np.empty((4, 4096, D), np.float32)
    for core in range(NCORES):
        b, h = core // 2, core % 2
        out[b, h * TOK:(h + 1) * TOK, :] = np.asarray(res.results[core]["outT"]).T
    return out
```
